# Optimizing a Trainium2 kernel written in Bass

```python
import math
import jax, jax.numpy as jnp
from jax import lax
import numpy as np

D_MODEL = 1024
BATCH = 16
SEQ = 2048
DEPTH = 1
DEC_BATCH = 32
DEC_SEQ = 32
PAST_LEN = 4096

CHUNK = 64
Q_BLOCK = 128
DIFF_HEADS = 4
HEAD_DIM = 64
DIFF_WIDTH = DIFF_HEADS * 2 * HEAD_DIM
GMLP_GROUPS = 4
GMLP_CHUNK = 128
GMLP_GROUP_DIM = 128
GMLP_WIDTH = GMLP_GROUPS * GMLP_GROUP_DIM
MIX_WIDTH = DIFF_WIDTH + GMLP_WIDTH
IN_COLS = 3 * DIFF_WIDTH + 2 * GMLP_WIDTH
D_FF = 2816
N_MEM = 256
MEM_HEADS = 4
MEM_HEAD_DIM = D_MODEL // MEM_HEADS
LN_EPS = 1e-5
ALPHA = (2 * DEPTH) ** 0.25
BETA = (8 * DEPTH) ** -0.25

kernel_name = "hybrid_diffattn_gmlp_macaron_deepnorm_stream_step"

F32 = jnp.float32


def layer_norm(h, g, b):
    h32 = h.astype(F32)
    mu = jnp.mean(h32, axis=-1, keepdims=True)
    var = jnp.mean(jnp.square(h32 - mu), axis=-1, keepdims=True)
    return ((h32 - mu) * lax.rsqrt(var + LN_EPS) * g.astype(F32) + b.astype(F32)).astype(h.dtype)


def post_norm(x, h, g, b):
    return layer_norm(ALPHA * x + h, g, b)


def swiglu_half(x, w_gu, w_down):
    gate, up = jnp.split(x @ w_gu, 2, axis=-1)
    return 0.5 * ((jax.nn.silu(gate) * up) @ w_down)


def lambda_init_for(layer_idx):
    return 0.8 - 0.6 * math.exp(-0.3 * layer_idx)


def diff_lambda(lq1, lk1, lq2, lk2, lam_init):
    return (jnp.exp(jnp.sum(lq1.astype(F32) * lk1.astype(F32)))
            - jnp.exp(jnp.sum(lq2.astype(F32) * lk2.astype(F32))) + lam_init)


def alibi_slopes():
    return jnp.exp2(-8.0 * jnp.arange(1, DIFF_HEADS + 1, dtype=F32) / DIFF_HEADS)


def diff_attention(q, k, v, q_start, lam):
    B, T = q.shape[0], q.shape[1]
    S = k.shape[1]
    qb = min(Q_BLOCK, T)
    nb = T // qb
    k_pos = jnp.arange(S, dtype=jnp.int32)
    slopes = alibi_slopes()
    scale = HEAD_DIM ** -0.5

    def block(args):
        q_blk, start = args
        q_pos = start + jnp.arange(qb, dtype=jnp.int32)
        s = jnp.einsum('bqhmd,bkhmd->bhmqk', q_blk, k).astype(F32) * scale
        dist = jnp.abs(q_pos[:, None] - k_pos[None, :]).astype(F32)
        allowed = (k_pos[None, :] // CHUNK) <= (q_pos[:, None] // CHUNK)
        s = s - slopes[None, :, None, None, None] * dist[None, None, None]
        s = jnp.where(allowed, s, -jnp.inf)
        p = jax.nn.softmax(s, axis=-1)
        a = p[:, :, 0] - lam * p[:, :, 1]
        return jnp.einsum('bhqk,bkhe->bqhe', a.astype(v.dtype), v)

    q_blocks = jnp.moveaxis(q.reshape(B, nb, qb, DIFF_HEADS, 2, HEAD_DIM), 1, 0)
    starts = q_start + qb * jnp.arange(nb, dtype=jnp.int32)
    out = lax.map(block, (q_blocks, starts))
    return jnp.moveaxis(out, 0, 1).reshape(B, T, DIFF_HEADS, 2 * HEAD_DIM)


def spatial_gating(u, v, ln_g, ln_b, ws, bs):
    B, T, _ = u.shape
    n = min(T, GMLP_CHUNK)
    nc = T // n
    v = layer_norm(v, ln_g, ln_b)
    vg = v.reshape(B, nc, n, GMLP_GROUPS, GMLP_GROUP_DIM)
    w = ws[:, :n, :n] * jnp.tril(jnp.ones((n, n), ws.dtype))
    mixed = jnp.einsum('gts,bcsgd->bctgd', w, vg) + bs[:, :n].T[None, None, :, :, None]
    out = u * mixed.reshape(B, T, GMLP_WIDTH)
    return out, vg.reshape(B, T, GMLP_GROUPS, GMLP_GROUP_DIM)


def cross_attention(x, mem_k, mem_v, wq, wo):
    B, T, _ = x.shape
    q = (x @ wq).reshape(B, T, MEM_HEADS, MEM_HEAD_DIM)
    s = jnp.einsum('bthd,bmhd->bhtm', q, mem_k).astype(F32) * (MEM_HEAD_DIM ** -0.5)
    p = jax.nn.softmax(s, axis=-1)
    o = jnp.einsum('bhtm,bmhd->bthd', p.astype(mem_v.dtype), mem_v).reshape(B, T, D_MODEL)
    return o @ wo


def trunk_layer(x, k_past, v_past, mem_k, mem_v, layer_idx, p):
    B, T, _ = x.shape
    x = post_norm(x, swiglu_half(x, p['ffn1_w_gu'], p['ffn1_w_down']), p['ln1_g'], p['ln1_b'])

    z = x @ p['w_in']
    q, k, v, gu, gv = jnp.split(
        z, [DIFF_WIDTH, 2 * DIFF_WIDTH, 3 * DIFF_WIDTH, 3 * DIFF_WIDTH + GMLP_WIDTH], axis=-1)
    q = q.reshape(B, T, DIFF_HEADS, 2, HEAD_DIM)
    k = k.reshape(B, T, DIFF_HEADS, 2, HEAD_DIM)
    v = v.reshape(B, T, DIFF_HEADS, 2 * HEAD_DIM)
    if k_past is None:
        q_start, k_all, v_all = 0, k, v
    else:
        q_start = k_past.shape[1]
        k_all = jnp.concatenate([k_past, k], axis=1)
        v_all = jnp.concatenate([v_past, v], axis=1)

    lam_init = lambda_init_for(layer_idx)
    lam = diff_lambda(p['lambda_q1'], p['lambda_k1'], p['lambda_q2'], p['lambda_k2'], lam_init)
    a = diff_attention(q, k_all, v_all, q_start, lam)
    a32 = a.astype(F32)
    a32 = a32 * lax.rsqrt(jnp.mean(jnp.square(a32), axis=-1, keepdims=True) + LN_EPS)
    a = (a32 * p['subln_g'].astype(F32) * (1.0 - lam_init)).astype(x.dtype).reshape(B, T, DIFF_WIDTH)

    g_out, v_rows = spatial_gating(jax.nn.gelu(gu), jax.nn.gelu(gv), p['gmlp_ln_g'], p['gmlp_ln_b'],
                                   p['gmlp_ws'], p['gmlp_bs'])

    mix = jnp.concatenate([a, g_out], axis=-1) @ p['w_out']
    x = post_norm(x, mix, p['ln2_g'], p['ln2_b'])
    x = post_norm(x, cross_attention(x, mem_k, mem_v, p['cross_wq'], p['cross_wo']), p['ln3_g'], p['ln3_b'])
    x = post_norm(x, swiglu_half(x, p['ffn2_w_gu'], p['ffn2_w_down']), p['ln4_g'], p['ln4_b'])
    return x, k, v, v_rows


def setup_inputs(seed: int = 0) -> dict:
    key = jax.random.key(seed)
    ks = iter(jax.random.split(key, 48))

    def nrm(shape, scale):
        return jax.random.normal(next(ks), shape, F32) * scale

    def gain(shape):
        return 1.0 + nrm(shape, 0.01)

    L = DEPTH
    return {
        "x_prompt": nrm((BATCH, SEQ, D_MODEL), 1.0),
        "x_sample": nrm((DEC_BATCH, DEC_SEQ, D_MODEL), 1.0),
        "cache_k": nrm((L, DEC_BATCH, PAST_LEN, DIFF_HEADS, 2, HEAD_DIM), 1.0),
        "cache_v": nrm((L, DEC_BATCH, PAST_LEN, DIFF_HEADS, 2 * HEAD_DIM), 1.0),
        "cache_mem_k": nrm((L, DEC_BATCH, N_MEM, MEM_HEADS, MEM_HEAD_DIM), 1.0),
        "cache_mem_v": nrm((L, DEC_BATCH, N_MEM, MEM_HEADS, MEM_HEAD_DIM), 1.0),
        "mem_prompt": nrm((BATCH, N_MEM, D_MODEL), 1.0),
        "ffn1_w_gu": nrm((L, D_MODEL, 2 * D_FF), D_MODEL ** -0.5),
        "ffn1_w_down": nrm((L, D_FF, D_MODEL), BETA * D_FF ** -0.5),
        "ln1_g": gain((L, D_MODEL)),
        "ln1_b": nrm((L, D_MODEL), 0.01),
        "w_in": nrm((L, D_MODEL, IN_COLS), D_MODEL ** -0.5),
        "lambda_q1": nrm((L, HEAD_DIM), 0.1),
        "lambda_k1": nrm((L, HEAD_DIM), 0.1),
        "lambda_q2": nrm((L, HEAD_DIM), 0.1),
        "lambda_k2": nrm((L, HEAD_DIM), 0.1),
        "subln_g": gain((L, 2 * HEAD_DIM)),
        "gmlp_ln_g": gain((L, GMLP_WIDTH)),
        "gmlp_ln_b": nrm((L, GMLP_WIDTH), 0.01),
        "gmlp_ws": nrm((L, GMLP_GROUPS, GMLP_CHUNK, GMLP_CHUNK), 0.05),
        "gmlp_bs": gain((L, GMLP_GROUPS, GMLP_CHUNK)),
        "w_out": nrm((L, MIX_WIDTH, D_MODEL), BETA * MIX_WIDTH ** -0.5),
        "ln2_g": gain((L, D_MODEL)),
        "ln2_b": nrm((L, D_MODEL), 0.01),
        "cross_wq": nrm((L, D_MODEL, D_MODEL), D_MODEL ** -0.5),
        "cross_wk": nrm((L, D_MODEL, D_MODEL), D_MODEL ** -0.5),
        "cross_wv": nrm((L, D_MODEL, D_MODEL), D_MODEL ** -0.5),
        "cross_wo": nrm((L, D_MODEL, D_MODEL), BETA * D_MODEL ** -0.5),
        "ln3_g": gain((L, D_MODEL)),
        "ln3_b": nrm((L, D_MODEL), 0.01),
        "ffn2_w_gu": nrm((L, D_MODEL, 2 * D_FF), D_MODEL ** -0.5),
        "ffn2_w_down": nrm((L, D_FF, D_MODEL), BETA * D_FF ** -0.5),
        "ln4_g": gain((L, D_MODEL)),
        "ln4_b": nrm((L, D_MODEL), 0.01),
    }


def reference(x_prompt, x_sample, cache_k, cache_v, cache_mem_k, cache_mem_v, mem_prompt,
              ffn1_w_gu, ffn1_w_down, ln1_g, ln1_b, w_in, lambda_q1, lambda_k1, lambda_q2, lambda_k2,
              subln_g, gmlp_ln_g, gmlp_ln_b, gmlp_ws, gmlp_bs, w_out, ln2_g, ln2_b,
              cross_wq, cross_wk, cross_wv, cross_wo, ln3_g, ln3_b,
              ffn2_w_gu, ffn2_w_down, ln4_g, ln4_b):
    xp, xs = x_prompt, x_sample
    Bp = mem_prompt.shape[0]
    kp_l, vp_l, mkp_l, mvp_l, ks_l, vs_l, gvs_l = [], [], [], [], [], [], []
    for l in range(DEPTH):
        p = {
            'ffn1_w_gu': ffn1_w_gu[l], 'ffn1_w_down': ffn1_w_down[l], 'ln1_g': ln1_g[l], 'ln1_b': ln1_b[l],
            'w_in': w_in[l], 'lambda_q1': lambda_q1[l], 'lambda_k1': lambda_k1[l],
            'lambda_q2': lambda_q2[l], 'lambda_k2': lambda_k2[l], 'subln_g': subln_g[l],
            'gmlp_ln_g': gmlp_ln_g[l], 'gmlp_ln_b': gmlp_ln_b[l], 'gmlp_ws': gmlp_ws[l], 'gmlp_bs': gmlp_bs[l],
            'w_out': w_out[l], 'ln2_g': ln2_g[l], 'ln2_b': ln2_b[l],
            'cross_wq': cross_wq[l], 'cross_wo': cross_wo[l], 'ln3_g': ln3_g[l], 'ln3_b': ln3_b[l],
            'ffn2_w_gu': ffn2_w_gu[l], 'ffn2_w_down': ffn2_w_down[l], 'ln4_g': ln4_g[l], 'ln4_b': ln4_b[l],
        }
        mem_k = (mem_prompt @ cross_wk[l]).reshape(Bp, N_MEM, MEM_HEADS, MEM_HEAD_DIM)
        mem_v = (mem_prompt @ cross_wv[l]).reshape(Bp, N_MEM, MEM_HEADS, MEM_HEAD_DIM)
        xp, kp, vp, _ = trunk_layer(xp, None, None, mem_k, mem_v, l, p)
        xs, ks_new, vs_new, gv_new = trunk_layer(xs, cache_k[l], cache_v[l], cache_mem_k[l], cache_mem_v[l], l, p)
        kp_l.append(kp); vp_l.append(vp); mkp_l.append(mem_k); mvp_l.append(mem_v)
        ks_l.append(ks_new); vs_l.append(vs_new); gvs_l.append(gv_new)
    new_k_prompt = jnp.stack(kp_l)
    new_v_prompt = jnp.stack(vp_l)
    new_mem_k_prompt = jnp.stack(mkp_l)
    new_mem_v_prompt = jnp.stack(mvp_l)
    new_k_sample = jnp.stack(ks_l)
    new_v_sample = jnp.stack(vs_l)
    new_gmlp_v_sample = jnp.stack(gvs_l)
    return (xp, xs, new_k_prompt, new_v_prompt, new_mem_k_prompt, new_mem_v_prompt,
            new_k_sample, new_v_sample, new_gmlp_v_sample)
```

```python
import contextlib
import math
import os

import numpy as np
import concourse.bass as bass
import concourse.mybir as mybir
from concourse.bass_utils import run_bass_kernel_spmd

F32 = mybir.dt.float32
BF16 = mybir.dt.bfloat16
AF = mybir.ActivationFunctionType
ALU = mybir.AluOpType
AX = mybir.AxisListType

N_CORES = 8
D = 1024
KC = 8
DFF = 2816
NFC = 22
SEQ = 2048
TT = 512
NSEQ = 2
NSMP = 4
DSEQ = 32
PAST = 4096
NMEM = 256
ALPHA = 2.0 ** 0.25
EPS = 1e-5
EPS_DN = EPS / (ALPHA * ALPHA)
LAM_INIT = 0.2
NSLOT = 5
SLOTW = 2816


class Res:
    __slots__ = ("name", "w", "r", "excl")

    def __init__(self, name, excl=False):
        self.name = name
        self.w = None
        self.r = {}
        self.excl = excl


class _Eng:
    def __init__(self, name, sem):
        self.name = name
        self.sem = sem
        self.count = 0
        self.q = []
        self.waited = {}


class _Lane:
    def __init__(self, idx, sem):
        self.idx = idx
        self.sem = sem
        self.count = 0


class Sched:
    ENGS = ("pe", "act", "dve", "pool", "sp")

    def __init__(self, nc, stack, n_lanes):
        self.nc = nc
        self.eng = {}
        for n in self.ENGS:
            self.eng[n] = _Eng(n, stack.enter_context(nc.semaphore("sem_" + n)))
        self.lanes = [_Lane(i, stack.enter_context(nc.semaphore("lane%d" % i))) for i in range(n_lanes)]
        self._lane_next = 0
        self.n_instr = 0

    def new_lane(self):
        l = self.lanes[self._lane_next]
        self._lane_next += 1
        return l

    def _collect(self, E, reads, writes, extra=()):
        need = {}

        def add(t):
            if t is None:
                return
            k = t[0]
            if k not in need or need[k][2] < t[2]:
                need[k] = t

        for r in reads:
            add(r.w)
            if r.excl:
                for t in r.r.values():
                    if t[0] != E.name:
                        add(t)
        for r in writes:
            add(r.w)
            for t in r.r.values():
                add(t)
        for t in extra:
            add(t)
        final = []
        for k, (kk, sem, val) in need.items():
            if E.waited.get(k, 0) >= val:
                continue
            if E.name == "pe" and k == "pe":
                continue
            E.waited[k] = val
            final.append((sem, val))
        return final

    @staticmethod
    def _mark(tok, reads, writes):
        k = tok[0]
        for r in reads:
            o = r.r.get(k)
            if o is None or o[2] < tok[2]:
                r.r[k] = tok
        for r in writes:
            r.w = tok
            r.r = {}

    def op(self, eng, fn, reads=(), writes=()):
        E = self.eng[eng]
        waits = self._collect(E, reads, writes)
        E.count += 1
        tok = (E.name, E.sem, E.count)
        E.q.append((waits, fn, E.sem, 1))
        self._mark(tok, reads, writes)
        self.n_instr += 1
        return tok

    def dma(self, queue, lane, fn, reads=(), writes=()):
        Q = self.eng[queue]
        key = "L%d" % lane.idx
        prev = (key, lane.sem, 16 * lane.count) if lane.count else None
        waits = self._collect(Q, reads, writes, extra=(prev,))
        lane.count += 1
        tok = (key, lane.sem, 16 * lane.count)
        Q.q.append((waits, fn, lane.sem, 16))
        self._mark(tok, reads, writes)
        self.n_instr += 1
        return tok

    def finish(self, queue="sp"):
        Q = self.eng[queue]
        for l in self.lanes:
            if l.count:
                k = "L%d" % l.idx
                if Q.waited.get(k, 0) < 16 * l.count:
                    Q.waited[k] = 16 * l.count
                    Q.q.append(([(l.sem, 16 * l.count)], None, None, 0))
        for n in ("pe", "act", "dve", "pool"):
            E = self.eng[n]
            if E.count and Q.waited.get(n, 0) < E.count:
                Q.waited[n] = E.count
                Q.q.append(([(E.sem, E.count)], None, None, 0))

    def emit(self, block):
        def replay(e, E):
            for waits, fn, sem, inc in E.q:
                for s, v in waits:
                    e.wait_ge(s, v)
                if fn is not None:
                    fn(e).then_inc(sem, inc)
            E.q = []

        @block.tensor
        def _(e):
            replay(e, self.eng["pe"])

        @block.scalar
        def _(e):
            replay(e, self.eng["act"])

        @block.vector
        def _(e):
            replay(e, self.eng["dve"])

        @block.gpsimd
        def _(e):
            replay(e, self.eng["pool"])

        @block.sync
        def _(e):
            replay(e, self.eng["sp"])


class RR:
    def __init__(self, items):
        self.items = list(items)
        self.i = 0

    def next(self):
        v = self.items[self.i % len(self.items)]
        self.i += 1
        return v


W_SPECS = [
    ("gu1", "gu", NFC), ("dn1", "dn", 8), ("win", "k256", 10), ("wout", "k256", 4),
    ("wq", "k256", 4), ("wo", "k256", 4), ("gu2", "gu", NFC), ("dn2", "dn", 8),
    ("wk", "k256", 4), ("wv", "k256", 4),
]
W_KIND = {n: k for n, k, _ in W_SPECS}
W_NCH = {n: c for n, _, c in W_SPECS}
TILE_ORDER = ["gu1", "dn1", "win", "wout", "wq", "wo", "gu2", "dn2"]


def chunk_width(kind):
    return SLOTW if kind == "dn" else 2048


def plan_chunks(n_seq, n_tiles, do_sample):
    plan = []
    for s in range(n_seq):
        for w in ("wk", "wv"):
            plan += [(w, j) for j in range(W_NCH[w])]
        for t in range(n_tiles):
            for w in TILE_ORDER:
                plan += [(w, j) for j in range(W_NCH[w])]
    if do_sample:
        for w in TILE_ORDER:
            plan += [(w, j) for j in range(W_NCH[w])]
    return plan


def build_nc(n_seq=NSEQ, n_tiles=SEQ // TT, do_sample=True, limit=10 ** 9, skip_prologue=False):
    nc = bass.Bass("TRN2", target_bir_lowering=False)

    def din(name, shape, dt=F32):
        return nc.dram_tensor(name, list(shape), dt, kind="ExternalInput").ap()

    def dout(name, shape, dt=F32):
        return nc.dram_tensor(name, list(shape), dt, kind="ExternalOutput").ap()

    xp = din("xp", [NSEQ, SEQ, D])
    xsm = din("xsm", [128, D])
    ck = din("ck", [NSMP, PAST, 512])
    cv = din("cv", [NSMP, PAST, 512])
    cmk = din("cmk", [NSMP, NMEM, D])
    cmv = din("cmv", [NSMP, NMEM, D])
    mp = din("mp", [NSEQ, NMEM, D])
    wsrc = {
        "gu1": din("w_gu1", [D, 2 * DFF]), "dn1": din("w_dn1", [DFF, D]), "win": din("w_in", [D, 2560]),
        "wout": din("w_out", [D, D]), "wq": din("w_q", [D, D]), "wk": din("w_k", [D, D]),
        "wv": din("w_v", [D, D]), "wo": din("w_o", [D, D]), "gu2": din("w_gu2", [D, 2 * DFF]),
        "dn2": din("w_dn2", [DFF, D]),
    }
    lnp_d = din("lnp", [128, 64])
    subg_d = din("subg", [128, 1])
    ggb_d = din("ggb", [128, 2, 512])
    wsp_d = din("ws_p", [128, 4, 128])
    wss_d = din("ws_s", [128, 4, 128])
    bsp_d = din("bs_p", [1, 512])
    bss_d = din("bs_s", [1, 512])
    lam_d = din("lamv", [128, 4, 64])
    ident_d = din("ident", [128, 128])
    alibi_d = din("alibi", [128, 36 * 4])
    dmask_d = din("dmask", [128, 4, 128])
    tril_d = din("tril", [128, 128])

    y_p = dout("y_p", [NSEQ, SEQ, D])
    y_s = dout("y_s", [128, D])
    nk_p = dout("nk_p", [NSEQ, SEQ, 512])
    nv_p = dout("nv_p", [NSEQ, SEQ, 512])
    nmk_p = dout("nmk_p", [NSEQ, NMEM, D])
    nmv_p = dout("nmv_p", [NSEQ, NMEM, D])
    nk_s = dout("nk_s", [128, 512])
    nv_s = dout("nv_s", [128, 512])
    ngv_s = dout("ngv_s", [128, 512])

    scr = {}
    for name, kind, nch in W_SPECS:
        scr[name] = nc.dram_tensor("scr_" + name, [nch, 128, chunk_width(kind)], BF16, kind="Internal").ap()
    RSCR = {name: Res("scr_" + name) for name, _, _ in W_SPECS}

    with contextlib.ExitStack() as st:
        S = Sched(nc, st, n_lanes=90)

        def sb(stack, name, shape, dt):
            return stack.enter_context(nc.sbuf_tensor("sb_" + name, list(shape), dt))

        ring = [sb(st, "ring%d" % i, [128, SLOTW], BF16) for i in range(NSLOT)]
        RS = [Res("ring%d" % i) for i in range(NSLOT)]
        LS = [S.new_lane() for _ in range(NSLOT)]

        with contextlib.ExitStack() as pst:
            NST = 5
            LOOK = NST - 1
            stg = [sb(pst, "stg%d" % i, [128, SLOTW], F32) for i in range(NST)]
            stb = [sb(pst, "stb%d" % i, [128, SLOTW], BF16) for i in range(NST)]
            RSTG = [Res("stg%d" % i) for i in range(NST)]
            RSTG2 = [Res("stg2_%d" % i) for i in range(NST)]
            RSTB = [Res("stb%d" % i) for i in range(NST)]
            LLD = [S.new_lane() for _ in range(NST)]
            LLD2 = [S.new_lane() for _ in range(NST)]
            LSTO = [S.new_lane() for _ in range(NST)]
            cast_eng = RR(["dve", "act"])
            chunks = [] if skip_prologue else [(name, kind, j) for name, kind, nch in W_SPECS for j in range(nch)]

            def p_load(ci):
                name, kind, j = chunks[ci]
                i = ci % NST
                src = wsrc[name]
                if kind == "gu":
                    v = stg[i][:, 0:2048].rearrange("p (k n) -> p k n", n=256)
                    sv = src.rearrange("(k p) n -> p k n", p=128)
                    S.dma("sp", LLD[i], lambda e: e.dma_start(out=v[:, :, 0:128], in_=sv[:, :, j * 128:(j + 1) * 128]), writes=[RSTG[i]])
                    S.dma("sp", LLD2[i], lambda e: e.dma_start(out=v[:, :, 128:256], in_=sv[:, :, DFF + j * 128:DFF + (j + 1) * 128]), writes=[RSTG2[i]])
                elif kind == "dn":
                    v = stg[i][:, 0:SLOTW].rearrange("p (f n) -> p f n", n=128)
                    sv = src.rearrange("(f p) n -> p f n", p=128)
                    S.dma("sp", LLD[i], lambda e: e.dma_start(out=v[:, :, :], in_=sv[:, :, j * 128:(j + 1) * 128]), writes=[RSTG[i]])
                else:
                    v = stg[i][:, 0:2048].rearrange("p (k n) -> p k n", n=256)
                    sv = src.rearrange("(k p) n -> p k n", p=128)
                    S.dma("sp", LLD[i], lambda e: e.dma_start(out=v[:, :, :], in_=sv[:, :, j * 256:(j + 1) * 256]), writes=[RSTG[i]])

            def p_cast_store(ci):
                name, kind, j = chunks[ci]
                i = ci % NST
                wdt = chunk_width(kind)
                ce = cast_eng.next()
                if ce == "act":
                    S.op("act", lambda e: e.activation(out=stb[i][:, 0:wdt], in_=stg[i][:, 0:wdt], func=AF.Copy),
                         reads=[RSTG[i], RSTG2[i]], writes=[RSTB[i]])
                else:
                    S.op(ce, lambda e: e.tensor_copy(out=stb[i][:, 0:wdt], in_=stg[i][:, 0:wdt]),
                         reads=[RSTG[i], RSTG2[i]], writes=[RSTB[i]])
                S.dma("sp", LSTO[i], lambda e: e.dma_start(out=scr[name][j, :, :], in_=stb[i][:, 0:wdt]),
                      reads=[RSTB[i]], writes=[RSCR[name]])

            for step in range(len(chunks) + LOOK):
                if step < len(chunks):
                    p_load(step)
                if step >= LOOK:
                    p_cast_store(step - LOOK)
            S.finish()
            with nc.Block() as block:
                S.emit(block)

        mst = st
        xf = sb(mst, "xf", [128, KC, TT], F32)
        xb = sb(mst, "xb", [128, KC, TT], BF16)
        hT = sb(mst, "hT", [128, NFC, TT], BF16)
        sq = sb(mst, "sq", [128, 4, TT], F32)
        kTb = sb(mst, "kTb", [128, 8256], BF16)
        vSb = sb(mst, "vSb", [128, 8448], BF16)
        memkT = sb(mst, "memkT", [128, 8, NMEM], BF16)
        memv = sb(mst, "memv", [128, 2, D], BF16)
        uT = sb(mst, "uT", [128, 4, TT], F32)
        Pt = [[sb(mst, "P%d_%d" % (m, j), [128, TT], BF16) for j in range(2)] for m in range(2)]
        s1 = sb(mst, "s1", [128, TT], F32)
        s2 = sb(mst, "s2", [128, TT], F32)
        m2 = sb(mst, "m2", [128, TT], F32)
        var = sb(mst, "var", [128, TT], F32)
        rstd = sb(mst, "rstd", [128, TT], F32)
        t1 = [sb(mst, "t1_%d" % j, [128, TT], F32) for j in range(4)]
        sg = [sb(mst, "sg%d" % j, [128, TT], F32) for j in range(2)]
        xs = [sb(mst, "xs%d" % j, [128, D], F32) for j in range(2)]
        kst = [sb(mst, "kst%d" % j, [128, 512], F32) for j in range(2)]
        vst = [sb(mst, "vst%d" % j, [128, 512], F32) for j in range(2)]
        gvf = [sb(mst, "gvf%d" % j, [128, 512], F32) for j in range(2)]
        vnb = [sb(mst, "vnb%d" % j, [128, 512], BF16) for j in range(2)]
        dtmp = [sb(mst, "dtmp%d" % j, [128, 128], F32) for j in range(2)]
        small = sb(mst, "small", [128, 32], F32)
        lamc = sb(mst, "lamc", [128, 16], F32)
        kTn = sb(mst, "kTn", [128, 4, 128], BF16)
        vnS = sb(mst, "vnS", [32, NSMP, 512], BF16)
        lnp = sb(mst, "lnp", [128, 64], F32)
        subg = sb(mst, "subg", [128, 1], F32)
        ggb = sb(mst, "ggb", [128, 2, 512], F32)
        WsT = [sb(mst, "WsT%d" % j, [128, 4, 128], BF16) for j in range(2)]
        bsr = [sb(mst, "bsr%d" % j, [33, 512], F32) for j in range(2)]
        bsb = [sb(mst, "bsb%d" % j, [33, 512], BF16) for j in range(2)]
        bstmp = sb(mst, "bstmp", [33, 512], BF16)
        ident = sb(mst, "ident", [128, 128], F32)
        alibi = sb(mst, "alibi", [128, 36 * 4], F32)
        dmask = sb(mst, "dmask", [128, 4, 128], F32)
        tril = sb(mst, "tril", [128, 128], F32)
        onesD = sb(mst, "onesD", [128, 128], F32)
        onesE = sb(mst, "onesE", [128, 128], F32)
        ones1 = sb(mst, "ones1", [1, 128], F32)
        onesB = sb(mst, "onesB", [128, 128], BF16)
        PS = [mst.enter_context(nc.psum_tensor("ps%d" % i, [128, 512], F32)) for i in range(8)]

        RX = [Res("xf%d" % c) for c in range(KC)]
        RXB = [Res("xb%d" % c) for c in range(KC)]
        RH = [Res("hT%d" % c) for c in range(NFC)]
        RSQ = [Res("sq%d" % c) for c in range(4)]
        RKT = [Res("kT%d" % h) for h in range(4)]
        RVS = Res("vS")
        RKTC = [Res("kTc%d" % i) for i in range(2)]
        RVC = [Res("vC%d" % i) for i in range(2)]
        RMK = Res("memkT")
        RMV = Res("memv")
        RU = [Res("uT%d" % g) for g in range(4)]
        RP = [[[Res("P%d_%d_%d" % (m, j, q)) for q in range(4)] for j in range(2)] for m in range(2)]
        PTc = [Pt[0][0], Pt[0][1]]
        RPTC = [RP[0][0], RP[0][1]]
        RS1, RS2, RM2, RVAR, RRSTD = Res("s1"), Res("s2"), Res("m2"), Res("var"), Res("rstd")
        RT1 = [Res("t1_%d" % j) for j in range(4)]
        RSG = [Res("sg%d" % j) for j in range(2)]
        RXS = [Res("xs%d" % j) for j in range(2)]
        RKST = [Res("kst%d" % j) for j in range(2)]
        RVST = [Res("vst%d" % j) for j in range(2)]
        RGVF = [Res("gvf%d" % j) for j in range(2)]
        RVNB = [Res("vnb%d" % j) for j in range(2)]
        RDT = [Res("dtmp%d" % j) for j in range(2)]
        RSM = Res("small")
        RLAM = Res("lamc")
        RKTN, RVNS = Res("kTn"), Res("vnS")
        RC = Res("consts")
        RWST = Res("WsT")
        RWSRAW = RGVF[0]
        wsraw = gvf[0][:, :].rearrange("p (g s) -> p g s", s=128)
        lamv = gvf[1][:, 0:256].rearrange("p (g s) -> p g s", s=64)
        RPS = [Res("ps%d" % i, excl=True) for i in range(8)]
        LXS = [S.new_lane() for _ in range(2)]
        LKST = [S.new_lane() for _ in range(2)]
        LVST = [S.new_lane() for _ in range(2)]
        LGV = [S.new_lane() for _ in range(2)]
        LSQ = [S.new_lane() for _ in range(4)]
        LC = S.new_lane()

        banks = RR(range(8))
        banksS = [RR([0, 1]), RR([2, 3])]

        QT0, AT0, GT0, QC0, OC0 = 0, 4, 8, 12, 0

        plan = plan_chunks(n_seq, n_tiles, do_sample)
        wstate = {"issued": 0, "consumed": 0}
        slot_of = {}

        def w_issue(slot):
            i = wstate["issued"]
            if i >= len(plan):
                return
            name, j = plan[i]
            wdt = chunk_width(W_KIND[name])
            wstate["issued"] += 1
            slot_of[i] = slot
            S.dma("sp", LS[slot], lambda e, slot=slot, name=name, j=j, wdt=wdt: e.dma_start(out=ring[slot][:, 0:wdt], in_=scr[name][j, :, :]),
                  reads=[RSCR[name]], writes=[RS[slot]])

        def w_acquire(name, j):
            i = wstate["consumed"]
            assert plan[i] == (name, j), (plan[i], name, j)
            wstate["consumed"] += 1
            return slot_of.pop(i)

        def w_release(slot):
            w_issue(slot)

        def mm(out, lhsT, rhs, start, stop, reads, writes):
            S.op("pe", lambda e: e.matmul(out, lhsT=lhsT, rhs=rhs, start=start, stop=stop), reads, writes)

        def tr(out, in_, reads, writes):
            S.op("pe", lambda e: e.transpose(out=out, in_=in_, identity=ident[:]), list(reads) + [RC], writes)

        def act(out, in_, func, reads, writes, bias=None, scale=None):
            kw = {}
            if bias is not None:
                kw["bias"] = bias
            if scale is not None:
                kw["scale"] = scale
            S.op("act", lambda e: e.activation(out=out, in_=in_, func=func, **kw), reads, writes)

        def cp(eng, out, in_, reads, writes):
            if eng == "act":
                act(out, in_, AF.Copy, reads, writes)
            else:
                S.op(eng, lambda e: e.tensor_copy(out=out, in_=in_), reads, writes)

        def tt_(eng, out, in0, in1, op, reads, writes):
            S.op(eng, lambda e: e.tensor_tensor(out=out, in0=in0, in1=in1, op=op), reads, writes)

        def ts_(eng, out, in0, s1_, s2_, op0, op1, reads, writes):
            if op1 is None:
                S.op(eng, lambda e: e.tensor_scalar(out=out, in0=in0, scalar1=s1_, scalar2=None, op0=op0), reads, writes)
            else:
                S.op(eng, lambda e: e.tensor_scalar(out=out, in0=in0, scalar1=s1_, scalar2=s2_, op0=op0, op1=op1), reads, writes)

        def stt(out, in0, scalar, in1, op0, op1, reads, writes):
            S.op("dve", lambda e: e.scalar_tensor_tensor(out=out, in0=in0, scalar=scalar, in1=in1, op0=op0, op1=op1), reads, writes)

        def dma(lane, out, in_, reads, writes):
            S.dma("sp", lane, lambda e: e.dma_start(out=out, in_=in_), reads, writes)

        for dst, src in ((lnp, lnp_d), (subg, subg_d), (ggb, ggb_d), (ident, ident_d),
                         (alibi, alibi_d), (dmask, dmask_d), (tril, tril_d)):
            dma(LC, dst[:], src, [], [RC])
        for wi, bsd in enumerate((bsp_d, bss_d)):
            dma(LC, bsr[wi][0:1, :], bsd, [], [RC])
            dma(LC, bsr[wi][32:33, :], bsd, [], [RC])
            S.op("pool", lambda e, wi=wi: e.memset(bsb[wi][:, :], 0.0), [], [RC])
            cp("dve", bsb[wi][0:1, :], bsr[wi][0:1, :], [RC], [RC])
            cp("dve", bstmp[32:33, :], bsr[wi][32:33, :], [RC], [RC])
            tt_("dve", bsb[wi][32:33, :], bsr[wi][32:33, :], bstmp[32:33, :], ALU.subtract, [RC], [RC])
        dma(LGV[1], lamv, lam_d, [], [RGVF[1]])
        S.op("pool", lambda e: e.memset(onesD[:], 1.0 / D), [], [RC])
        S.op("pool", lambda e: e.memset(onesE[:], 1.0 / 128), [], [RC])
        S.op("pool", lambda e: e.memset(ones1[:], 1.0), [], [RC])
        S.op("pool", lambda e: e.memset(onesB[:], 1.0), [], [RC])
        tt_("dve", sq[:, 0, 0:64], lamv[:, 0, :], lamv[:, 1, :], ALU.mult, [RGVF[1]], [RSQ[0]])
        tt_("dve", sq[:, 0, 64:128], lamv[:, 2, :], lamv[:, 3, :], ALU.mult, [RGVF[1], RSQ[0]], [RSQ[0]])
        S.op("dve", lambda e: e.tensor_reduce(out=lamc[:, 2:3], in_=sq[:, 0, 0:64], axis=AX.X, op=ALU.add), [RSQ[0]], [RLAM])
        S.op("dve", lambda e: e.tensor_reduce(out=lamc[:, 3:4], in_=sq[:, 0, 64:128], axis=AX.X, op=ALU.add), [RSQ[0], RLAM], [RLAM])
        act(lamc[:, 4:6], lamc[:, 2:4], AF.Exp, [RLAM], [RLAM])
        tt_("dve", lamc[:, 6:7], lamc[:, 5:6], lamc[:, 4:5], ALU.subtract, [RLAM], [RLAM])
        ts_("dve", lamc[:, 0:1], lamc[:, 6:7], -LAM_INIT, None, ALU.add, None, [RLAM], [RLAM])
        ts_("dve", lamc[:, 1:2], subg[:, 0:1], 1.0 - LAM_INIT, None, ALU.mult, None, [RLAM, RC], [RLAM])
        for k_, eps in enumerate((EPS, EPS_DN)):
            S.op("pool", lambda e, k_=k_, eps=eps: e.memset(lamc[:, 8 + k_:9 + k_], eps), [RLAM], [RLAM])
        LNC = (-16.0, -16.0, -4.0, -1.0)
        for h_ in range(4):
            S.op("pool", lambda e, h_=h_: e.memset(lamc[:, 10 + h_:11 + h_], LNC[h_]), [RLAM], [RLAM])
        neg_lam = lamc[:, 0:1]
        g08 = lamc[:, 1:2]
        eps_cols = {EPS: lamc[:, 8:9], EPS_DN: lamc[:, 9:10]}
        for wi, wsd in enumerate((wsp_d, wss_d)):
            dma(LGV[0], wsraw, wsd, [], [RWSRAW])
            for g in range(4):
                b = banks.next()
                tr(PS[b][:, 0:128], wsraw[:, g, :], [RWSRAW], [RPS[b]])
                tt_("dve", WsT[wi][:, g, :], PS[b][:, 0:128], tril[:], ALU.mult, [RPS[b], RC], [RWST])

        for sl in range(NSLOT):
            w_issue(sl)

        xs_rr = RR([0, 1])
        kst_rr = RR([0, 1])
        vst_rr = RR([0, 1])
        gv_rr = RR([0, 1])
        sg_rr = RR([0, 1])
        t1_rr = RR([0, 1, 2, 3])
        ev_rr = RR(["dve", "act"])

        xstage = [
            (xs[0][:, :], [RXS[0]], LXS[0]),
            (xs[1][:, :], [RXS[1]], LXS[1]),
            (sq[:, 0:2, :].rearrange("p c t -> p (c t)"), [RSQ[0], RSQ[1]], LSQ[0]),
            (sq[:, 2:4, :].rearrange("p c t -> p (c t)"), [RSQ[2], RSQ[3]], LSQ[1]),
        ]

        def x_dma(src_ap, k):
            buf, rr_, lane = xstage[k]
            dma(lane, buf, src_ap, [], rr_)

        def x_transpose(blk, k, dst_f32=True, dst_bf=True):
            buf, rr_, lane = xstage[k]
            for half in range(2):
                b = banks.next()
                for cc in range(4):
                    c = half * 4 + cc
                    tr(PS[b][:, cc * 128:(cc + 1) * 128], buf[:, c * 128:(c + 1) * 128], rr_, [RPS[b]])
                pv = PS[b][:, 0:512].rearrange("p (c t) -> p c t", t=128)
                cs = slice(half * 4, half * 4 + 4)
                ts = slice(blk * 128, (blk + 1) * 128)
                if dst_f32:
                    cp("dve", xf[:, cs, ts], pv, [RPS[b]], RX[cs])
                if dst_bf:
                    cp("act", xb[:, cs, ts], pv, [RPS[b]], RXB[cs])

        def load_T(src_fn, NB, dst_f32=True, prefetched=False):
            for blk in range(NB):
                k = blk if NB == 4 else xs_rr.next()
                if not prefetched:
                    x_dma(src_fn(blk), k)
                x_transpose(blk, k, dst_f32)

        def ln_acc(T, oc):
            if oc == 0:
                act(s1[:, 0:T], xf[:, 0, 0:T], AF.Copy, [RX[0]], [RS1])
                act(s2[:, 0:T], xf[:, 0, 0:T], AF.Square, [RX[0]], [RS2])
            else:
                j = t1_rr.next()
                act(t1[j][:, 0:T], xf[:, oc, 0:T], AF.Square, [RX[oc]], [RT1[j]])
                tt_("pool", s1[:, 0:T], s1[:, 0:T], xf[:, oc, 0:T], ALU.add, [RS1, RX[oc]], [RS1])
                tt_("dve", s2[:, 0:T], s2[:, 0:T], t1[j][:, 0:T], ALU.add, [RS2, RT1[j]], [RS2])

        def layer_norm(T, gi, eps, want_bf=True):
            bm, be = banks.next(), banks.next()
            mm(PS[bm][:, 0:T], onesD[:], s1[:, 0:T], True, True, [RC, RS1], [RPS[bm]])
            mm(PS[be][:, 0:T], onesD[:], s2[:, 0:T], True, True, [RC, RS2], [RPS[be]])
            act(m2[:, 0:T], PS[bm][:, 0:T], AF.Square, [RPS[bm]], [RM2])
            tt_("dve", var[:, 0:T], PS[be][:, 0:T], m2[:, 0:T], ALU.subtract, [RPS[be], RM2], [RVAR])
            act(var[:, 0:T], var[:, 0:T], AF.Ln, [RVAR, RLAM], [RVAR], bias=eps_ap(eps), scale=1.0)
            act(rstd[:, 0:T], var[:, 0:T], AF.Exp, [RVAR], [RRSTD], scale=-0.5)
            for c in range(KC):
                j = t1_rr.next()
                tt_("dve", t1[j][:, 0:T], xf[:, c, 0:T], PS[bm][:, 0:T], ALU.subtract, [RX[c], RPS[bm]], [RT1[j]])
                tt_("dve" if c in (0, 3, 6) else "pool", t1[j][:, 0:T], t1[j][:, 0:T], rstd[:, 0:T], ALU.mult, [RT1[j], RRSTD], [RT1[j]])
                gcol = lnp[:, gi * 16 + c:gi * 16 + c + 1]
                bcol = lnp[:, gi * 16 + 8 + c:gi * 16 + 8 + c + 1]
                if want_bf:
                    act(xb[:, c, 0:T], t1[j][:, 0:T], AF.Identity, [RT1[j], RC], [RXB[c]], bias=bcol, scale=gcol)
                act(xf[:, c, 0:T], t1[j][:, 0:T], AF.Identity, [RT1[j], RC], [RX[c]], bias=bcol, scale=gcol)

        def eps_ap(eps):
            return eps_cols[eps]

        FFN_A = 14

        def ffn_gu(T, gu, fc0, fc1):
            NG = 3
            if fc0 == 0:
                sl = [w_acquire(gu, fc) for fc in range(NG)]
                bgs = [(banks.next(), banks.next()) for _ in range(NG)]
                for kc in range(KC):
                    for g in range(NG):
                        mm(PS[bgs[g][0]][:, 0:T], ring[sl[g]][:, kc * 256:kc * 256 + 128], xb[:, kc, 0:T], kc == 0, kc == KC - 1, [RS[sl[g]], RXB[kc]], [RPS[bgs[g][0]]])
                        mm(PS[bgs[g][1]][:, 0:T], ring[sl[g]][:, kc * 256 + 128:kc * 256 + 256], xb[:, kc, 0:T], kc == 0, kc == KC - 1, [RS[sl[g]], RXB[kc]], [RPS[bgs[g][1]]])
                for g in range(NG):
                    w_release(sl[g])
                for g in range(NG):
                    j = sg_rr.next()
                    act(sg[j][:, 0:T], PS[bgs[g][0]][:, 0:T], AF.Silu, [RPS[bgs[g][0]]], [RSG[j]])
                    tt_("dve", hT[:, g, 0:T], PS[bgs[g][1]][:, 0:T], sg[j][:, 0:T], ALU.mult, [RPS[bgs[g][1]], RSG[j]], [RH[g]])
                fc0 = NG
            for fc in range(fc0, fc1):
                s = w_acquire(gu, fc)
                bg, bu = banks.next(), banks.next()
                for kc in range(KC):
                    mm(PS[bg][:, 0:T], ring[s][:, kc * 256:kc * 256 + 128], xb[:, kc, 0:T], kc == 0, kc == KC - 1, [RS[s], RXB[kc]], [RPS[bg]])
                for kc in range(KC):
                    mm(PS[bu][:, 0:T], ring[s][:, kc * 256 + 128:kc * 256 + 256], xb[:, kc, 0:T], kc == 0, kc == KC - 1, [RS[s], RXB[kc]], [RPS[bu]])
                w_release(s)
                j = sg_rr.next()
                act(sg[j][:, 0:T], PS[bg][:, 0:T], AF.Silu, [RPS[bg]], [RSG[j]])
                tt_("dve", hT[:, fc, 0:T], PS[bu][:, 0:T], sg[j][:, 0:T], ALU.mult, [RPS[bu], RSG[j]], [RH[fc]])

        def ffn(T, gu, dn, fc_start=0):
            ffn_gu(T, gu, fc_start, NFC)
            for oc in range(KC):
                s = w_acquire(dn, oc)
                b = banks.next()
                for fc in range(NFC):
                    mm(PS[b][:, 0:T], ring[s][:, fc * 128:(fc + 1) * 128], hT[:, fc, 0:T], fc == 0, fc == NFC - 1, [RS[s], RH[fc]], [RPS[b]])
                w_release(s)
                stt(xf[:, oc, 0:T], PS[b][:, 0:T], 0.5 / ALPHA, xf[:, oc, 0:T], ALU.mult, ALU.add, [RPS[b], RX[oc]], [RX[oc]])
                ln_acc(T, oc)

        def proj_fm(wname, j, T, rhs_fn, rhs_res_fn, evac):
            proj_fm_multi(wname, [j], T, rhs_fn, rhs_res_fn, lambda jj, ol, b: evac(ol, b))

        def proj_fm_multi(wname, js, T, rhs_fn, rhs_res_fn, evac):
            sl = [w_acquire(wname, j) for j in js]
            bs_ = [[banks.next(), banks.next()] for _ in js]
            for kc in range(KC):
                for gi_, s in enumerate(sl):
                    for ol in range(2):
                        b = bs_[gi_][ol]
                        mm(PS[b][:, 0:T], ring[s][:, kc * 256 + ol * 128:kc * 256 + (ol + 1) * 128], rhs_fn(kc), kc == 0, kc == KC - 1,
                           [RS[s], rhs_res_fn(kc)], [RPS[b]])
            for s in sl:
                w_release(s)
            for gi_, j in enumerate(js):
                for ol in range(2):
                    evac(j, ol, bs_[gi_][ol])

        def resid_proj(wname, T, rhs_fn, rhs_res_fn):
            for j in range(4):
                def ev(ol, b, j=j):
                    oc = 2 * j + ol
                    stt(xf[:, oc, 0:T], PS[b][:, 0:T], 1.0 / ALPHA, xf[:, oc, 0:T], ALU.mult, ALU.add, [RPS[b], RX[oc]], [RX[oc]])
                    ln_acc(T, oc)
                proj_fm(wname, j, T, rhs_fn, rhs_res_fn, ev)

        def xb_rhs(T):
            return (lambda kc: xb[:, kc, 0:T]), (lambda kc: RXB[kc])

        def w_in_stage(T, sample, pos0, seq, k_dst, v_dst, gblk0):
            NB = T // 128
            rf, rr_ = xb_rhs(T)
            def evq(j, ol, b):
                h = 2 * j + ol
                cp("act", hT[:, QT0 + h, 0:T], PS[b][:, 0:T], [RPS[b]], [RH[QT0 + h]])
            proj_fm_multi("win", [0, 1], T, rf, rr_, evq)
            tmb = [banks.next() for _ in range(NB)]
            for j in range(2):
                s = w_acquire("win", 2 + j)
                fb = []
                for ol in range(2):
                    b = banks.next()
                    fb.append(b)
                    for kc in range(KC):
                        mm(PS[b][:, 0:T], ring[s][:, kc * 256 + ol * 128:kc * 256 + (ol + 1) * 128], xb[:, kc, 0:T], kc == 0, kc == KC - 1,
                           [RS[s], RXB[kc]], [RPS[b]])
                for blk in range(NB):
                    for kc in range(KC):
                        mm(PS[tmb[blk]][:, j * 256:(j + 1) * 256], xb[:, kc, blk * 128:(blk + 1) * 128], ring[s][:, kc * 256:(kc + 1) * 256],
                           kc == 0, kc == KC - 1, [RS[s], RXB[kc]], [RPS[tmb[blk]]])
                w_release(s)
                for ol in range(2):
                    h = 2 * j + ol
                    if sample:
                        cp("act", kTn[:, h, 0:T], PS[fb[ol]][:, 0:T], [RPS[fb[ol]]], [RKTN])
                    else:
                        cp("act", kTb[:, h * SEQ + pos0:h * SEQ + pos0 + T], PS[fb[ol]][:, 0:T], [RPS[fb[ol]]], [RKT[h]])
            for blk in range(NB):
                i = kst_rr.next()
                cp("dve", kst[i][:, :], PS[tmb[blk]][:, 0:512], [RPS[tmb[blk]]], [RKST[i]])
                dma(LKST[i], k_dst(blk), kst[i][:, :], [RKST[i]], [])
            if not sample:
                tmb = [banks.next() for _ in range(NB)]
                for j in range(2):
                    s = w_acquire("win", 4 + j)
                    for blk in range(NB):
                        for kc in range(KC):
                            mm(PS[tmb[blk]][:, j * 256:(j + 1) * 256], xb[:, kc, blk * 128:(blk + 1) * 128], ring[s][:, kc * 256:(kc + 1) * 256],
                               kc == 0, kc == KC - 1, [RS[s], RXB[kc]], [RPS[tmb[blk]]])
                    w_release(s)
                for blk in range(NB):
                    i = vst_rr.next()
                    cp("dve", vst[i][:, :], PS[tmb[blk]][:, 0:512], [RPS[tmb[blk]]], [RVST[i]])
                    dma(LVST[i], v_dst(blk), vst[i][:, :], [RVST[i]], [])
                    gb = gblk0 + blk
                    cp("act", vSb[:, gb * 512:(gb + 1) * 512], PS[tmb[blk]][:, 0:512], [RPS[tmb[blk]]], [RVS])
            else:
                tmb = [banks.next() for _ in range(NSMP)]
                for j in range(2):
                    s = w_acquire("win", 4 + j)
                    for b_ in range(NSMP):
                        for kc in range(KC):
                            mm(PS[tmb[b_]][0:32, j * 256:(j + 1) * 256], xb[:, kc, b_ * 32:(b_ + 1) * 32], ring[s][:, kc * 256:(kc + 1) * 256],
                               kc == 0, kc == KC - 1, [RS[s], RXB[kc]], [RPS[tmb[b_]]])
                    w_release(s)
                for b_ in range(NSMP):
                    i = vst_rr.next()
                    cp("dve", vst[i][0:32, :], PS[tmb[b_]][0:32, 0:512], [RPS[tmb[b_]]], [RVST[i]])
                    dma(LVST[i], nv_s[b_ * 32:(b_ + 1) * 32, :], vst[i][0:32, :], [RVST[i]], [])
                    cp("act", vnS[0:32, b_, :], PS[tmb[b_]][0:32, 0:512], [RPS[tmb[b_]]], [RVNS])
            for j in range(2):
                def ev(ol, b, j=j):
                    g = 2 * j + ol
                    act(uT[:, g, 0:T], PS[b][:, 0:T], AF.Gelu_apprx_tanh, [RPS[b]], [RU[g]])
                proj_fm("win", 6 + j, T, rf, rr_, ev)
            tmb = [banks.next() for _ in range(NB)]
            for j in range(2):
                s = w_acquire("win", 8 + j)
                for blk in range(NB):
                    for kc in range(KC):
                        mm(PS[tmb[blk]][:, j * 256:(j + 1) * 256], xb[:, kc, blk * 128:(blk + 1) * 128], ring[s][:, kc * 256:(kc + 1) * 256],
                           kc == 0, kc == KC - 1, [RS[s], RXB[kc]], [RPS[tmb[blk]]])
                w_release(s)
            wi = 1 if sample else 0

            def gv_chain(blk, buf, rbuf, i):
                S.op("dve", lambda e: e.bn_stats(out=small[:, 16:22], in_=buf), [rbuf, RSM], [RSM])
                S.op("dve", lambda e: e.bn_aggr(out=small[:, 22:24], in_=small[:, 16:22]), [RSM], [RSM])
                act(small[:, 24:25], small[:, 23:24], AF.Ln, [RSM, RLAM], [RSM], bias=eps_ap(EPS), scale=1.0)
                act(small[:, 25:26], small[:, 24:25], AF.Exp, [RSM], [RSM], scale=-0.5)
                stt(small[:, 26:27], small[:, 22:23], -1.0, small[:, 25:26], ALU.mult, ALU.mult, [RSM], [RSM])
                act(buf, buf, AF.Identity, [rbuf, RSM], [rbuf], bias=small[:, 26:27], scale=small[:, 25:26])
                tt_("pool", buf, buf, ggb[:, 0, :], ALU.mult, [rbuf, RC], [rbuf])
                tt_("pool", buf, buf, ggb[:, 1, :], ALU.add, [rbuf, RC], [rbuf])
                cp("act", vnb[i][:, :], buf, [rbuf], [RVNB[i]])
                if sample:
                    dma(LGV[i], ngv_s[:, :], buf, [rbuf], [])

            def gv_spatial(blk, i, b2):
                for g in range(4):
                    mm(PS[b2][:, g * 128:(g + 1) * 128], vnb[i][:, g * 128:(g + 1) * 128], WsT[wi][:, g, :], True, False, [RVNB[i], RWST], [RPS[b2]])
                    mm(PS[b2][:, g * 128:(g + 1) * 128], onesB[0:33, :], bsb[wi][0:33, g * 128:(g + 1) * 128], False, True, [RC], [RPS[b2]])
                ts = slice(blk * 128, (blk + 1) * 128)
                tt_("dve", hT[:, GT0:GT0 + 4, ts], PS[b2][:, 0:512].rearrange("p (g t) -> p g t", t=128), uT[:, :, ts], ALU.mult,
                    [RPS[b2]] + RU, RH[GT0:GT0 + 4])

            deferred = []
            for blk in range(NB):
                b = tmb[blk]
                if sample:
                    i = gv_rr.next()
                    act(gvf[i][:, :], PS[b][:, 0:512], AF.Gelu_apprx_tanh, [RPS[b]], [RGVF[i]])
                    gv_chain(blk, gvf[i][:, :], RGVF[i], i)
                    gv_spatial(blk, i, banks.next())
                else:
                    act(sq[:, blk, :], PS[b][:, 0:512], AF.Gelu_apprx_tanh, [RPS[b]], [RSQ[blk]])
                    i = blk % 2
                    deferred.append((lambda blk=blk, i=i: gv_chain(blk, sq[:, blk, :], RSQ[blk], i),
                                     lambda b2, blk=blk, i=i: gv_spatial(blk, i, b2)))
            return deferred

        def attn_finish(h, T, c0, o1, o2, z1, z2, ro1, ro2, rz1, rz2, sbank, centre=False, mid=None):
            for zap, rz, tj in ((z1, rz1, 0), (z2, rz2, 1)):
                if centre:
                    act(t1[tj][:, 0:T], zap, AF.Ln, [rz], [RT1[tj]], scale=math.exp(LNC[h]))
                    act(t1[tj][:, 0:T], t1[tj][:, 0:T], AF.Exp, [RT1[tj], RLAM], [RT1[tj]], scale=-1.0, bias=lamc[:, 10 + h:11 + h])
                else:
                    act(t1[tj][:, 0:T], zap, AF.Ln, [rz], [RT1[tj]])
                    act(t1[tj][:, 0:T], t1[tj][:, 0:T], AF.Exp, [RT1[tj]], [RT1[tj]], scale=-1.0)
            tt_("dve", s1[:, 0:T], o1, t1[0][:, 0:T], ALU.mult, [ro1, RT1[0]], [RS1])
            tt_("dve", s2[:, 0:T], o2, t1[1][:, 0:T], ALU.mult, [ro2, RT1[1]], [RS2])
            stt(m2[:, 0:T], s2[:, 0:T], neg_lam, s1[:, 0:T], ALU.mult, ALU.add, [RS1, RS2, RLAM], [RM2])
            act(var[:, 0:T], m2[:, 0:T], AF.Square, [RM2], [RVAR])
            b = sbank
            mm(PS[b][:, 0:T], onesE[:], var[:, 0:T], True, True, [RC, RVAR], [RPS[b]])
            if mid is not None:
                mid()
            act(rstd[:, 0:T], PS[b][:, 0:T], AF.Ln, [RPS[b], RLAM], [RRSTD], bias=eps_ap(EPS), scale=1.0)
            act(rstd[:, 0:T], rstd[:, 0:T], AF.Exp, [RRSTD], [RRSTD], scale=-0.5)
            stt(hT[:, AT0 + h, c0:c0 + T], m2[:, 0:T], g08, rstd[:, 0:T], ALU.mult, ALU.mult, [RM2, RRSTD, RLAM], [RH[AT0 + h]])

        def diff_attn_prompt(Q, deferred):
            T = TT
            O1, O2, Z1, Z2 = 4, 5, 6, 7
            OZ = (O1, O2, Z1, Z2)
            nJ = 4 * Q + 4
            steps = [(h, J) for h in range(4) for J in range(nJ)]

            def bcol(d, h):
                return alibi[:, (d + 3) * 4 + h:(d + 3) * 4 + h + 1]

            def emit_S(h, J):
                c0 = max(J - 4 * Q, 0) * 128
                sbk = [banksS[0].next(), banksS[1].next()]
                for m in range(2):
                    mm(PS[sbk[m]][:, c0:T], kTb[m * 64:(m + 1) * 64, h * SEQ + J * 128:h * SEQ + (J + 1) * 128],
                       hT[m * 64:(m + 1) * 64, QT0 + h, c0:T], True, True, [RKT[h], RH[QT0 + h]], [RPS[sbk[m]]])
                return sbk, c0

            def exp_diag(h, m, jb, sb_, qb, d):
                qs = slice(qb * 128, (qb + 1) * 128)
                act(dtmp[m][:, :], PS[sb_][:, qs], AF.Exp, [RPS[sb_], RC], [RDT[m]], bias=bcol(d, h), scale=0.125)
                tt_("dve", Pt[m][jb][:, qs], dtmp[m][:, :], dmask[:, h, :], ALU.mult, [RDT[m], RC], [RP[m][jb][qb]])

            def emit_E(h, J, sbk, c0):
                jb = J % 2
                G = 1 if h == 0 else 4
                qb_lo = c0 // 128
                for m in range(2):
                    sb_ = sbk[m]
                    for g0 in range(0, 4, G):
                        blocks = [qb for qb in range(g0, g0 + G) if qb >= qb_lo]
                        if not blocks:
                            continue
                        d = 4 * Q + g0 - J
                        rest = []
                        for qb in blocks:
                            if 4 * Q + qb == J:
                                exp_diag(h, m, jb, sb_, qb, d)
                            else:
                                rest.append(qb)
                        if rest:
                            lo_, hi_ = rest[0] * 128, (rest[-1] + 1) * 128
                            act(Pt[m][jb][:, lo_:hi_], PS[sb_][:, lo_:hi_], AF.Exp, [RPS[sb_], RC], RP[m][jb][rest[0]:rest[-1] + 1],
                                bias=bcol(d, h), scale=0.125)

            def emit_PV(h, J, c0):
                jb = J % 2
                first, last = (J == 0), (J == nJ - 1)
                for m in range(2):
                    prs = RP[m][jb][c0 // 128:4]
                    mm(PS[OZ[m]][:, c0:T], vSb[:, J * 512 + h * 128:J * 512 + (h + 1) * 128], Pt[m][jb][:, c0:T], first, last,
                       [RVS] + prs, [RPS[OZ[m]]])
                    mm(PS[OZ[2 + m]][:, c0:T], onesB[:], Pt[m][jb][:, c0:T], first, last, [RC] + prs, [RPS[OZ[2 + m]]])

            deferred[0][0]()
            cur = emit_S(*steps[0])
            e_done = set()
            for i, (h, J) in enumerate(steps):
                nxt = emit_S(*steps[i + 1]) if i + 1 < len(steps) else None
                if i not in e_done:
                    emit_E(h, J, *cur)
                emit_PV(h, J, cur[1])
                if J == nJ - 1:
                    mid = None
                    if nxt is not None:
                        def mid(i=i, nxt=nxt):
                            emit_E(steps[i + 1][0], steps[i + 1][1], *nxt)
                            e_done.add(i + 1)
                    attn_finish(h, T, 0, PS[O1][:, 0:T], PS[O2][:, 0:T], PS[Z1][:, 0:T], PS[Z2][:, 0:T], RPS[O1], RPS[O2], RPS[Z1], RPS[Z2], Z1, centre=True, mid=mid)
                    if h + 1 < 4:
                        deferred[h + 1][0]()
                    deferred[h][1](O2)
                cur = nxt

        def cross_attn(T, c0, mk_res, mv_res):
            for hh in range(4):
                pb = []
                for mb in range(2):
                    b = banks.next()
                    for j in range(2):
                        mm(PS[b][:, 0:T], memkT[:, hh * 2 + j, mb * 128:(mb + 1) * 128], hT[:, QC0 + hh * 2 + j, c0:c0 + T], j == 0, j == 1,
                           [mk_res, RH[QC0 + hh * 2 + j]], [RPS[b]])
                    act(PTc[mb][:, 0:T], PS[b][:, 0:T], AF.Exp, [RPS[b]], RPTC[mb], scale=1.0 / 16.0)
                ob = [banks.next(), banks.next()]
                zb = banks.next()
                for j in range(2):
                    for mb in range(2):
                        mm(PS[ob[j]][:, 0:T], memv[:, mb, hh * 256 + j * 128:hh * 256 + (j + 1) * 128], PTc[mb][:, 0:T], mb == 0, mb == 1,
                           [mv_res] + RPTC[mb], [RPS[ob[j]]])
                for mb in range(2):
                    mm(PS[zb][:, 0:T], onesB[:], PTc[mb][:, 0:T], mb == 0, mb == 1, [RC] + RPTC[mb], [RPS[zb]])
                j_ = t1_rr.next()
                act(t1[j_][:, 0:T], PS[zb][:, 0:T], AF.Ln, [RPS[zb]], [RT1[j_]])
                act(t1[j_][:, 0:T], t1[j_][:, 0:T], AF.Exp, [RT1[j_]], [RT1[j_]], scale=-1.0)
                for j in range(2):
                    tt_("dve", hT[:, OC0 + hh * 2 + j, c0:c0 + T], PS[ob[j]][:, 0:T], t1[j_][:, 0:T], ALU.mult, [RPS[ob[j]], RT1[j_]],
                        [RH[OC0 + hh * 2 + j]])

        def store_y(NB, dst_fn):
            for blk in range(NB):
                i = kst_rr.next()
                vst_rr.next()
                for half in range(2):
                    b = banks.next()
                    for cc in range(4):
                        c = half * 4 + cc
                        tr(PS[b][:, cc * 128:(cc + 1) * 128], xf[:, c, blk * 128:(blk + 1) * 128], [RX[c]], [RPS[b]])
                    stg_, rs_, ln_ = (kst[i], RKST[i], LKST[i]) if half == 0 else (vst[i], RVST[i], LVST[i])
                    cp(ev_rr.next(), stg_[:, :], PS[b][:, 0:512], [RPS[b]], [rs_])
                    dma(ln_, dst_fn(blk)[:, half * 512:(half + 1) * 512], stg_[:, :], [rs_], [])

        def mem_phase_prompt(seq, part=None):
            SUB = 99
            if part in (None, 1):
                load_T(lambda blk: mp[seq, blk * 128:(blk + 1) * 128, :], 2, dst_f32=False)
            todo = (("wk", True), ("wv", False)) if part is None else ((("wk", True),) if part == 1 else (("wv", False),))
            for wname, is_k in todo:
                dst = nmk_p if is_k else nmv_p
                for pair in range(2):
                    mb_ = [banks.next(), banks.next()]
                    for jj in range(2):
                        s = w_acquire(wname, pair * 2 + jj)
                        for blk in range(2):
                            for kc in range(KC):
                                mm(PS[mb_[blk]][:, jj * 256:(jj + 1) * 256], xb[:, kc, blk * 128:(blk + 1) * 128], ring[s][:, kc * 256:(kc + 1) * 256],
                                   kc == 0, kc == KC - 1, [RS[s], RXB[kc]], [RPS[mb_[blk]]])
                        if SUB >= 2:
                            w_release(s)
                    if SUB <= 2:
                        continue
                    for blk in range(2):
                        i = kst_rr.next()
                        cp("dve", kst[i][:, :], PS[mb_[blk]][:, 0:512], [RPS[mb_[blk]]], [RKST[i]])
                        dma(LKST[i], dst[seq, blk * 128:(blk + 1) * 128, pair * 512:(pair + 1) * 512], kst[i][:, :], [RKST[i]], [])
                        if SUB <= 3:
                            continue
                        if is_k:
                            b = banks.next()
                            for cl in range(4):
                                tr(PS[b][:, cl * 128:(cl + 1) * 128], kst[i][:, cl * 128:(cl + 1) * 128], [RKST[i]], [RPS[b]])
                            cp("act", memkT[:, pair * 4:pair * 4 + 4, blk * 128:(blk + 1) * 128],
                               PS[b][:, 0:512].rearrange("p (c t) -> p c t", t=128), [RPS[b]], [RMK])
                        else:
                            cp("act", memv[:, blk, pair * 512:(pair + 1) * 512], PS[mb_[blk]][:, 0:512], [RPS[mb_[blk]]], [RMV])

        def mem_phase_sample(b_):
            for blk in range(2):
                i = xs_rr.next()
                dma(LXS[i], xs[i][:, :], cmk[b_, blk * 128:(blk + 1) * 128, :], [], [RXS[i]])
                for half in range(2):
                    b = banks.next()
                    for cc in range(4):
                        c = half * 4 + cc
                        tr(PS[b][:, cc * 128:(cc + 1) * 128], xs[i][:, c * 128:(c + 1) * 128], [RXS[i]], [RPS[b]])
                    cp("act", memkT[:, half * 4:half * 4 + 4, blk * 128:(blk + 1) * 128],
                       PS[b][:, 0:512].rearrange("p (c t) -> p c t", t=128), [RPS[b]], [RMK])
            for blk in range(2):
                i = xs_rr.next()
                dma(LXS[i], xs[i][:, :], cmv[b_, blk * 128:(blk + 1) * 128, :], [], [RXS[i]])
                cp("pool", memv[:, blk, :], xs[i][:, :], [RXS[i]], [RMV])

        def diff_attn_sample():
            O1, O2, Z1, Z2 = 4, 5, 6, 7
            OZ = (O1, O2, Z1, Z2)
            T = DSEQ
            units = [(b_, h) for b_ in range(NSMP) for h in range(4)]

            def prefetch_fns(u):
                b_, h = units[u]
                bf = u % 2
                kbase = bf * 4128
                vbase = bf * 4224
                ckv = ck[b_].rearrange("(j p) f -> p j f", p=128)
                cvv = cv[b_].rearrange("(j p) f -> p j f", p=128)

                def grp_fn(grp, pbanks=None):
                    i = xs_rr.next()
                    dma(LXS[i], xs[i][:, :].rearrange("p (j f) -> p j f", f=128), ckv[:, grp * 8:(grp + 1) * 8, h * 128:(h + 1) * 128], [], [RXS[i]])
                    for half in range(2):
                        b = pbanks[half] if pbanks else banks.next()
                        for cc in range(4):
                            jj = half * 4 + cc
                            tr(PS[b][:, cc * 128:(cc + 1) * 128], xs[i][:, jj * 128:(jj + 1) * 128], [RXS[i]], [RPS[b]])
                        k0 = kbase + (grp * 8 + half * 4) * 128
                        cp(ev_rr.next(), kTb[:, k0:k0 + 512], PS[b][:, 0:512], [RPS[b]], [RKTC[bf]])
                    g2 = grp % 2
                    dma(LSQ[g2], sq[:, 2 * g2:2 * g2 + 2, :].rearrange("p c (j f) -> p (c j) f", f=128),
                        cvv[:, grp * 8:(grp + 1) * 8, h * 128:(h + 1) * 128], [], [RSQ[2 * g2], RSQ[2 * g2 + 1]])
                    cp("dve", vSb[:, vbase + grp * 1024:vbase + (grp + 1) * 1024].rearrange("p (c t) -> p c t", t=512), sq[:, 2 * g2:2 * g2 + 2, :],
                       [RSQ[2 * g2], RSQ[2 * g2 + 1]], [RVC[bf]])

                def tail_fn():
                    cp("pool", kTb[:, kbase + 4096:kbase + 4128], kTn[:, h, b_ * 32:(b_ + 1) * 32], [RKTN], [RKTC[bf]])

                return [lambda pb=None, g=g: grp_fn(g, pb) for g in range(4)] + [lambda pb=None: tail_fn()]

            for f in prefetch_fns(0):
                f()
            steps = [(u, J) for u in range(len(units)) for J in range(33)]
            sched = {3: 0, 11: 1, 19: 2, 27: 3, 30: 4}
            OB, ZB = 6, 7
            Pbuf = [Pt[0][0], Pt[0][1], Pt[1][0]]
            RPb = [RP[0][0], RP[0][1], RP[1][0]]

            def up(u):
                b_, h = units[u]
                bf = u % 2
                return b_, h, bf, bf * 4128, bf * 4224, slice(b_ * 32, (b_ + 1) * 32)

            def sbanks(n):
                return [2 * (n % 3), 2 * (n % 3) + 1]

            def S_(n):
                u, J = steps[n]
                b_, h, bf, kbase, vbase, qcol = up(u)
                nk = 128 if J < 32 else 32
                sbk = sbanks(n)
                for m in range(2):
                    mm(PS[sbk[m]][0:nk, 0:T], kTb[m * 64:(m + 1) * 64, kbase + J * 128:kbase + J * 128 + nk],
                       hT[m * 64:(m + 1) * 64, QT0 + h, qcol], True, True, [RKTC[bf], RH[QT0 + h]], [RPS[sbk[m]]])

            def E_(n):
                u, J = steps[n]
                b_, h, bf, kbase, vbase, qcol = up(u)
                nk = 128 if J < 32 else 32
                d = 32 - J
                sbk = sbanks(n)
                P = Pbuf[n % 3]
                rps = RPb[n % 3]
                bias = alibi[0:nk, (d + 3) * 4 + h:(d + 3) * 4 + h + 1]
                for m in range(2):
                    pc = slice(m * T, (m + 1) * T)
                    if d > 0:
                        act(P[0:nk, pc], PS[sbk[m]][0:nk, 0:T], AF.Exp, [RPS[sbk[m]], RC], [rps[m]], bias=bias, scale=0.125)
                    else:
                        act(dtmp[m][0:nk, 0:T], PS[sbk[m]][0:nk, 0:T], AF.Exp, [RPS[sbk[m]], RC], [RDT[m]], bias=bias, scale=0.125)
                        tt_("dve", P[0:nk, pc], dtmp[m][0:nk, 0:T], dmask[0:nk, h, 0:T], ALU.mult, [RDT[m], RC], [rps[m]])

            def PV_(n):
                u, J = steps[n]
                b_, h, bf, kbase, vbase, qcol = up(u)
                nk = 128 if J < 32 else 32
                P = Pbuf[n % 3]
                rps = RPb[n % 3][0:2]
                first, last = (J == 0), (J == 32)
                if J < 32:
                    lv = vSb[:, vbase + J * 128:vbase + (J + 1) * 128]
                    lres = RVC[bf]
                else:
                    lv = vnS[0:32, b_, h * 128:(h + 1) * 128]
                    lres = RVNS
                mm(PS[OB][:, 0:2 * T], lv, P[0:nk, 0:2 * T], first, last, [lres] + rps, [RPS[OB]])
                mm(PS[ZB][:, 0:2 * T], onesB[0:nk, :], P[0:nk, 0:2 * T], first, last, [RC] + rps, [RPS[ZB]])

            S_(0)
            if len(steps) > 1:
                S_(1)
            e_done = set()
            nxt_fns = prefetch_fns(1)
            for n, (u, J) in enumerate(steps):
                if n + 2 < len(steps):
                    S_(n + 2)
                if n not in e_done:
                    E_(n)
                PV_(n)
                if J in sched and nxt_fns:
                    nxt_fns[sched[J]](sbanks(n))
                if J == 32:
                    b_, h = units[u]
                    mid = None
                    if n + 1 < len(steps):
                        def mid(n=n):
                            E_(n + 1)
                            e_done.add(n + 1)
                    attn_finish(h, T, b_ * 32, PS[OB][:, 0:T], PS[OB][:, T:2 * T], PS[ZB][:, 0:T], PS[ZB][:, T:2 * T],
                                RPS[OB], RPS[OB], RPS[ZB], RPS[ZB], ZB, mid=mid)
                    nxt_fns = prefetch_fns(u + 2) if u + 2 < len(units) else []

        def mix_rhs(T):
            def f(kc):
                return hT[:, (AT0 + kc) if kc < 4 else (GT0 + kc - 4), 0:T]

            def r(kc):
                return RH[(AT0 + kc) if kc < 4 else (GT0 + kc - 4)]
            return f, r

        def oc_rhs(T):
            return (lambda kc: hT[:, OC0 + kc, 0:T]), (lambda kc: RH[OC0 + kc])

        def q_cross(T):
            rf, rr_ = xb_rhs(T)

            def ev(j, ol, b):
                c8 = 2 * j + ol
                cp("act", hT[:, QC0 + c8, 0:T], PS[b][:, 0:T], [RPS[b]], [RH[QC0 + c8]])
            proj_fm_multi("wq", [0, 1], T, rf, rr_, ev)
            proj_fm_multi("wq", [2, 3], T, rf, rr_, ev)

        class _Stop(Exception):
            pass

        nst = [0]

        def stage(fn, *a):
            if nst[0] >= limit:
                raise _Stop()
            nst[0] += 1
            fn(*a)

        try:
            mem_done = set()
            smp_pre = False
            for seq in range(n_seq):
                if seq not in mem_done:
                    stage(mem_phase_prompt, seq)
                pre = False
                for Q in range(n_tiles):
                    T = TT
                    pos0 = Q * TT
                    if not pre:
                        stage(load_T, lambda blk, seq=seq, pos0=pos0: xp[seq, pos0 + blk * 128:pos0 + (blk + 1) * 128, :], 4, True, Q > 0)
                    if Q + 1 < n_tiles and nst[0] < limit:
                        for k in range(2):
                            x_dma(xp[seq, pos0 + TT + k * 128:pos0 + TT + (k + 1) * 128, :], k)
                    if Q == n_tiles - 1 and seq + 1 == n_seq and do_sample and limit > 10 ** 8:
                        x_dma(xsm[:, :], 0)
                    stage(ffn, T, "gu1", "dn1", FFN_A if pre else 0)
                    stage(layer_norm, T, 0, EPS_DN)
                    dfr = []
                    stage(lambda *a: dfr.extend(w_in_stage(*a)), T, False, pos0, seq,
                          lambda blk, seq=seq, pos0=pos0: nk_p[seq, pos0 + blk * 128:pos0 + (blk + 1) * 128, :],
                          lambda blk, seq=seq, pos0=pos0: nv_p[seq, pos0 + blk * 128:pos0 + (blk + 1) * 128, :],
                          Q * 4)
                    stage(diff_attn_prompt, Q, dfr)
                    if Q + 1 < n_tiles and nst[0] < limit:
                        for k in range(2, 4):
                            x_dma(xp[seq, pos0 + TT + k * 128:pos0 + TT + (k + 1) * 128, :], k)
                    stage(resid_proj, "wout", T, *mix_rhs(T))
                    stage(layer_norm, T, 1, EPS_DN)
                    stage(q_cross, T)
                    stage(cross_attn, T, 0, RMK, RMV)
                    stage(resid_proj, "wo", T, *oc_rhs(T))
                    stage(layer_norm, T, 2, EPS_DN)
                    stage(ffn, T, "gu2", "dn2")
                    pre = (Q + 1 < n_tiles) and nst[0] < limit
                    last = (Q == n_tiles - 1) and nst[0] < limit and limit > 10 ** 8
                    mem_pre = last and seq + 1 < n_seq
                    smp_pre = last and seq + 1 == n_seq and do_sample
                    if mem_pre:
                        mem_phase_prompt(seq + 1, 1)
                    if smp_pre:
                        x_transpose(0, 0, dst_f32=False, dst_bf=True)
                        ffn_gu(128, "gu1", 0, 4)
                    if pre:
                        for blk in range(4):
                            x_transpose(blk, blk, dst_f32=False, dst_bf=True)
                        ffn_gu(T, "gu1", 0, 4)
                    stage(layer_norm, T, 3, EPS_DN, False)
                    if pre:
                        ffn_gu(T, "gu1", 4, 10)
                    if mem_pre:
                        mem_phase_prompt(seq + 1, 2)
                        mem_done.add(seq + 1)
                    if smp_pre:
                        ffn_gu(128, "gu1", 4, 10)
                    stage(store_y, 4, lambda blk, seq=seq, pos0=pos0: y_p[seq, pos0 + blk * 128:pos0 + (blk + 1) * 128, :])
                    if pre:
                        ffn_gu(T, "gu1", 10, FFN_A)
                        for blk in range(4):
                            x_transpose(blk, blk, dst_f32=True, dst_bf=False)
                    if smp_pre:
                        ffn_gu(128, "gu1", 10, FFN_A)
                        x_transpose(0, 0, dst_f32=True, dst_bf=False)
            if do_sample:
                T = 128
                if not smp_pre:
                    stage(load_T, lambda blk: xsm[:, :], 1)
                stage(ffn, T, "gu1", "dn1", FFN_A if smp_pre else 0)
                stage(layer_norm, T, 0, EPS_DN)
                stage(w_in_stage, T, True, 0, 0, lambda blk: nk_s[:, :], None, 0)
                stage(diff_attn_sample)
                stage(resid_proj, "wout", T, *mix_rhs(T))
                stage(layer_norm, T, 1, EPS_DN)
                stage(q_cross, T)
                for b_ in range(NSMP):
                    stage(mem_phase_sample, b_)
                    stage(cross_attn, DSEQ, b_ * 32, RMK, RMV)
                stage(resid_proj, "wo", T, *oc_rhs(T))
                stage(layer_norm, T, 2, EPS_DN)
                stage(ffn, T, "gu2", "dn2")
                stage(layer_norm, T, 3, EPS_DN, False)
                stage(store_y, 1, lambda blk: y_s[:, :])
            assert wstate["consumed"] == len(plan), (wstate, len(plan))
        except _Stop:
            pass
        S.finish()
        with nc.Block() as block:
            S.emit(block)
    return nc, S.n_instr


def _consts():
    slopes = 2.0 ** (-8.0 * np.arange(1, 5, dtype=np.float64) / 4)
    p = np.arange(128, dtype=np.float64)
    alibi = np.zeros((128, 36 * 4), np.float32)
    for d in range(-3, 33):
        for h in range(4):
            alibi[:, (d + 3) * 4 + h] = slopes[h] * (p - 128 * d)
    j = np.arange(128)[:, None]
    i = np.arange(128)[None, :]
    dmask = np.zeros((128, 4, 128), np.float32)
    allowed = (j // 64) <= (i // 64)
    for h in range(4):
        m = np.where(j <= i, 1.0, np.exp(-2.0 * slopes[h] * (j - i)))
        dmask[:, h, :] = np.where(allowed, m, 0.0)
    tril = (i >= j).astype(np.float32)
    return alibi, dmask, tril, np.eye(128, dtype=np.float32)


def _prep_inputs(inp):
    f = lambda a: np.ascontiguousarray(np.asarray(a, dtype=np.float32))
    alibi, dmask, tril, ident = _consts()
    lnp = np.zeros((128, 64), np.float32)
    for gi, (g, b) in enumerate((("ln1_g", "ln1_b"), ("ln2_g", "ln2_b"), ("ln3_g", "ln3_b"), ("ln4_g", "ln4_b"))):
        lnp[:, gi * 16:gi * 16 + 8] = f(inp[g])[0].reshape(8, 128).T
        lnp[:, gi * 16 + 8:gi * 16 + 16] = f(inp[b])[0].reshape(8, 128).T
    ggb = np.zeros((128, 2, 512), np.float32)
    ggb[:, 0, :] = f(inp["gmlp_ln_g"])[0][None, :]
    ggb[:, 1, :] = f(inp["gmlp_ln_b"])[0][None, :]
    ws = f(inp["gmlp_ws"])[0]
    ws_p = np.ascontiguousarray(ws.transpose(1, 0, 2))
    ws_s = np.zeros((128, 4, 128), np.float32)
    for r in range(4):
        ws_s[r * 32:(r + 1) * 32, :, r * 32:(r + 1) * 32] = ws[:, :32, :32].transpose(1, 0, 2)
    bs = f(inp["gmlp_bs"])[0]
    bs_p = bs.reshape(1, 512).copy()
    bs_s = np.ascontiguousarray(np.tile(bs[:, :32], (1, 4)).reshape(1, 512))
    lamv = np.zeros((128, 4, 64), np.float32)
    for k_, n in enumerate(("lambda_q1", "lambda_k1", "lambda_q2", "lambda_k2")):
        lamv[:, k_, :] = f(inp[n])[0][None, :]
    shared = {
        "w_gu1": f(inp["ffn1_w_gu"])[0], "w_dn1": f(inp["ffn1_w_down"])[0], "w_in": f(inp["w_in"])[0],
        "w_out": f(inp["w_out"])[0], "w_q": f(inp["cross_wq"])[0], "w_k": f(inp["cross_wk"])[0],
        "w_v": f(inp["cross_wv"])[0], "w_o": f(inp["cross_wo"])[0], "w_gu2": f(inp["ffn2_w_gu"])[0],
        "w_dn2": f(inp["ffn2_w_down"])[0],
        "lnp": lnp, "subg": f(inp["subln_g"])[0].reshape(128, 1).copy(), "ggb": ggb,
        "ws_p": ws_p, "ws_s": ws_s, "bs_p": bs_p, "bs_s": bs_s, "lamv": lamv,
        "ident": ident, "alibi": alibi, "dmask": dmask, "tril": tril,
    }
    x_prompt, x_sample = f(inp["x_prompt"]), f(inp["x_sample"])
    cache_k, cache_v = f(inp["cache_k"]), f(inp["cache_v"])
    cmk, cmv, mpr = f(inp["cache_mem_k"]), f(inp["cache_mem_v"]), f(inp["mem_prompt"])
    maps = []
    for c in range(N_CORES):
        m = dict(shared)
        m["xp"] = x_prompt[c * NSEQ:(c + 1) * NSEQ]
        m["xsm"] = x_sample[c * NSMP:(c + 1) * NSMP].reshape(128, D)
        m["ck"] = cache_k[0, c * NSMP:(c + 1) * NSMP].reshape(NSMP, PAST, 512)
        m["cv"] = cache_v[0, c * NSMP:(c + 1) * NSMP].reshape(NSMP, PAST, 512)
        m["cmk"] = cmk[0, c * NSMP:(c + 1) * NSMP].reshape(NSMP, NMEM, D)
        m["cmv"] = cmv[0, c * NSMP:(c + 1) * NSMP].reshape(NSMP, NMEM, D)
        m["mp"] = mpr[c * NSEQ:(c + 1) * NSEQ]
        maps.append(m)
    return maps


def _assemble(results):
    cat = lambda k: np.concatenate([np.asarray(r[k]) for r in results], axis=0)
    B, BS = N_CORES * NSEQ, N_CORES * NSMP
    y_p = cat("y_p").reshape(B, SEQ, D)
    y_s = cat("y_s").reshape(BS, DSEQ, D)
    nk_p = cat("nk_p").reshape(1, B, SEQ, 4, 2, 64)
    nv_p = cat("nv_p").reshape(1, B, SEQ, 4, 128)
    nmk_p = cat("nmk_p").reshape(1, B, NMEM, 4, 256)
    nmv_p = cat("nmv_p").reshape(1, B, NMEM, 4, 256)
    nk_s = cat("nk_s").reshape(1, BS, DSEQ, 4, 2, 64)
    nv_s = cat("nv_s").reshape(1, BS, DSEQ, 4, 128)
    ngv_s = cat("ngv_s").reshape(1, BS, DSEQ, 4, 128)
    return tuple(np.ascontiguousarray(a, dtype=np.float32) for a in (y_p, y_s, nk_p, nv_p, nmk_p, nmv_p, nk_s, nv_s, ngv_s))


def kernel(**inputs):
    maps = _prep_inputs(inputs)
    nc, _ = build_nc()
    res = run_bass_kernel_spmd(nc, maps, core_ids=list(range(N_CORES)))
    return _assemble(res.results)
```

```python
import contextlib
import math
import os

import numpy as np
import concourse.bass as bass
import concourse.mybir as mybir
from concourse.bass_utils import run_bass_kernel_spmd

F32 = mybir.dt.float32
BF16 = mybir.dt.bfloat16
AF = mybir.ActivationFunctionType
ALU = mybir.AluOpType
AX = mybir.AxisListType

N_CORES = 8
D = 1024
KC = 8
DFF = 2816
NFC = 22
SEQ = 2048
TT = 512
NSEQ = 2
NSMP = 4
DSEQ = 32
PAST = 4096
NMEM = 256
ALPHA = 2.0 ** 0.25
EPS = 1e-5
EPS_DN = EPS / (ALPHA * ALPHA)
LAM_INIT = 0.2
NSLOT = 5
SLOTW = 2816


class Res:
    __slots__ = ("name", "w", "r", "excl")

    def __init__(self, name, excl=False):
        self.name = name
        self.w = None
        self.r = {}
        self.excl = excl


class _Eng:
    def __init__(self, name, sem):
        self.name = name
        self.sem = sem
        self.count = 0
        self.q = []
        self.waited = {}


class _Lane:
    def __init__(self, idx, sem):
        self.idx = idx
        self.sem = sem
        self.count = 0


class Sched:
    ENGS = ("pe", "act", "dve", "pool", "sp")

    def __init__(self, nc, stack, n_lanes):
        self.nc = nc
        self.eng = {}
        for n in self.ENGS:
            self.eng[n] = _Eng(n, stack.enter_context(nc.semaphore("sem_" + n)))
        self.lanes = [_Lane(i, stack.enter_context(nc.semaphore("lane%d" % i))) for i in range(n_lanes)]
        self._lane_next = 0
        self.n_instr = 0

    def new_lane(self):
        l = self.lanes[self._lane_next]
        self._lane_next += 1
        return l

    def _collect(self, E, reads, writes, extra=()):
        need = {}

        def add(t):
            if t is None:
                return
            k = t[0]
            if k not in need or need[k][2] < t[2]:
                need[k] = t

        for r in reads:
            add(r.w)
            if r.excl:
                for t in r.r.values():
                    if t[0] != E.name:
                        add(t)
        for r in writes:
            add(r.w)
            for t in r.r.values():
                add(t)
        for t in extra:
            add(t)
        final = []
        for k, (kk, sem, val) in need.items():
            if E.waited.get(k, 0) >= val:
                continue
            if E.name == "pe" and k == "pe":
                continue
            E.waited[k] = val
            final.append((sem, val))
        return final

    @staticmethod
    def _mark(tok, reads, writes):
        k = tok[0]
        for r in reads:
            o = r.r.get(k)
            if o is None or o[2] < tok[2]:
                r.r[k] = tok
        for r in writes:
            r.w = tok
            r.r = {}

    def op(self, eng, fn, reads=(), writes=()):
        E = self.eng[eng]
        waits = self._collect(E, reads, writes)
        E.count += 1
        tok = (E.name, E.sem, E.count)
        E.q.append((waits, fn, E.sem, 1))
        self._mark(tok, reads, writes)
        self.n_instr += 1
        return tok

    def dma(self, queue, lane, fn, reads=(), writes=()):
        Q = self.eng[queue]
        key = "L%d" % lane.idx
        prev = (key, lane.sem, 16 * lane.count) if lane.count else None
        waits = self._collect(Q, reads, writes, extra=(prev,))
        lane.count += 1
        tok = (key, lane.sem, 16 * lane.count)
        Q.q.append((waits, fn, lane.sem, 16))
        self._mark(tok, reads, writes)
        self.n_instr += 1
        return tok

    def finish(self, queue="sp"):
        Q = self.eng[queue]
        for l in self.lanes:
            if l.count:
                k = "L%d" % l.idx
                if Q.waited.get(k, 0) < 16 * l.count:
                    Q.waited[k] = 16 * l.count
                    Q.q.append(([(l.sem, 16 * l.count)], None, None, 0))
        for n in ("pe", "act", "dve", "pool"):
            E = self.eng[n]
            if E.count and Q.waited.get(n, 0) < E.count:
                Q.waited[n] = E.count
                Q.q.append(([(E.sem, E.count)], None, None, 0))

    def emit(self, block):
        def replay(e, E):
            for waits, fn, sem, inc in E.q:
                for s, v in waits:
                    e.wait_ge(s, v)
                if fn is not None:
                    fn(e).then_inc(sem, inc)
            E.q = []

        @block.tensor
        def _(e):
            replay(e, self.eng["pe"])

        @block.scalar
        def _(e):
            replay(e, self.eng["act"])

        @block.vector
        def _(e):
            replay(e, self.eng["dve"])

        @block.gpsimd
        def _(e):
            replay(e, self.eng["pool"])

        @block.sync
        def _(e):
            replay(e, self.eng["sp"])


class RR:
    def __init__(self, items):
        self.items = list(items)
        self.i = 0

    def next(self):
        v = self.items[self.i % len(self.items)]
        self.i += 1
        return v


W_SPECS = [
    ("gu1", "gu", NFC), ("dn1", "dn", 8), ("win", "k256", 10), ("wout", "k256", 4),
    ("wq", "k256", 4), ("wo", "k256", 4), ("gu2", "gu", NFC), ("dn2", "dn", 8),
    ("wk", "k256", 4), ("wv", "k256", 4),
]
W_KIND = {n: k for n, k, _ in W_SPECS}
W_NCH = {n: c for n, _, c in W_SPECS}
TILE_ORDER = ["gu1", "dn1", "win", "wout", "wq", "wo", "gu2", "dn2"]


def chunk_width(kind):
    return SLOTW if kind == "dn" else 2048


def plan_chunks(n_seq, n_tiles, do_sample):
    plan = []
    for s in range(n_seq):
        for w in ("wk", "wv"):
            plan += [(w, j) for j in range(W_NCH[w])]
        for t in range(n_tiles):
            for w in TILE_ORDER:
                plan += [(w, j) for j in range(W_NCH[w])]
    if do_sample:
        for w in TILE_ORDER:
            plan += [(w, j) for j in range(W_NCH[w])]
    return plan


def build_nc(n_seq=NSEQ, n_tiles=SEQ // TT, do_sample=True, limit=10 ** 9, skip_prologue=False):
    nc = bass.Bass("TRN2", target_bir_lowering=False)

    def din(name, shape, dt=F32):
        return nc.dram_tensor(name, list(shape), dt, kind="ExternalInput").ap()

    def dout(name, shape, dt=F32):
        return nc.dram_tensor(name, list(shape), dt, kind="ExternalOutput").ap()

    xp = din("xp", [NSEQ, SEQ, D])
    xsm = din("xsm", [128, D])
    ck = din("ck", [NSMP, PAST, 512])
    cv = din("cv", [NSMP, PAST, 512])
    cmk = din("cmk", [NSMP, NMEM, D])
    cmv = din("cmv", [NSMP, NMEM, D])
    mp = din("mp", [NSEQ, NMEM, D])
    wsrc = {
        "gu1": din("w_gu1", [D, 2 * DFF]), "dn1": din("w_dn1", [DFF, D]), "win": din("w_in", [D, 2560]),
        "wout": din("w_out", [D, D]), "wq": din("w_q", [D, D]), "wk": din("w_k", [D, D]),
        "wv": din("w_v", [D, D]), "wo": din("w_o", [D, D]), "gu2": din("w_gu2", [D, 2 * DFF]),
        "dn2": din("w_dn2", [DFF, D]),
    }
    lnp_d = din("lnp", [128, 64])
    subg_d = din("subg", [128, 1])
    ggb_d = din("ggb", [128, 2, 512])
    wsp_d = din("ws_p", [128, 4, 128])
    wss_d = din("ws_s", [128, 4, 128])
    bsp_d = din("bs_p", [1, 512])
    bss_d = din("bs_s", [1, 512])
    lam_d = din("lamv", [128, 4, 64])
    ident_d = din("ident", [128, 128])
    alibi_d = din("alibi", [128, 36 * 4])
    dmask_d = din("dmask", [128, 4, 128])
    tril_d = din("tril", [128, 128])

    y_p = dout("y_p", [NSEQ, SEQ, D])
    y_s = dout("y_s", [128, D])
    nk_p = dout("nk_p", [NSEQ, SEQ, 512])
    nv_p = dout("nv_p", [NSEQ, SEQ, 512])
    nmk_p = dout("nmk_p", [NSEQ, NMEM, D])
    nmv_p = dout("nmv_p", [NSEQ, NMEM, D])
    nk_s = dout("nk_s", [128, 512])
    nv_s = dout("nv_s", [128, 512])
    ngv_s = dout("ngv_s", [128, 512])

    scr = {}
    for name, kind, nch in W_SPECS:
        scr[name] = nc.dram_tensor("scr_" + name, [nch, 128, chunk_width(kind)], BF16, kind="Internal").ap()
    RSCR = {name: Res("scr_" + name) for name, _, _ in W_SPECS}

    with contextlib.ExitStack() as st:
        S = Sched(nc, st, n_lanes=90)

        def sb(stack, name, shape, dt):
            return stack.enter_context(nc.sbuf_tensor("sb_" + name, list(shape), dt))

        ring = [sb(st, "ring%d" % i, [128, SLOTW], BF16) for i in range(NSLOT)]
        RS = [Res("ring%d" % i) for i in range(NSLOT)]
        LS = [S.new_lane() for _ in range(NSLOT)]

        with contextlib.ExitStack() as pst:
            NST = 5
            LOOK = NST - 1
            stg = [sb(pst, "stg%d" % i, [128, SLOTW], F32) for i in range(NST)]
            stb = [sb(pst, "stb%d" % i, [128, SLOTW], BF16) for i in range(NST)]
            RSTG = [Res("stg%d" % i) for i in range(NST)]
            RSTG2 = [Res("stg2_%d" % i) for i in range(NST)]
            RSTB = [Res("stb%d" % i) for i in range(NST)]
            LLD = [S.new_lane() for _ in range(NST)]
            LLD2 = [S.new_lane() for _ in range(NST)]
            LSTO = [S.new_lane() for _ in range(NST)]
            cast_eng = RR(["dve", "act"])
            chunks = [] if skip_prologue else [(name, kind, j) for name, kind, nch in W_SPECS for j in range(nch)]

            def p_load(ci):
                name, kind, j = chunks[ci]
                i = ci % NST
                src = wsrc[name]
                if kind == "gu":
                    v = stg[i][:, 0:2048].rearrange("p (k n) -> p k n", n=256)
                    sv = src.rearrange("(k p) n -> p k n", p=128)
                    S.dma("sp", LLD[i], lambda e: e.dma_start(out=v[:, :, 0:128], in_=sv[:, :, j * 128:(j + 1) * 128]), writes=[RSTG[i]])
                    S.dma("sp", LLD2[i], lambda e: e.dma_start(out=v[:, :, 128:256], in_=sv[:, :, DFF + j * 128:DFF + (j + 1) * 128]), writes=[RSTG2[i]])
                elif kind == "dn":
                    v = stg[i][:, 0:SLOTW].rearrange("p (f n) -> p f n", n=128)
                    sv = src.rearrange("(f p) n -> p f n", p=128)
                    S.dma("sp", LLD[i], lambda e: e.dma_start(out=v[:, :, :], in_=sv[:, :, j * 128:(j + 1) * 128]), writes=[RSTG[i]])
                else:
                    v = stg[i][:, 0:2048].rearrange("p (k n) -> p k n", n=256)
                    sv = src.rearrange("(k p) n -> p k n", p=128)
                    S.dma("sp", LLD[i], lambda e: e.dma_start(out=v[:, :, :], in_=sv[:, :, j * 256:(j + 1) * 256]), writes=[RSTG[i]])

            def p_cast_store(ci):
                name, kind, j = chunks[ci]
                i = ci % NST
                wdt = chunk_width(kind)
                ce = cast_eng.next()
                if ce == "act":
                    S.op("act", lambda e: e.activation(out=stb[i][:, 0:wdt], in_=stg[i][:, 0:wdt], func=AF.Copy),
                         reads=[RSTG[i], RSTG2[i]], writes=[RSTB[i]])
                else:
                    S.op(ce, lambda e: e.tensor_copy(out=stb[i][:, 0:wdt], in_=stg[i][:, 0:wdt]),
                         reads=[RSTG[i], RSTG2[i]], writes=[RSTB[i]])
                S.dma("sp", LSTO[i], lambda e: e.dma_start(out=scr[name][j, :, :], in_=stb[i][:, 0:wdt]),
                      reads=[RSTB[i]], writes=[RSCR[name]])

            for step in range(len(chunks) + LOOK):
                if step < len(chunks):
                    p_load(step)
                if step >= LOOK:
                    p_cast_store(step - LOOK)
            S.finish()
            with nc.Block() as block:
                S.emit(block)

        mst = st
        xf = sb(mst, "xf", [128, KC, TT], F32)
        xb = sb(mst, "xb", [128, KC, TT], BF16)
        hT = sb(mst, "hT", [128, NFC, TT], BF16)
        sq = sb(mst, "sq", [128, 4, TT], F32)
        kTb = sb(mst, "kTb", [128, 8256], BF16)
        vSb = sb(mst, "vSb", [128, 8448], BF16)
        memkT = sb(mst, "memkT", [128, 8, NMEM], BF16)
        memv = sb(mst, "memv", [128, 2, D], BF16)
        uT = sb(mst, "uT", [128, 4, TT], F32)
        Pt = [[sb(mst, "P%d_%d" % (m, j), [128, TT], BF16) for j in range(2)] for m in range(2)]
        s1 = sb(mst, "s1", [128, TT], F32)
        s2 = sb(mst, "s2", [128, TT], F32)
        m2 = sb(mst, "m2", [128, TT], F32)
        var = sb(mst, "var", [128, TT], F32)
        rstd = sb(mst, "rstd", [128, TT], F32)
        t1 = [sb(mst, "t1_%d" % j, [128, TT], F32) for j in range(4)]
        sg = [sb(mst, "sg%d" % j, [128, TT], F32) for j in range(2)]
        xs = [sb(mst, "xs%d" % j, [128, D], F32) for j in range(2)]
        kst = [sb(mst, "kst%d" % j, [128, 512], F32) for j in range(2)]
        vst = [sb(mst, "vst%d" % j, [128, 512], F32) for j in range(2)]
        gvf = [sb(mst, "gvf%d" % j, [128, 512], F32) for j in range(2)]
        vnb = [sb(mst, "vnb%d" % j, [128, 512], BF16) for j in range(2)]
        dtmp = [sb(mst, "dtmp%d" % j, [128, 128], F32) for j in range(2)]
        small = sb(mst, "small", [128, 32], F32)
        lamc = sb(mst, "lamc", [128, 16], F32)
        kTn = sb(mst, "kTn", [128, 4, 128], BF16)
        vnS = sb(mst, "vnS", [32, NSMP, 512], BF16)
        lnp = sb(mst, "lnp", [128, 64], F32)
        subg = sb(mst, "subg", [128, 1], F32)
        ggb = sb(mst, "ggb", [128, 2, 512], F32)
        WsT = [sb(mst, "WsT%d" % j, [128, 4, 128], BF16) for j in range(2)]
        bsr = [sb(mst, "bsr%d" % j, [33, 512], F32) for j in range(2)]
        bsb = [sb(mst, "bsb%d" % j, [33, 512], BF16) for j in range(2)]
        bstmp = sb(mst, "bstmp", [33, 512], BF16)
        ident = sb(mst, "ident", [128, 128], F32)
        alibi = sb(mst, "alibi", [128, 36 * 4], F32)
        dmask = sb(mst, "dmask", [128, 4, 128], F32)
        tril = sb(mst, "tril", [128, 128], F32)
        onesD = sb(mst, "onesD", [128, 128], F32)
        onesE = sb(mst, "onesE", [128, 128], F32)
        ones1 = sb(mst, "ones1", [1, 128], F32)
        onesB = sb(mst, "onesB", [128, 128], BF16)
        PS = [mst.enter_context(nc.psum_tensor("ps%d" % i, [128, 512], F32)) for i in range(8)]

        RX = [Res("xf%d" % c) for c in range(KC)]
        RXB = [Res("xb%d" % c) for c in range(KC)]
        RH = [Res("hT%d" % c) for c in range(NFC)]
        RSQ = [Res("sq%d" % c) for c in range(4)]
        RKT = [Res("kT%d" % h) for h in range(4)]
        RVS = Res("vS")
        RKTC = [Res("kTc%d" % i) for i in range(2)]
        RVC = [Res("vC%d" % i) for i in range(2)]
        RMK = Res("memkT")
        RMV = Res("memv")
        RU = [Res("uT%d" % g) for g in range(4)]
        RP = [[[Res("P%d_%d_%d" % (m, j, q)) for q in range(4)] for j in range(2)] for m in range(2)]
        PTc = [Pt[0][0], Pt[0][1]]
        RPTC = [RP[0][0], RP[0][1]]
        RS1, RS2, RM2, RVAR, RRSTD = Res("s1"), Res("s2"), Res("m2"), Res("var"), Res("rstd")
        RT1 = [Res("t1_%d" % j) for j in range(4)]
        RSG = [Res("sg%d" % j) for j in range(2)]
        RXS = [Res("xs%d" % j) for j in range(2)]
        RKST = [Res("kst%d" % j) for j in range(2)]
        RVST = [Res("vst%d" % j) for j in range(2)]
        RGVF = [Res("gvf%d" % j) for j in range(2)]
        RVNB = [Res("vnb%d" % j) for j in range(2)]
        RDT = [Res("dtmp%d" % j) for j in range(2)]
        RSM = Res("small")
        RLAM = Res("lamc")
        RKTN, RVNS = Res("kTn"), Res("vnS")
        RC = Res("consts")
        RWST = Res("WsT")
        RWSRAW = RGVF[0]
        wsraw = gvf[0][:, :].rearrange("p (g s) -> p g s", s=128)
        lamv = gvf[1][:, 0:256].rearrange("p (g s) -> p g s", s=64)
        RPS = [Res("ps%d" % i, excl=True) for i in range(8)]
        LXS = [S.new_lane() for _ in range(2)]
        LKST = [S.new_lane() for _ in range(2)]
        LVST = [S.new_lane() for _ in range(2)]
        LGV = [S.new_lane() for _ in range(2)]
        LSQ = [S.new_lane() for _ in range(4)]
        LC = S.new_lane()

        banks = RR(range(8))
        banksS = [RR([0, 1]), RR([2, 3])]

        QT0, AT0, GT0, QC0, OC0 = 0, 4, 8, 12, 0

        plan = plan_chunks(n_seq, n_tiles, do_sample)
        wstate = {"issued": 0, "consumed": 0}
        slot_of = {}

        def w_issue(slot):
            i = wstate["issued"]
            if i >= len(plan):
                return
            name, j = plan[i]
            wdt = chunk_width(W_KIND[name])
            wstate["issued"] += 1
            slot_of[i] = slot
            S.dma("sp", LS[slot], lambda e, slot=slot, name=name, j=j, wdt=wdt: e.dma_start(out=ring[slot][:, 0:wdt], in_=scr[name][j, :, :]),
                  reads=[RSCR[name]], writes=[RS[slot]])

        def w_acquire(name, j):
            i = wstate["consumed"]
            assert plan[i] == (name, j), (plan[i], name, j)
            wstate["consumed"] += 1
            return slot_of.pop(i)

        def w_release(slot):
            w_issue(slot)

        def mm(out, lhsT, rhs, start, stop, reads, writes):
            S.op("pe", lambda e: e.matmul(out, lhsT=lhsT, rhs=rhs, start=start, stop=stop), reads, writes)

        def tr(out, in_, reads, writes):
            S.op("pe", lambda e: e.transpose(out=out, in_=in_, identity=ident[:]), list(reads) + [RC], writes)

        def act(out, in_, func, reads, writes, bias=None, scale=None):
            kw = {}
            if bias is not None:
                kw["bias"] = bias
            if scale is not None:
                kw["scale"] = scale
            S.op("act", lambda e: e.activation(out=out, in_=in_, func=func, **kw), reads, writes)

        def cp(eng, out, in_, reads, writes):
            if eng == "act":
                act(out, in_, AF.Copy, reads, writes)
            else:
                S.op(eng, lambda e: e.tensor_copy(out=out, in_=in_), reads, writes)

        def tt_(eng, out, in0, in1, op, reads, writes):
            S.op(eng, lambda e: e.tensor_tensor(out=out, in0=in0, in1=in1, op=op), reads, writes)

        def ts_(eng, out, in0, s1_, s2_, op0, op1, reads, writes):
            if op1 is None:
                S.op(eng, lambda e: e.tensor_scalar(out=out, in0=in0, scalar1=s1_, scalar2=None, op0=op0), reads, writes)
            else:
                S.op(eng, lambda e: e.tensor_scalar(out=out, in0=in0, scalar1=s1_, scalar2=s2_, op0=op0, op1=op1), reads, writes)

        def stt(out, in0, scalar, in1, op0, op1, reads, writes):
            S.op("dve", lambda e: e.scalar_tensor_tensor(out=out, in0=in0, scalar=scalar, in1=in1, op0=op0, op1=op1), reads, writes)

        def dma(lane, out, in_, reads, writes):
            S.dma("sp", lane, lambda e: e.dma_start(out=out, in_=in_), reads, writes)

        for dst, src in ((lnp, lnp_d), (subg, subg_d), (ggb, ggb_d), (ident, ident_d),
                         (alibi, alibi_d), (dmask, dmask_d), (tril, tril_d)):
            dma(LC, dst[:], src, [], [RC])
        for wi, bsd in enumerate((bsp_d, bss_d)):
            dma(LC, bsr[wi][0:1, :], bsd, [], [RC])
            dma(LC, bsr[wi][32:33, :], bsd, [], [RC])
            S.op("pool", lambda e, wi=wi: e.memset(bsb[wi][:, :], 0.0), [], [RC])
            cp("dve", bsb[wi][0:1, :], bsr[wi][0:1, :], [RC], [RC])
            cp("dve", bstmp[32:33, :], bsr[wi][32:33, :], [RC], [RC])
            tt_("dve", bsb[wi][32:33, :], bsr[wi][32:33, :], bstmp[32:33, :], ALU.subtract, [RC], [RC])
        dma(LGV[1], lamv, lam_d, [], [RGVF[1]])
        S.op("pool", lambda e: e.memset(onesD[:], 1.0 / D), [], [RC])
        S.op("pool", lambda e: e.memset(onesE[:], 1.0 / 128), [], [RC])
        S.op("pool", lambda e: e.memset(ones1[:], 1.0), [], [RC])
        S.op("pool", lambda e: e.memset(onesB[:], 1.0), [], [RC])
        tt_("dve", sq[:, 0, 0:64], lamv[:, 0, :], lamv[:, 1, :], ALU.mult, [RGVF[1]], [RSQ[0]])
        tt_("dve", sq[:, 0, 64:128], lamv[:, 2, :], lamv[:, 3, :], ALU.mult, [RGVF[1], RSQ[0]], [RSQ[0]])
        S.op("dve", lambda e: e.tensor_reduce(out=lamc[:, 2:3], in_=sq[:, 0, 0:64], axis=AX.X, op=ALU.add), [RSQ[0]], [RLAM])
        S.op("dve", lambda e: e.tensor_reduce(out=lamc[:, 3:4], in_=sq[:, 0, 64:128], axis=AX.X, op=ALU.add), [RSQ[0], RLAM], [RLAM])
        act(lamc[:, 4:6], lamc[:, 2:4], AF.Exp, [RLAM], [RLAM])
        tt_("dve", lamc[:, 6:7], lamc[:, 5:6], lamc[:, 4:5], ALU.subtract, [RLAM], [RLAM])
        ts_("dve", lamc[:, 0:1], lamc[:, 6:7], -LAM_INIT, None, ALU.add, None, [RLAM], [RLAM])
        ts_("dve", lamc[:, 1:2], subg[:, 0:1], 1.0 - LAM_INIT, None, ALU.mult, None, [RLAM, RC], [RLAM])
        for k_, eps in enumerate((EPS, EPS_DN)):
            S.op("pool", lambda e, k_=k_, eps=eps: e.memset(lamc[:, 8 + k_:9 + k_], eps), [RLAM], [RLAM])
        LNC = (-16.0, -16.0, -4.0, -1.0)
        for h_ in range(4):
            S.op("pool", lambda e, h_=h_: e.memset(lamc[:, 10 + h_:11 + h_], LNC[h_]), [RLAM], [RLAM])
        neg_lam = lamc[:, 0:1]
        g08 = lamc[:, 1:2]
        eps_cols = {EPS: lamc[:, 8:9], EPS_DN: lamc[:, 9:10]}
        for wi, wsd in enumerate((wsp_d, wss_d)):
            dma(LGV[0], wsraw, wsd, [], [RWSRAW])
            for g in range(4):
                b = banks.next()
                tr(PS[b][:, 0:128], wsraw[:, g, :], [RWSRAW], [RPS[b]])
                tt_("dve", WsT[wi][:, g, :], PS[b][:, 0:128], tril[:], ALU.mult, [RPS[b], RC], [RWST])

        for sl in range(NSLOT):
            w_issue(sl)

        xs_rr = RR([0, 1])
        kst_rr = RR([0, 1])
        vst_rr = RR([0, 1])
        gv_rr = RR([0, 1])
        sg_rr = RR([0, 1])
        t1_rr = RR([0, 1, 2, 3])
        ev_rr = RR(["dve", "act"])

        xstage = [
            (xs[0][:, :], [RXS[0]], LXS[0]),
            (xs[1][:, :], [RXS[1]], LXS[1]),
            (sq[:, 0:2, :].rearrange("p c t -> p (c t)"), [RSQ[0], RSQ[1]], LSQ[0]),
            (sq[:, 2:4, :].rearrange("p c t -> p (c t)"), [RSQ[2], RSQ[3]], LSQ[1]),
        ]

        def x_dma(src_ap, k):
            buf, rr_, lane = xstage[k]
            dma(lane, buf, src_ap, [], rr_)

        def x_transpose(blk, k, dst_f32=True, dst_bf=True):
            buf, rr_, lane = xstage[k]
            for half in range(2):
                b = banks.next()
                for cc in range(4):
                    c = half * 4 + cc
                    tr(PS[b][:, cc * 128:(cc + 1) * 128], buf[:, c * 128:(c + 1) * 128], rr_, [RPS[b]])
                pv = PS[b][:, 0:512].rearrange("p (c t) -> p c t", t=128)
                cs = slice(half * 4, half * 4 + 4)
                ts = slice(blk * 128, (blk + 1) * 128)
                if dst_f32:
                    cp("dve", xf[:, cs, ts], pv, [RPS[b]], RX[cs])
                if dst_bf:
                    cp("act", xb[:, cs, ts], pv, [RPS[b]], RXB[cs])

        def load_T(src_fn, NB, dst_f32=True, prefetched=False):
            for blk in range(NB):
                k = blk if NB == 4 else xs_rr.next()
                if not prefetched:
                    x_dma(src_fn(blk), k)
                x_transpose(blk, k, dst_f32)

        def ln_acc(T, oc):
            if oc == 0:
                act(s1[:, 0:T], xf[:, 0, 0:T], AF.Copy, [RX[0]], [RS1])
                act(s2[:, 0:T], xf[:, 0, 0:T], AF.Square, [RX[0]], [RS2])
            else:
                j = t1_rr.next()
                act(t1[j][:, 0:T], xf[:, oc, 0:T], AF.Square, [RX[oc]], [RT1[j]])
                tt_("pool", s1[:, 0:T], s1[:, 0:T], xf[:, oc, 0:T], ALU.add, [RS1, RX[oc]], [RS1])
                tt_("dve", s2[:, 0:T], s2[:, 0:T], t1[j][:, 0:T], ALU.add, [RS2, RT1[j]], [RS2])

        def layer_norm(T, gi, eps, want_bf=True):
            bm, be = banks.next(), banks.next()
            mm(PS[bm][:, 0:T], onesD[:], s1[:, 0:T], True, True, [RC, RS1], [RPS[bm]])
            mm(PS[be][:, 0:T], onesD[:], s2[:, 0:T], True, True, [RC, RS2], [RPS[be]])
            act(m2[:, 0:T], PS[bm][:, 0:T], AF.Square, [RPS[bm]], [RM2])
            tt_("dve", var[:, 0:T], PS[be][:, 0:T], m2[:, 0:T], ALU.subtract, [RPS[be], RM2], [RVAR])
            act(var[:, 0:T], var[:, 0:T], AF.Ln, [RVAR, RLAM], [RVAR], bias=eps_ap(eps), scale=1.0)
            act(rstd[:, 0:T], var[:, 0:T], AF.Exp, [RVAR], [RRSTD], scale=-0.5)
            for c in range(KC):
                j = t1_rr.next()
                tt_("dve", t1[j][:, 0:T], xf[:, c, 0:T], PS[bm][:, 0:T], ALU.subtract, [RX[c], RPS[bm]], [RT1[j]])
                tt_("dve" if c in (0, 3, 6) else "pool", t1[j][:, 0:T], t1[j][:, 0:T], rstd[:, 0:T], ALU.mult, [RT1[j], RRSTD], [RT1[j]])
                gcol = lnp[:, gi * 16 + c:gi * 16 + c + 1]
                bcol = lnp[:, gi * 16 + 8 + c:gi * 16 + 8 + c + 1]
                if want_bf:
                    act(xb[:, c, 0:T], t1[j][:, 0:T], AF.Identity, [RT1[j], RC], [RXB[c]], bias=bcol, scale=gcol)
                act(xf[:, c, 0:T], t1[j][:, 0:T], AF.Identity, [RT1[j], RC], [RX[c]], bias=bcol, scale=gcol)

        def eps_ap(eps):
            return eps_cols[eps]

        FFN_A = 14

        def ffn_gu(T, gu, fc0, fc1):
            NG = 3
            if fc0 == 0:
                sl = [w_acquire(gu, fc) for fc in range(NG)]
                bgs = [(banks.next(), banks.next()) for _ in range(NG)]
                for kc in range(KC):
                    for g in range(NG):
                        mm(PS[bgs[g][0]][:, 0:T], ring[sl[g]][:, kc * 256:kc * 256 + 128], xb[:, kc, 0:T], kc == 0, kc == KC - 1, [RS[sl[g]], RXB[kc]], [RPS[bgs[g][0]]])
                        mm(PS[bgs[g][1]][:, 0:T], ring[sl[g]][:, kc * 256 + 128:kc * 256 + 256], xb[:, kc, 0:T], kc == 0, kc == KC - 1, [RS[sl[g]], RXB[kc]], [RPS[bgs[g][1]]])
                for g in range(NG):
                    w_release(sl[g])
                for g in range(NG):
                    j = sg_rr.next()
                    act(sg[j][:, 0:T], PS[bgs[g][0]][:, 0:T], AF.Silu, [RPS[bgs[g][0]]], [RSG[j]])
                    tt_("dve", hT[:, g, 0:T], PS[bgs[g][1]][:, 0:T], sg[j][:, 0:T], ALU.mult, [RPS[bgs[g][1]], RSG[j]], [RH[g]])
                fc0 = NG
            for fc in range(fc0, fc1):
                s = w_acquire(gu, fc)
                bg, bu = banks.next(), banks.next()
                for kc in range(KC):
                    mm(PS[bg][:, 0:T], ring[s][:, kc * 256:kc * 256 + 128], xb[:, kc, 0:T], kc == 0, kc == KC - 1, [RS[s], RXB[kc]], [RPS[bg]])
                for kc in range(KC):
                    mm(PS[bu][:, 0:T], ring[s][:, kc * 256 + 128:kc * 256 + 256], xb[:, kc, 0:T], kc == 0, kc == KC - 1, [RS[s], RXB[kc]], [RPS[bu]])
                w_release(s)
                j = sg_rr.next()
                act(sg[j][:, 0:T], PS[bg][:, 0:T], AF.Silu, [RPS[bg]], [RSG[j]])
                tt_("dve", hT[:, fc, 0:T], PS[bu][:, 0:T], sg[j][:, 0:T], ALU.mult, [RPS[bu], RSG[j]], [RH[fc]])

        def ffn(T, gu, dn, fc_start=0):
            ffn_gu(T, gu, fc_start, NFC)
            for oc in range(KC):
                s = w_acquire(dn, oc)
                b = banks.next()
                for fc in range(NFC):
                    mm(PS[b][:, 0:T], ring[s][:, fc * 128:(fc + 1) * 128], hT[:, fc, 0:T], fc == 0, fc == NFC - 1, [RS[s], RH[fc]], [RPS[b]])
                w_release(s)
                stt(xf[:, oc, 0:T], PS[b][:, 0:T], 0.5 / ALPHA, xf[:, oc, 0:T], ALU.mult, ALU.add, [RPS[b], RX[oc]], [RX[oc]])
                ln_acc(T, oc)

        def proj_fm(wname, j, T, rhs_fn, rhs_res_fn, evac):
            proj_fm_multi(wname, [j], T, rhs_fn, rhs_res_fn, lambda jj, ol, b: evac(ol, b))

        def proj_fm_multi(wname, js, T, rhs_fn, rhs_res_fn, evac):
            sl = [w_acquire(wname, j) for j in js]
            bs_ = [[banks.next(), banks.next()] for _ in js]
            for kc in range(KC):
                for gi_, s in enumerate(sl):
                    for ol in range(2):
                        b = bs_[gi_][ol]
                        mm(PS[b][:, 0:T], ring[s][:, kc * 256 + ol * 128:kc * 256 + (ol + 1) * 128], rhs_fn(kc), kc == 0, kc == KC - 1,
                           [RS[s], rhs_res_fn(kc)], [RPS[b]])
            for s in sl:
                w_release(s)
            for gi_, j in enumerate(js):
                for ol in range(2):
                    evac(j, ol, bs_[gi_][ol])

        def resid_proj(wname, T, rhs_fn, rhs_res_fn):
            for j in range(4):
                def ev(ol, b, j=j):
                    oc = 2 * j + ol
                    stt(xf[:, oc, 0:T], PS[b][:, 0:T], 1.0 / ALPHA, xf[:, oc, 0:T], ALU.mult, ALU.add, [RPS[b], RX[oc]], [RX[oc]])
                    ln_acc(T, oc)
                proj_fm(wname, j, T, rhs_fn, rhs_res_fn, ev)

        def xb_rhs(T):
            return (lambda kc: xb[:, kc, 0:T]), (lambda kc: RXB[kc])

        def w_in_stage(T, sample, pos0, seq, k_dst, v_dst, gblk0):
            NB = T // 128
            rf, rr_ = xb_rhs(T)
            def evq(j, ol, b):
                h = 2 * j + ol
                cp("act", hT[:, QT0 + h, 0:T], PS[b][:, 0:T], [RPS[b]], [RH[QT0 + h]])
            proj_fm_multi("win", [0, 1], T, rf, rr_, evq)
            tmb = [banks.next() for _ in range(NB)]
            for j in range(2):
                s = w_acquire("win", 2 + j)
                fb = []
                for ol in range(2):
                    b = banks.next()
                    fb.append(b)
                    for kc in range(KC):
                        mm(PS[b][:, 0:T], ring[s][:, kc * 256 + ol * 128:kc * 256 + (ol + 1) * 128], xb[:, kc, 0:T], kc == 0, kc == KC - 1,
                           [RS[s], RXB[kc]], [RPS[b]])
                for blk in range(NB):
                    for kc in range(KC):
                        mm(PS[tmb[blk]][:, j * 256:(j + 1) * 256], xb[:, kc, blk * 128:(blk + 1) * 128], ring[s][:, kc * 256:(kc + 1) * 256],
                           kc == 0, kc == KC - 1, [RS[s], RXB[kc]], [RPS[tmb[blk]]])
                w_release(s)
                for ol in range(2):
                    h = 2 * j + ol
                    if sample:
                        cp("act", kTn[:, h, 0:T], PS[fb[ol]][:, 0:T], [RPS[fb[ol]]], [RKTN])
                    else:
                        cp("act", kTb[:, h * SEQ + pos0:h * SEQ + pos0 + T], PS[fb[ol]][:, 0:T], [RPS[fb[ol]]], [RKT[h]])
            for blk in range(NB):
                i = kst_rr.next()
                cp("dve", kst[i][:, :], PS[tmb[blk]][:, 0:512], [RPS[tmb[blk]]], [RKST[i]])
                dma(LKST[i], k_dst(blk), kst[i][:, :], [RKST[i]], [])
            if not sample:
                tmb = [banks.next() for _ in range(NB)]
                for j in range(2):
                    s = w_acquire("win", 4 + j)
                    for blk in range(NB):
                        for kc in range(KC):
                            mm(PS[tmb[blk]][:, j * 256:(j + 1) * 256], xb[:, kc, blk * 128:(blk + 1) * 128], ring[s][:, kc * 256:(kc + 1) * 256],
                               kc == 0, kc == KC - 1, [RS[s], RXB[kc]], [RPS[tmb[blk]]])
                    w_release(s)
                for blk in range(NB):
                    i = vst_rr.next()
                    cp("dve", vst[i][:, :], PS[tmb[blk]][:, 0:512], [RPS[tmb[blk]]], [RVST[i]])
                    dma(LVST[i], v_dst(blk), vst[i][:, :], [RVST[i]], [])
                    gb = gblk0 + blk
                    cp("act", vSb[:, gb * 512:(gb + 1) * 512], PS[tmb[blk]][:, 0:512], [RPS[tmb[blk]]], [RVS])
            else:
                tmb = [banks.next() for _ in range(NSMP)]
                for j in range(2):
                    s = w_acquire("win", 4 + j)
                    for b_ in range(NSMP):
                        for kc in range(KC):
                            mm(PS[tmb[b_]][0:32, j * 256:(j + 1) * 256], xb[:, kc, b_ * 32:(b_ + 1) * 32], ring[s][:, kc * 256:(kc + 1) * 256],
                               kc == 0, kc == KC - 1, [RS[s], RXB[kc]], [RPS[tmb[b_]]])
                    w_release(s)
                for b_ in range(NSMP):
                    i = vst_rr.next()
                    cp("dve", vst[i][0:32, :], PS[tmb[b_]][0:32, 0:512], [RPS[tmb[b_]]], [RVST[i]])
                    dma(LVST[i], nv_s[b_ * 32:(b_ + 1) * 32, :], vst[i][0:32, :], [RVST[i]], [])
                    cp("act", vnS[0:32, b_, :], PS[tmb[b_]][0:32, 0:512], [RPS[tmb[b_]]], [RVNS])
            for j in range(2):
                def ev(ol, b, j=j):
                    g = 2 * j + ol
                    act(uT[:, g, 0:T], PS[b][:, 0:T], AF.Gelu_apprx_tanh, [RPS[b]], [RU[g]])
                proj_fm("win", 6 + j, T, rf, rr_, ev)
            tmb = [banks.next() for _ in range(NB)]
            for j in range(2):
                s = w_acquire("win", 8 + j)
                for blk in range(NB):
                    for kc in range(KC):
                        mm(PS[tmb[blk]][:, j * 256:(j + 1) * 256], xb[:, kc, blk * 128:(blk + 1) * 128], ring[s][:, kc * 256:(kc + 1) * 256],
                           kc == 0, kc == KC - 1, [RS[s], RXB[kc]], [RPS[tmb[blk]]])
                w_release(s)
            wi = 1 if sample else 0

            def gv_chain(blk, buf, rbuf, i):
                S.op("dve", lambda e: e.bn_stats(out=small[:, 16:22], in_=buf), [rbuf, RSM], [RSM])
                S.op("dve", lambda e: e.bn_aggr(out=small[:, 22:24], in_=small[:, 16:22]), [RSM], [RSM])
                act(small[:, 24:25], small[:, 23:24], AF.Ln, [RSM, RLAM], [RSM], bias=eps_ap(EPS), scale=1.0)
                act(small[:, 25:26], small[:, 24:25], AF.Exp, [RSM], [RSM], scale=-0.5)
                stt(small[:, 26:27], small[:, 22:23], -1.0, small[:, 25:26], ALU.mult, ALU.mult, [RSM], [RSM])
                act(buf, buf, AF.Identity, [rbuf, RSM], [rbuf], bias=small[:, 26:27], scale=small[:, 25:26])
                tt_("pool", buf, buf, ggb[:, 0, :], ALU.mult, [rbuf, RC], [rbuf])
                tt_("pool", buf, buf, ggb[:, 1, :], ALU.add, [rbuf, RC], [rbuf])
                cp("act" if sample else "pool", vnb[i][:, :], buf, [rbuf], [RVNB[i]])
                if sample:
                    dma(LGV[i], ngv_s[:, :], buf, [rbuf], [])

            def gv_spatial(blk, i, b2):
                for g in range(4):
                    mm(PS[b2][:, g * 128:(g + 1) * 128], vnb[i][:, g * 128:(g + 1) * 128], WsT[wi][:, g, :], True, False, [RVNB[i], RWST], [RPS[b2]])
                    mm(PS[b2][:, g * 128:(g + 1) * 128], onesB[0:33, :], bsb[wi][0:33, g * 128:(g + 1) * 128], False, True, [RC], [RPS[b2]])
                ts = slice(blk * 128, (blk + 1) * 128)
                tt_("dve", hT[:, GT0:GT0 + 4, ts], PS[b2][:, 0:512].rearrange("p (g t) -> p g t", t=128), uT[:, :, ts], ALU.mult,
                    [RPS[b2]] + RU, RH[GT0:GT0 + 4])

            deferred = []
            for blk in range(NB):
                b = tmb[blk]
                if sample:
                    i = gv_rr.next()
                    act(gvf[i][:, :], PS[b][:, 0:512], AF.Gelu_apprx_tanh, [RPS[b]], [RGVF[i]])
                    gv_chain(blk, gvf[i][:, :], RGVF[i], i)
                    gv_spatial(blk, i, banks.next())
                else:
                    act(sq[:, blk, :], PS[b][:, 0:512], AF.Gelu_apprx_tanh, [RPS[b]], [RSQ[blk]])
                    i = blk % 2
                    deferred.append((lambda blk=blk, i=i: gv_chain(blk, sq[:, blk, :], RSQ[blk], i),
                                     lambda b2, blk=blk, i=i: gv_spatial(blk, i, b2)))
            return deferred

        def attn_finish(h, T, c0, o1, o2, z1, z2, ro1, ro2, rz1, rz2, sbank, centre=False, mid=None):
            for zap, rz, tj in ((z1, rz1, 0), (z2, rz2, 1)):
                if centre:
                    act(t1[tj][:, 0:T], zap, AF.Ln, [rz], [RT1[tj]], scale=math.exp(LNC[h]))
                    act(t1[tj][:, 0:T], t1[tj][:, 0:T], AF.Exp, [RT1[tj], RLAM], [RT1[tj]], scale=-1.0, bias=lamc[:, 10 + h:11 + h])
                else:
                    act(t1[tj][:, 0:T], zap, AF.Ln, [rz], [RT1[tj]])
                    act(t1[tj][:, 0:T], t1[tj][:, 0:T], AF.Exp, [RT1[tj]], [RT1[tj]], scale=-1.0)
            tt_("dve", s1[:, 0:T], o1, t1[0][:, 0:T], ALU.mult, [ro1, RT1[0]], [RS1])
            tt_("dve", s2[:, 0:T], o2, t1[1][:, 0:T], ALU.mult, [ro2, RT1[1]], [RS2])
            stt(m2[:, 0:T], s2[:, 0:T], neg_lam, s1[:, 0:T], ALU.mult, ALU.add, [RS1, RS2, RLAM], [RM2])
            act(var[:, 0:T], m2[:, 0:T], AF.Square, [RM2], [RVAR])
            b = sbank
            mm(PS[b][:, 0:T], onesE[:], var[:, 0:T], True, True, [RC, RVAR], [RPS[b]])
            if mid is not None:
                mid()
            act(rstd[:, 0:T], PS[b][:, 0:T], AF.Ln, [RPS[b], RLAM], [RRSTD], bias=eps_ap(EPS), scale=1.0)
            act(rstd[:, 0:T], rstd[:, 0:T], AF.Exp, [RRSTD], [RRSTD], scale=-0.5)
            stt(hT[:, AT0 + h, c0:c0 + T], m2[:, 0:T], g08, rstd[:, 0:T], ALU.mult, ALU.mult, [RM2, RRSTD, RLAM], [RH[AT0 + h]])

        def diff_attn_prompt(Q, deferred):
            T = TT
            O1, O2, Z1, Z2 = 4, 5, 6, 7
            OZ = (O1, O2, Z1, Z2)
            nJ = 4 * Q + 4
            steps = [(h, J) for h in range(4) for J in range(nJ)]

            def bcol(d, h):
                return alibi[:, (d + 3) * 4 + h:(d + 3) * 4 + h + 1]

            def emit_S(h, J):
                c0 = max(J - 4 * Q, 0) * 128
                sbk = [banksS[0].next(), banksS[1].next()]
                for m in range(2):
                    mm(PS[sbk[m]][:, c0:T], kTb[m * 64:(m + 1) * 64, h * SEQ + J * 128:h * SEQ + (J + 1) * 128],
                       hT[m * 64:(m + 1) * 64, QT0 + h, c0:T], True, True, [RKT[h], RH[QT0 + h]], [RPS[sbk[m]]])
                return sbk, c0

            def exp_diag(h, m, jb, sb_, qb, d):
                qs = slice(qb * 128, (qb + 1) * 128)
                act(dtmp[m][:, :], PS[sb_][:, qs], AF.Exp, [RPS[sb_], RC], [RDT[m]], bias=bcol(d, h), scale=0.125)
                tt_("dve", Pt[m][jb][:, qs], dtmp[m][:, :], dmask[:, h, :], ALU.mult, [RDT[m], RC], [RP[m][jb][qb]])

            def emit_E(h, J, sbk, c0):
                jb = J % 2
                G = 1 if h == 0 else 4
                qb_lo = c0 // 128
                for m in range(2):
                    sb_ = sbk[m]
                    for g0 in range(0, 4, G):
                        blocks = [qb for qb in range(g0, g0 + G) if qb >= qb_lo]
                        if not blocks:
                            continue
                        d = 4 * Q + g0 - J
                        rest = []
                        for qb in blocks:
                            if 4 * Q + qb == J:
                                exp_diag(h, m, jb, sb_, qb, d)
                            else:
                                rest.append(qb)
                        if rest:
                            lo_, hi_ = rest[0] * 128, (rest[-1] + 1) * 128
                            act(Pt[m][jb][:, lo_:hi_], PS[sb_][:, lo_:hi_], AF.Exp, [RPS[sb_], RC], RP[m][jb][rest[0]:rest[-1] + 1],
                                bias=bcol(d, h), scale=0.125)

            def emit_PV(h, J, c0):
                jb = J % 2
                first, last = (J == 0), (J == nJ - 1)
                for m in range(2):
                    prs = RP[m][jb][c0 // 128:4]
                    mm(PS[OZ[m]][:, c0:T], vSb[:, J * 512 + h * 128:J * 512 + (h + 1) * 128], Pt[m][jb][:, c0:T], first, last,
                       [RVS] + prs, [RPS[OZ[m]]])
                    mm(PS[OZ[2 + m]][:, c0:T], onesB[:], Pt[m][jb][:, c0:T], first, last, [RC] + prs, [RPS[OZ[2 + m]]])

            deferred[0][0]()
            cur = emit_S(*steps[0])
            e_done = set()
            for i, (h, J) in enumerate(steps):
                nxt = emit_S(*steps[i + 1]) if i + 1 < len(steps) else None
                if i not in e_done:
                    emit_E(h, J, *cur)
                emit_PV(h, J, cur[1])
                if J == nJ - 1:
                    mid = None
                    if nxt is not None:
                        def mid(i=i, nxt=nxt):
                            emit_E(steps[i + 1][0], steps[i + 1][1], *nxt)
                            e_done.add(i + 1)
                    attn_finish(h, T, 0, PS[O1][:, 0:T], PS[O2][:, 0:T], PS[Z1][:, 0:T], PS[Z2][:, 0:T], RPS[O1], RPS[O2], RPS[Z1], RPS[Z2], Z1, centre=True, mid=mid)
                    deferred[h][1](O2)
                elif h >= 1 and J == nJ // 2:
                    deferred[h][0]()
                cur = nxt

        def cross_attn(T, c0, mk_res, mv_res):
            for hh in range(4):
                pb = []
                for mb in range(2):
                    b = banks.next()
                    for j in range(2):
                        mm(PS[b][:, 0:T], memkT[:, hh * 2 + j, mb * 128:(mb + 1) * 128], hT[:, QC0 + hh * 2 + j, c0:c0 + T], j == 0, j == 1,
                           [mk_res, RH[QC0 + hh * 2 + j]], [RPS[b]])
                    act(PTc[mb][:, 0:T], PS[b][:, 0:T], AF.Exp, [RPS[b]], RPTC[mb], scale=1.0 / 16.0)
                ob = [banks.next(), banks.next()]
                zb = banks.next()
                for j in range(2):
                    for mb in range(2):
                        mm(PS[ob[j]][:, 0:T], memv[:, mb, hh * 256 + j * 128:hh * 256 + (j + 1) * 128], PTc[mb][:, 0:T], mb == 0, mb == 1,
                           [mv_res] + RPTC[mb], [RPS[ob[j]]])
                for mb in range(2):
                    mm(PS[zb][:, 0:T], onesB[:], PTc[mb][:, 0:T], mb == 0, mb == 1, [RC] + RPTC[mb], [RPS[zb]])
                j_ = t1_rr.next()
                act(t1[j_][:, 0:T], PS[zb][:, 0:T], AF.Ln, [RPS[zb]], [RT1[j_]])
                act(t1[j_][:, 0:T], t1[j_][:, 0:T], AF.Exp, [RT1[j_]], [RT1[j_]], scale=-1.0)
                for j in range(2):
                    tt_("dve", hT[:, OC0 + hh * 2 + j, c0:c0 + T], PS[ob[j]][:, 0:T], t1[j_][:, 0:T], ALU.mult, [RPS[ob[j]], RT1[j_]],
                        [RH[OC0 + hh * 2 + j]])

        def store_y(NB, dst_fn):
            for blk in range(NB):
                i = kst_rr.next()
                vst_rr.next()
                for half in range(2):
                    b = banks.next()
                    for cc in range(4):
                        c = half * 4 + cc
                        tr(PS[b][:, cc * 128:(cc + 1) * 128], xf[:, c, blk * 128:(blk + 1) * 128], [RX[c]], [RPS[b]])
                    stg_, rs_, ln_ = (kst[i], RKST[i], LKST[i]) if half == 0 else (vst[i], RVST[i], LVST[i])
                    cp(ev_rr.next(), stg_[:, :], PS[b][:, 0:512], [RPS[b]], [rs_])
                    dma(ln_, dst_fn(blk)[:, half * 512:(half + 1) * 512], stg_[:, :], [rs_], [])

        def mem_phase_prompt(seq, part=None):
            SUB = 99
            if part in (None, 1):
                load_T(lambda blk: mp[seq, blk * 128:(blk + 1) * 128, :], 2, dst_f32=False)
            todo = (("wk", True), ("wv", False)) if part is None else ((("wk", True),) if part == 1 else (("wv", False),))
            for wname, is_k in todo:
                dst = nmk_p if is_k else nmv_p
                for pair in range(2):
                    mb_ = [banks.next(), banks.next()]
                    for jj in range(2):
                        s = w_acquire(wname, pair * 2 + jj)
                        for blk in range(2):
                            for kc in range(KC):
                                mm(PS[mb_[blk]][:, jj * 256:(jj + 1) * 256], xb[:, kc, blk * 128:(blk + 1) * 128], ring[s][:, kc * 256:(kc + 1) * 256],
                                   kc == 0, kc == KC - 1, [RS[s], RXB[kc]], [RPS[mb_[blk]]])
                        if SUB >= 2:
                            w_release(s)
                    if SUB <= 2:
                        continue
                    for blk in range(2):
                        i = kst_rr.next()
                        cp("dve", kst[i][:, :], PS[mb_[blk]][:, 0:512], [RPS[mb_[blk]]], [RKST[i]])
                        dma(LKST[i], dst[seq, blk * 128:(blk + 1) * 128, pair * 512:(pair + 1) * 512], kst[i][:, :], [RKST[i]], [])
                        if SUB <= 3:
                            continue
                        if is_k:
                            b = banks.next()
                            for cl in range(4):
                                tr(PS[b][:, cl * 128:(cl + 1) * 128], kst[i][:, cl * 128:(cl + 1) * 128], [RKST[i]], [RPS[b]])
                            cp("act", memkT[:, pair * 4:pair * 4 + 4, blk * 128:(blk + 1) * 128],
                               PS[b][:, 0:512].rearrange("p (c t) -> p c t", t=128), [RPS[b]], [RMK])
                        else:
                            cp("act", memv[:, blk, pair * 512:(pair + 1) * 512], PS[mb_[blk]][:, 0:512], [RPS[mb_[blk]]], [RMV])

        def mem_phase_sample(b_):
            for blk in range(2):
                i = xs_rr.next()
                dma(LXS[i], xs[i][:, :], cmk[b_, blk * 128:(blk + 1) * 128, :], [], [RXS[i]])
                for half in range(2):
                    b = banks.next()
                    for cc in range(4):
                        c = half * 4 + cc
                        tr(PS[b][:, cc * 128:(cc + 1) * 128], xs[i][:, c * 128:(c + 1) * 128], [RXS[i]], [RPS[b]])
                    cp("act", memkT[:, half * 4:half * 4 + 4, blk * 128:(blk + 1) * 128],
                       PS[b][:, 0:512].rearrange("p (c t) -> p c t", t=128), [RPS[b]], [RMK])
            for blk in range(2):
                i = xs_rr.next()
                dma(LXS[i], xs[i][:, :], cmv[b_, blk * 128:(blk + 1) * 128, :], [], [RXS[i]])
                cp("pool", memv[:, blk, :], xs[i][:, :], [RXS[i]], [RMV])

        def diff_attn_sample():
            O1, O2, Z1, Z2 = 4, 5, 6, 7
            OZ = (O1, O2, Z1, Z2)
            T = DSEQ
            units = [(b_, h) for b_ in range(NSMP) for h in range(4)]

            def prefetch_fns(u):
                b_, h = units[u]
                bf = u % 2
                kbase = bf * 4128
                vbase = bf * 4224
                ckv = ck[b_].rearrange("(j p) f -> p j f", p=128)
                cvv = cv[b_].rearrange("(j p) f -> p j f", p=128)

                def grp_fn(grp, pbanks=None):
                    i = xs_rr.next()
                    dma(LXS[i], xs[i][:, :].rearrange("p (j f) -> p j f", f=128), ckv[:, grp * 8:(grp + 1) * 8, h * 128:(h + 1) * 128], [], [RXS[i]])
                    for half in range(2):
                        b = pbanks[half] if pbanks else banks.next()
                        for cc in range(4):
                            jj = half * 4 + cc
                            tr(PS[b][:, cc * 128:(cc + 1) * 128], xs[i][:, jj * 128:(jj + 1) * 128], [RXS[i]], [RPS[b]])
                        k0 = kbase + (grp * 8 + half * 4) * 128
                        cp(ev_rr.next(), kTb[:, k0:k0 + 512], PS[b][:, 0:512], [RPS[b]], [RKTC[bf]])
                    g2 = grp % 2
                    dma(LSQ[g2], sq[:, 2 * g2:2 * g2 + 2, :].rearrange("p c (j f) -> p (c j) f", f=128),
                        cvv[:, grp * 8:(grp + 1) * 8, h * 128:(h + 1) * 128], [], [RSQ[2 * g2], RSQ[2 * g2 + 1]])
                    cp("dve", vSb[:, vbase + grp * 1024:vbase + (grp + 1) * 1024].rearrange("p (c t) -> p c t", t=512), sq[:, 2 * g2:2 * g2 + 2, :],
                       [RSQ[2 * g2], RSQ[2 * g2 + 1]], [RVC[bf]])

                def tail_fn():
                    cp("pool", kTb[:, kbase + 4096:kbase + 4128], kTn[:, h, b_ * 32:(b_ + 1) * 32], [RKTN], [RKTC[bf]])

                return [lambda pb=None, g=g: grp_fn(g, pb) for g in range(4)] + [lambda pb=None: tail_fn()]

            for f in prefetch_fns(0):
                f()
            steps = [(u, J) for u in range(len(units)) for J in range(33)]
            sched = {3: 0, 11: 1, 19: 2, 27: 3, 30: 4}
            OB, ZB = 6, 7
            Pbuf = [Pt[0][0], Pt[0][1], Pt[1][0]]
            RPb = [RP[0][0], RP[0][1], RP[1][0]]

            def up(u):
                b_, h = units[u]
                bf = u % 2
                return b_, h, bf, bf * 4128, bf * 4224, slice(b_ * 32, (b_ + 1) * 32)

            def sbanks(n):
                return [2 * (n % 3), 2 * (n % 3) + 1]

            def S_(n):
                u, J = steps[n]
                b_, h, bf, kbase, vbase, qcol = up(u)
                nk = 128 if J < 32 else 32
                sbk = sbanks(n)
                for m in range(2):
                    mm(PS[sbk[m]][0:nk, 0:T], kTb[m * 64:(m + 1) * 64, kbase + J * 128:kbase + J * 128 + nk],
                       hT[m * 64:(m + 1) * 64, QT0 + h, qcol], True, True, [RKTC[bf], RH[QT0 + h]], [RPS[sbk[m]]])

            def E_(n):
                u, J = steps[n]
                b_, h, bf, kbase, vbase, qcol = up(u)
                nk = 128 if J < 32 else 32
                d = 32 - J
                sbk = sbanks(n)
                P = Pbuf[n % 3]
                rps = RPb[n % 3]
                bias = alibi[0:nk, (d + 3) * 4 + h:(d + 3) * 4 + h + 1]
                for m in range(2):
                    pc = slice(m * T, (m + 1) * T)
                    if d > 0:
                        act(P[0:nk, pc], PS[sbk[m]][0:nk, 0:T], AF.Exp, [RPS[sbk[m]], RC], [rps[m]], bias=bias, scale=0.125)
                    else:
                        act(dtmp[m][0:nk, 0:T], PS[sbk[m]][0:nk, 0:T], AF.Exp, [RPS[sbk[m]], RC], [RDT[m]], bias=bias, scale=0.125)
                        tt_("dve", P[0:nk, pc], dtmp[m][0:nk, 0:T], dmask[0:nk, h, 0:T], ALU.mult, [RDT[m], RC], [rps[m]])

            def PV_(n):
                u, J = steps[n]
                b_, h, bf, kbase, vbase, qcol = up(u)
                nk = 128 if J < 32 else 32
                P = Pbuf[n % 3]
                rps = RPb[n % 3][0:2]
                first, last = (J == 0), (J == 32)
                if J < 32:
                    lv = vSb[:, vbase + J * 128:vbase + (J + 1) * 128]
                    lres = RVC[bf]
                else:
                    lv = vnS[0:32, b_, h * 128:(h + 1) * 128]
                    lres = RVNS
                mm(PS[OB][:, 0:2 * T], lv, P[0:nk, 0:2 * T], first, last, [lres] + rps, [RPS[OB]])
                mm(PS[ZB][:, 0:2 * T], onesB[0:nk, :], P[0:nk, 0:2 * T], first, last, [RC] + rps, [RPS[ZB]])

            S_(0)
            if len(steps) > 1:
                S_(1)
            e_done = set()
            nxt_fns = prefetch_fns(1)
            for n, (u, J) in enumerate(steps):
                if n + 2 < len(steps):
                    S_(n + 2)
                if n not in e_done:
                    E_(n)
                PV_(n)
                if J in sched and nxt_fns:
                    nxt_fns[sched[J]](sbanks(n))
                if J == 32:
                    b_, h = units[u]
                    mid = None
                    if n + 1 < len(steps):
                        def mid(n=n):
                            E_(n + 1)
                            e_done.add(n + 1)
                    attn_finish(h, T, b_ * 32, PS[OB][:, 0:T], PS[OB][:, T:2 * T], PS[ZB][:, 0:T], PS[ZB][:, T:2 * T],
                                RPS[OB], RPS[OB], RPS[ZB], RPS[ZB], ZB, mid=mid)
                    nxt_fns = prefetch_fns(u + 2) if u + 2 < len(units) else []

        def mix_rhs(T):
            def f(kc):
                return hT[:, (AT0 + kc) if kc < 4 else (GT0 + kc - 4), 0:T]

            def r(kc):
                return RH[(AT0 + kc) if kc < 4 else (GT0 + kc - 4)]
            return f, r

        def oc_rhs(T):
            return (lambda kc: hT[:, OC0 + kc, 0:T]), (lambda kc: RH[OC0 + kc])

        def q_cross(T):
            rf, rr_ = xb_rhs(T)

            def ev(j, ol, b):
                c8 = 2 * j + ol
                cp("act", hT[:, QC0 + c8, 0:T], PS[b][:, 0:T], [RPS[b]], [RH[QC0 + c8]])
            proj_fm_multi("wq", [0, 1], T, rf, rr_, ev)
            proj_fm_multi("wq", [2, 3], T, rf, rr_, ev)

        class _Stop(Exception):
            pass

        nst = [0]

        def stage(fn, *a):
            if nst[0] >= limit:
                raise _Stop()
            nst[0] += 1
            fn(*a)

        try:
            mem_done = set()
            smp_pre = False
            for seq in range(n_seq):
                if seq not in mem_done:
                    stage(mem_phase_prompt, seq)
                pre = False
                for Q in range(n_tiles):
                    T = TT
                    pos0 = Q * TT
                    if not pre:
                        stage(load_T, lambda blk, seq=seq, pos0=pos0: xp[seq, pos0 + blk * 128:pos0 + (blk + 1) * 128, :], 4, True, Q > 0)
                    if Q + 1 < n_tiles and nst[0] < limit:
                        for k in range(2):
                            x_dma(xp[seq, pos0 + TT + k * 128:pos0 + TT + (k + 1) * 128, :], k)
                    if Q == n_tiles - 1 and seq + 1 == n_seq and do_sample and limit > 10 ** 8:
                        x_dma(xsm[:, :], 0)
                    stage(ffn, T, "gu1", "dn1", FFN_A if pre else 0)
                    stage(layer_norm, T, 0, EPS_DN)
                    dfr = []
                    stage(lambda *a: dfr.extend(w_in_stage(*a)), T, False, pos0, seq,
                          lambda blk, seq=seq, pos0=pos0: nk_p[seq, pos0 + blk * 128:pos0 + (blk + 1) * 128, :],
                          lambda blk, seq=seq, pos0=pos0: nv_p[seq, pos0 + blk * 128:pos0 + (blk + 1) * 128, :],
                          Q * 4)
                    stage(diff_attn_prompt, Q, dfr)
                    if Q + 1 < n_tiles and nst[0] < limit:
                        for k in range(2, 4):
                            x_dma(xp[seq, pos0 + TT + k * 128:pos0 + TT + (k + 1) * 128, :], k)
                    stage(resid_proj, "wout", T, *mix_rhs(T))
                    stage(layer_norm, T, 1, EPS_DN)
                    stage(q_cross, T)
                    stage(cross_attn, T, 0, RMK, RMV)
                    stage(resid_proj, "wo", T, *oc_rhs(T))
                    stage(layer_norm, T, 2, EPS_DN)
                    stage(ffn, T, "gu2", "dn2")
                    pre = (Q + 1 < n_tiles) and nst[0] < limit
                    last = (Q == n_tiles - 1) and nst[0] < limit and limit > 10 ** 8
                    mem_pre = last and seq + 1 < n_seq
                    smp_pre = last and seq + 1 == n_seq and do_sample
                    if mem_pre:
                        mem_phase_prompt(seq + 1, 1)
                    if smp_pre:
                        x_transpose(0, 0, dst_f32=False, dst_bf=True)
                        ffn_gu(128, "gu1", 0, 4)
                    if pre:
                        for blk in range(4):
                            x_transpose(blk, blk, dst_f32=False, dst_bf=True)
                        ffn_gu(T, "gu1", 0, 4)
                    stage(layer_norm, T, 3, EPS_DN, False)
                    if pre:
                        ffn_gu(T, "gu1", 4, 10)
                    if mem_pre:
                        mem_phase_prompt(seq + 1, 2)
                        mem_done.add(seq + 1)
                    if smp_pre:
                        ffn_gu(128, "gu1", 4, 10)
                    stage(store_y, 4, lambda blk, seq=seq, pos0=pos0: y_p[seq, pos0 + blk * 128:pos0 + (blk + 1) * 128, :])
                    if pre:
                        ffn_gu(T, "gu1", 10, FFN_A)
                        for blk in range(4):
                            x_transpose(blk, blk, dst_f32=True, dst_bf=False)
                    if smp_pre:
                        ffn_gu(128, "gu1", 10, FFN_A)
                        x_transpose(0, 0, dst_f32=True, dst_bf=False)
            if do_sample:
                T = 128
                if not smp_pre:
                    stage(load_T, lambda blk: xsm[:, :], 1)
                stage(ffn, T, "gu1", "dn1", FFN_A if smp_pre else 0)
                stage(layer_norm, T, 0, EPS_DN)
                stage(w_in_stage, T, True, 0, 0, lambda blk: nk_s[:, :], None, 0)
                stage(diff_attn_sample)
                stage(resid_proj, "wout", T, *mix_rhs(T))
                stage(layer_norm, T, 1, EPS_DN)
                stage(q_cross, T)
                for b_ in range(NSMP):
                    stage(mem_phase_sample, b_)
                    stage(cross_attn, DSEQ, b_ * 32, RMK, RMV)
                stage(resid_proj, "wo", T, *oc_rhs(T))
                stage(layer_norm, T, 2, EPS_DN)
                stage(ffn, T, "gu2", "dn2")
                stage(layer_norm, T, 3, EPS_DN, False)
                stage(store_y, 1, lambda blk: y_s[:, :])
            assert wstate["consumed"] == len(plan), (wstate, len(plan))
        except _Stop:
            pass
        S.finish()
        with nc.Block() as block:
            S.emit(block)
    return nc, S.n_instr


def _consts():
    slopes = 2.0 ** (-8.0 * np.arange(1, 5, dtype=np.float64) / 4)
    p = np.arange(128, dtype=np.float64)
    alibi = np.zeros((128, 36 * 4), np.float32)
    for d in range(-3, 33):
        for h in range(4):
            alibi[:, (d + 3) * 4 + h] = slopes[h] * (p - 128 * d)
    j = np.arange(128)[:, None]
    i = np.arange(128)[None, :]
    dmask = np.zeros((128, 4, 128), np.float32)
    allowed = (j // 64) <= (i // 64)
    for h in range(4):
        m = np.where(j <= i, 1.0, np.exp(-2.0 * slopes[h] * (j - i)))
        dmask[:, h, :] = np.where(allowed, m, 0.0)
    tril = (i >= j).astype(np.float32)
    return alibi, dmask, tril, np.eye(128, dtype=np.float32)


def _prep_inputs(inp):
    f = lambda a: np.ascontiguousarray(np.asarray(a, dtype=np.float32))
    alibi, dmask, tril, ident = _consts()
    lnp = np.zeros((128, 64), np.float32)
    for gi, (g, b) in enumerate((("ln1_g", "ln1_b"), ("ln2_g", "ln2_b"), ("ln3_g", "ln3_b"), ("ln4_g", "ln4_b"))):
        lnp[:, gi * 16:gi * 16 + 8] = f(inp[g])[0].reshape(8, 128).T
        lnp[:, gi * 16 + 8:gi * 16 + 16] = f(inp[b])[0].reshape(8, 128).T
    ggb = np.zeros((128, 2, 512), np.float32)
    ggb[:, 0, :] = f(inp["gmlp_ln_g"])[0][None, :]
    ggb[:, 1, :] = f(inp["gmlp_ln_b"])[0][None, :]
    ws = f(inp["gmlp_ws"])[0]
    ws_p = np.ascontiguousarray(ws.transpose(1, 0, 2))
    ws_s = np.zeros((128, 4, 128), np.float32)
    for r in range(4):
        ws_s[r * 32:(r + 1) * 32, :, r * 32:(r + 1) * 32] = ws[:, :32, :32].transpose(1, 0, 2)
    bs = f(inp["gmlp_bs"])[0]
    bs_p = bs.reshape(1, 512).copy()
    bs_s = np.ascontiguousarray(np.tile(bs[:, :32], (1, 4)).reshape(1, 512))
    lamv = np.zeros((128, 4, 64), np.float32)
    for k_, n in enumerate(("lambda_q1", "lambda_k1", "lambda_q2", "lambda_k2")):
        lamv[:, k_, :] = f(inp[n])[0][None, :]
    shared = {
        "w_gu1": f(inp["ffn1_w_gu"])[0], "w_dn1": f(inp["ffn1_w_down"])[0], "w_in": f(inp["w_in"])[0],
        "w_out": f(inp["w_out"])[0], "w_q": f(inp["cross_wq"])[0], "w_k": f(inp["cross_wk"])[0],
        "w_v": f(inp["cross_wv"])[0], "w_o": f(inp["cross_wo"])[0], "w_gu2": f(inp["ffn2_w_gu"])[0],
        "w_dn2": f(inp["ffn2_w_down"])[0],
        "lnp": lnp, "subg": f(inp["subln_g"])[0].reshape(128, 1).copy(), "ggb": ggb,
        "ws_p": ws_p, "ws_s": ws_s, "bs_p": bs_p, "bs_s": bs_s, "lamv": lamv,
        "ident": ident, "alibi": alibi, "dmask": dmask, "tril": tril,
    }
    x_prompt, x_sample = f(inp["x_prompt"]), f(inp["x_sample"])
    cache_k, cache_v = f(inp["cache_k"]), f(inp["cache_v"])
    cmk, cmv, mpr = f(inp["cache_mem_k"]), f(inp["cache_mem_v"]), f(inp["mem_prompt"])
    maps = []
    for c in range(N_CORES):
        m = dict(shared)
        m["xp"] = x_prompt[c * NSEQ:(c + 1) * NSEQ]
        m["xsm"] = x_sample[c * NSMP:(c + 1) * NSMP].reshape(128, D)
        m["ck"] = cache_k[0, c * NSMP:(c + 1) * NSMP].reshape(NSMP, PAST, 512)
        m["cv"] = cache_v[0, c * NSMP:(c + 1) * NSMP].reshape(NSMP, PAST, 512)
        m["cmk"] = cmk[0, c * NSMP:(c + 1) * NSMP].reshape(NSMP, NMEM, D)
        m["cmv"] = cmv[0, c * NSMP:(c + 1) * NSMP].reshape(NSMP, NMEM, D)
        m["mp"] = mpr[c * NSEQ:(c + 1) * NSEQ]
        maps.append(m)
    return maps


def _assemble(results):
    cat = lambda k: np.concatenate([np.asarray(r[k]) for r in results], axis=0)
    B, BS = N_CORES * NSEQ, N_CORES * NSMP
    y_p = cat("y_p").reshape(B, SEQ, D)
    y_s = cat("y_s").reshape(BS, DSEQ, D)
    nk_p = cat("nk_p").reshape(1, B, SEQ, 4, 2, 64)
    nv_p = cat("nv_p").reshape(1, B, SEQ, 4, 128)
    nmk_p = cat("nmk_p").reshape(1, B, NMEM, 4, 256)
    nmv_p = cat("nmv_p").reshape(1, B, NMEM, 4, 256)
    nk_s = cat("nk_s").reshape(1, BS, DSEQ, 4, 2, 64)
    nv_s = cat("nv_s").reshape(1, BS, DSEQ, 4, 128)
    ngv_s = cat("ngv_s").reshape(1, BS, DSEQ, 4, 128)
    return tuple(np.ascontiguousarray(a, dtype=np.float32) for a in (y_p, y_s, nk_p, nv_p, nmk_p, nmv_p, nk_s, nv_s, ngv_s))


def kernel(**inputs):
    maps = _prep_inputs(inputs)
    nc, _ = build_nc()
    res = run_bass_kernel_spmd(nc, maps, core_ids=list(range(N_CORES)))
    return _assemble(res.results)
```

```python
import contextlib
import math
import os

import numpy as np
import concourse.bass as bass
import concourse.mybir as mybir
from concourse.bass_utils import run_bass_kernel_spmd

F32 = mybir.dt.float32
BF16 = mybir.dt.bfloat16
AF = mybir.ActivationFunctionType
ALU = mybir.AluOpType
AX = mybir.AxisListType

N_CORES = 8
D = 1024
KC = 8
DFF = 2816
NFC = 22
SEQ = 2048
TT = 512
NSEQ = 2
NSMP = 4
DSEQ = 32
PAST = 4096
NMEM = 256
ALPHA = 2.0 ** 0.25
EPS = 1e-5
EPS_DN = EPS / (ALPHA * ALPHA)
LAM_INIT = 0.2
NSLOT = 5
SLOTW = 2816


class Res:
    __slots__ = ("name", "w", "r", "excl")

    def __init__(self, name, excl=False):
        self.name = name
        self.w = None
        self.r = {}
        self.excl = excl


class _Eng:
    def __init__(self, name, sem):
        self.name = name
        self.sem = sem
        self.count = 0
        self.q = []
        self.waited = {}


class _Lane:
    def __init__(self, idx, sem):
        self.idx = idx
        self.sem = sem
        self.count = 0


class Sched:
    ENGS = ("pe", "act", "dve", "pool", "sp")

    def __init__(self, nc, stack, n_lanes):
        self.nc = nc
        self.eng = {}
        for n in self.ENGS:
            self.eng[n] = _Eng(n, stack.enter_context(nc.semaphore("sem_" + n)))
        self.lanes = [_Lane(i, stack.enter_context(nc.semaphore("lane%d" % i))) for i in range(n_lanes)]
        self._lane_next = 0
        self.n_instr = 0

    def new_lane(self):
        l = self.lanes[self._lane_next]
        self._lane_next += 1
        return l

    def _collect(self, E, reads, writes, extra=()):
        need = {}

        def add(t):
            if t is None:
                return
            k = t[0]
            if k not in need or need[k][2] < t[2]:
                need[k] = t

        for r in reads:
            add(r.w)
            if r.excl:
                for t in r.r.values():
                    if t[0] != E.name:
                        add(t)
        for r in writes:
            add(r.w)
            for t in r.r.values():
                add(t)
        for t in extra:
            add(t)
        final = []
        for k, (kk, sem, val) in need.items():
            if E.waited.get(k, 0) >= val:
                continue
            if E.name == "pe" and k == "pe":
                continue
            E.waited[k] = val
            final.append((sem, val))
        return final

    @staticmethod
    def _mark(tok, reads, writes):
        k = tok[0]
        for r in reads:
            o = r.r.get(k)
            if o is None or o[2] < tok[2]:
                r.r[k] = tok
        for r in writes:
            r.w = tok
            r.r = {}

    def op(self, eng, fn, reads=(), writes=()):
        E = self.eng[eng]
        waits = self._collect(E, reads, writes)
        E.count += 1
        tok = (E.name, E.sem, E.count)
        E.q.append((waits, fn, E.sem, 1))
        self._mark(tok, reads, writes)
        self.n_instr += 1
        return tok

    def dma(self, queue, lane, fn, reads=(), writes=()):
        Q = self.eng[queue]
        key = "L%d" % lane.idx
        prev = (key, lane.sem, 16 * lane.count) if lane.count else None
        waits = self._collect(Q, reads, writes, extra=(prev,))
        lane.count += 1
        tok = (key, lane.sem, 16 * lane.count)
        Q.q.append((waits, fn, lane.sem, 16))
        self._mark(tok, reads, writes)
        self.n_instr += 1
        return tok

    def finish(self, queue="sp"):
        Q = self.eng[queue]
        for l in self.lanes:
            if l.count:
                k = "L%d" % l.idx
                if Q.waited.get(k, 0) < 16 * l.count:
                    Q.waited[k] = 16 * l.count
                    Q.q.append(([(l.sem, 16 * l.count)], None, None, 0))
        for n in ("pe", "act", "dve", "pool"):
            E = self.eng[n]
            if E.count and Q.waited.get(n, 0) < E.count:
                Q.waited[n] = E.count
                Q.q.append(([(E.sem, E.count)], None, None, 0))

    def emit(self, block):
        def replay(e, E):
            for waits, fn, sem, inc in E.q:
                for s, v in waits:
                    e.wait_ge(s, v)
                if fn is not None:
                    fn(e).then_inc(sem, inc)
            E.q = []

        @block.tensor
        def _(e):
            replay(e, self.eng["pe"])

        @block.scalar
        def _(e):
            replay(e, self.eng["act"])

        @block.vector
        def _(e):
            replay(e, self.eng["dve"])

        @block.gpsimd
        def _(e):
            replay(e, self.eng["pool"])

        @block.sync
        def _(e):
            replay(e, self.eng["sp"])


class RR:
    def __init__(self, items):
        self.items = list(items)
        self.i = 0

    def next(self):
        v = self.items[self.i % len(self.items)]
        self.i += 1
        return v


W_SPECS = [
    ("gu1", "gu", NFC), ("dn1", "dn", 8), ("win", "k256", 10), ("wout", "k256", 4),
    ("wq", "k256", 4), ("wo", "k256", 4), ("gu2", "gu", NFC), ("dn2", "dn", 8),
    ("wk", "k256", 4), ("wv", "k256", 4),
]
W_KIND = {n: k for n, k, _ in W_SPECS}
W_NCH = {n: c for n, _, c in W_SPECS}
TILE_ORDER = ["gu1", "dn1", "win", "wout", "wq", "wo", "gu2", "dn2"]


def chunk_width(kind):
    return SLOTW if kind == "dn" else 2048


def plan_chunks(n_seq, n_tiles, do_sample):
    plan = []
    for s in range(n_seq):
        for w in ("wk", "wv"):
            plan += [(w, j) for j in range(W_NCH[w])]
        for t in range(n_tiles):
            for w in TILE_ORDER:
                plan += [(w, j) for j in range(W_NCH[w])]
    if do_sample:
        for w in TILE_ORDER:
            plan += [(w, j) for j in range(W_NCH[w])]
    return plan


def build_nc(n_seq=NSEQ, n_tiles=SEQ // TT, do_sample=True, limit=10 ** 9, skip_prologue=False):
    nc = bass.Bass("TRN2", target_bir_lowering=False)

    def din(name, shape, dt=F32):
        return nc.dram_tensor(name, list(shape), dt, kind="ExternalInput").ap()

    def dout(name, shape, dt=F32):
        return nc.dram_tensor(name, list(shape), dt, kind="ExternalOutput").ap()

    xp = din("xp", [NSEQ, SEQ, D])
    xsm = din("xsm", [128, D])
    ck = din("ck", [NSMP, PAST, 512])
    cv = din("cv", [NSMP, PAST, 512])
    cmk = din("cmk", [NSMP, NMEM, D])
    cmv = din("cmv", [NSMP, NMEM, D])
    mp = din("mp", [NSEQ, NMEM, D])
    wsrc = {
        "gu1": din("w_gu1", [D, 2 * DFF]), "dn1": din("w_dn1", [DFF, D]), "win": din("w_in", [D, 2560]),
        "wout": din("w_out", [D, D]), "wq": din("w_q", [D, D]), "wk": din("w_k", [D, D]),
        "wv": din("w_v", [D, D]), "wo": din("w_o", [D, D]), "gu2": din("w_gu2", [D, 2 * DFF]),
        "dn2": din("w_dn2", [DFF, D]),
    }
    lnp_d = din("lnp", [128, 64])
    subg_d = din("subg", [128, 1])
    ggb_d = din("ggb", [128, 2, 512])
    wsp_d = din("ws_p", [128, 4, 128])
    wss_d = din("ws_s", [128, 4, 128])
    bsp_d = din("bs_p", [1, 512])
    bss_d = din("bs_s", [1, 512])
    lam_d = din("lamv", [128, 4, 64])
    ident_d = din("ident", [128, 128])
    alibi_d = din("alibi", [128, 36 * 4])
    dmask_d = din("dmask", [128, 4, 128])
    tril_d = din("tril", [128, 128])

    y_p = dout("y_p", [NSEQ, SEQ, D])
    y_s = dout("y_s", [128, D])
    nk_p = dout("nk_p", [NSEQ, SEQ, 512])
    nv_p = dout("nv_p", [NSEQ, SEQ, 512])
    nmk_p = dout("nmk_p", [NSEQ, NMEM, D])
    nmv_p = dout("nmv_p", [NSEQ, NMEM, D])
    nk_s = dout("nk_s", [128, 512])
    nv_s = dout("nv_s", [128, 512])
    ngv_s = dout("ngv_s", [128, 512])

    scr = {}
    for name, kind, nch in W_SPECS:
        scr[name] = nc.dram_tensor("scr_" + name, [nch, 128, chunk_width(kind)], BF16, kind="Internal").ap()
    RSCR = {name: Res("scr_" + name) for name, _, _ in W_SPECS}

    with contextlib.ExitStack() as st:
        S = Sched(nc, st, n_lanes=90)

        def sb(stack, name, shape, dt):
            return stack.enter_context(nc.sbuf_tensor("sb_" + name, list(shape), dt))

        ring = [sb(st, "ring%d" % i, [128, SLOTW], BF16) for i in range(NSLOT)]
        RS = [Res("ring%d" % i) for i in range(NSLOT)]
        LS = [S.new_lane() for _ in range(NSLOT)]

        with contextlib.ExitStack() as pst:
            NST = 5
            LOOK = NST - 1
            stg = [sb(pst, "stg%d" % i, [128, SLOTW], F32) for i in range(NST)]
            stb = [sb(pst, "stb%d" % i, [128, SLOTW], BF16) for i in range(NST)]
            RSTG = [Res("stg%d" % i) for i in range(NST)]
            RSTG2 = [Res("stg2_%d" % i) for i in range(NST)]
            RSTB = [Res("stb%d" % i) for i in range(NST)]
            LLD = [S.new_lane() for _ in range(NST)]
            LLD2 = [S.new_lane() for _ in range(NST)]
            LSTO = [S.new_lane() for _ in range(NST)]
            cast_eng = RR(["dve", "act"])
            chunks = [] if skip_prologue else [(name, kind, j) for name, kind, nch in W_SPECS for j in range(nch)]

            def p_load(ci):
                name, kind, j = chunks[ci]
                i = ci % NST
                src = wsrc[name]
                if kind == "gu":
                    v = stg[i][:, 0:2048].rearrange("p (k n) -> p k n", n=256)
                    sv = src.rearrange("(k p) n -> p k n", p=128)
                    S.dma("sp", LLD[i], lambda e: e.dma_start(out=v[:, :, 0:128], in_=sv[:, :, j * 128:(j + 1) * 128]), writes=[RSTG[i]])
                    S.dma("sp", LLD2[i], lambda e: e.dma_start(out=v[:, :, 128:256], in_=sv[:, :, DFF + j * 128:DFF + (j + 1) * 128]), writes=[RSTG2[i]])
                elif kind == "dn":
                    v = stg[i][:, 0:SLOTW].rearrange("p (f n) -> p f n", n=128)
                    sv = src.rearrange("(f p) n -> p f n", p=128)
                    S.dma("sp", LLD[i], lambda e: e.dma_start(out=v[:, :, :], in_=sv[:, :, j * 128:(j + 1) * 128]), writes=[RSTG[i]])
                else:
                    v = stg[i][:, 0:2048].rearrange("p (k n) -> p k n", n=256)
                    sv = src.rearrange("(k p) n -> p k n", p=128)
                    S.dma("sp", LLD[i], lambda e: e.dma_start(out=v[:, :, :], in_=sv[:, :, j * 256:(j + 1) * 256]), writes=[RSTG[i]])

            def p_cast_store(ci):
                name, kind, j = chunks[ci]
                i = ci % NST
                wdt = chunk_width(kind)
                ce = cast_eng.next()
                if ce == "act":
                    S.op("act", lambda e: e.activation(out=stb[i][:, 0:wdt], in_=stg[i][:, 0:wdt], func=AF.Copy),
                         reads=[RSTG[i], RSTG2[i]], writes=[RSTB[i]])
                else:
                    S.op(ce, lambda e: e.tensor_copy(out=stb[i][:, 0:wdt], in_=stg[i][:, 0:wdt]),
                         reads=[RSTG[i], RSTG2[i]], writes=[RSTB[i]])
                S.dma("sp", LSTO[i], lambda e: e.dma_start(out=scr[name][j, :, :], in_=stb[i][:, 0:wdt]),
                      reads=[RSTB[i]], writes=[RSCR[name]])

            for step in range(len(chunks) + LOOK):
                if step < len(chunks):
                    p_load(step)
                if step >= LOOK:
                    p_cast_store(step - LOOK)
            S.finish()
            with nc.Block() as block:
                S.emit(block)

        mst = st
        xf = sb(mst, "xf", [128, KC, TT], F32)
        xb = sb(mst, "xb", [128, KC, TT], BF16)
        hT = sb(mst, "hT", [128, NFC, TT], BF16)
        sq = sb(mst, "sq", [128, 4, TT], F32)
        kTb = sb(mst, "kTb", [128, 8256], BF16)
        vSb = sb(mst, "vSb", [128, 8448], BF16)
        memkT = sb(mst, "memkT", [128, 8, NMEM], BF16)
        memv = sb(mst, "memv", [128, 2, D], BF16)
        uT = sb(mst, "uT", [128, 4, TT], F32)
        Pt = [[sb(mst, "P%d_%d" % (m, j), [128, TT], BF16) for j in range(2)] for m in range(2)]
        s1 = sb(mst, "s1", [128, TT], F32)
        s2 = sb(mst, "s2", [128, TT], F32)
        m2 = sb(mst, "m2", [128, TT], F32)
        var = sb(mst, "var", [128, TT], F32)
        rstd = sb(mst, "rstd", [128, TT], F32)
        t1 = [sb(mst, "t1_%d" % j, [128, TT], F32) for j in range(4)]
        sg = [sb(mst, "sg%d" % j, [128, TT], F32) for j in range(2)]
        xs = [sb(mst, "xs%d" % j, [128, D], F32) for j in range(2)]
        kst = [sb(mst, "kst%d" % j, [128, 512], F32) for j in range(2)]
        vst = [sb(mst, "vst%d" % j, [128, 512], F32) for j in range(2)]
        gvf = [sb(mst, "gvf%d" % j, [128, 512], F32) for j in range(2)]
        vnb = [sb(mst, "vnb%d" % j, [128, 512], BF16) for j in range(2)]
        dtmp = [sb(mst, "dtmp%d" % j, [128, 128], F32) for j in range(2)]
        small = sb(mst, "small", [128, 32], F32)
        lamc = sb(mst, "lamc", [128, 16], F32)
        kTn = sb(mst, "kTn", [128, 4, 128], BF16)
        vnS = sb(mst, "vnS", [32, NSMP, 512], BF16)
        lnp = sb(mst, "lnp", [128, 64], F32)
        subg = sb(mst, "subg", [128, 1], F32)
        ggb = sb(mst, "ggb", [128, 2, 512], F32)
        WsT = [sb(mst, "WsT%d" % j, [128, 4, 128], BF16) for j in range(2)]
        bsr = [sb(mst, "bsr%d" % j, [33, 512], F32) for j in range(2)]
        bsb = [sb(mst, "bsb%d" % j, [33, 512], BF16) for j in range(2)]
        bstmp = sb(mst, "bstmp", [33, 512], BF16)
        ident = sb(mst, "ident", [128, 128], F32)
        alibi = sb(mst, "alibi", [128, 36 * 4], F32)
        dmask = sb(mst, "dmask", [128, 4, 128], F32)
        tril = sb(mst, "tril", [128, 128], F32)
        onesD = sb(mst, "onesD", [128, 128], F32)
        onesE = sb(mst, "onesE", [128, 128], F32)
        ones1 = sb(mst, "ones1", [1, 128], F32)
        onesB = sb(mst, "onesB", [128, 128], BF16)
        PS = [mst.enter_context(nc.psum_tensor("ps%d" % i, [128, 512], F32)) for i in range(8)]

        RX = [Res("xf%d" % c) for c in range(KC)]
        RXB = [Res("xb%d" % c) for c in range(KC)]
        RH = [Res("hT%d" % c) for c in range(NFC)]
        RSQ = [Res("sq%d" % c) for c in range(4)]
        RKT = [Res("kT%d" % h) for h in range(4)]
        RVS = Res("vS")
        RKTC = [Res("kTc%d" % i) for i in range(2)]
        RVC = [Res("vC%d" % i) for i in range(2)]
        RMK = Res("memkT")
        RMV = Res("memv")
        RU = [Res("uT%d" % g) for g in range(4)]
        RP = [[[Res("P%d_%d_%d" % (m, j, q)) for q in range(4)] for j in range(2)] for m in range(2)]
        PTc = [Pt[0][0], Pt[0][1]]
        RPTC = [RP[0][0], RP[0][1]]
        RS1, RS2, RM2, RVAR, RRSTD = Res("s1"), Res("s2"), Res("m2"), Res("var"), Res("rstd")
        RT1 = [Res("t1_%d" % j) for j in range(4)]
        RSG = [Res("sg%d" % j) for j in range(2)]
        RXS = [Res("xs%d" % j) for j in range(2)]
        RKST = [Res("kst%d" % j) for j in range(2)]
        RVST = [Res("vst%d" % j) for j in range(2)]
        RGVF = [Res("gvf%d" % j) for j in range(2)]
        RVNB = [Res("vnb%d" % j) for j in range(2)]
        RDT = [Res("dtmp%d" % j) for j in range(2)]
        RSM = Res("small")
        RLAM = Res("lamc")
        RKTN, RVNS = Res("kTn"), Res("vnS")
        RC = Res("consts")
        RWST = Res("WsT")
        RWSRAW = RGVF[0]
        wsraw = gvf[0][:, :].rearrange("p (g s) -> p g s", s=128)
        lamv = gvf[1][:, 0:256].rearrange("p (g s) -> p g s", s=64)
        RPS = [Res("ps%d" % i, excl=True) for i in range(8)]
        LXS = [S.new_lane() for _ in range(2)]
        LKST = [S.new_lane() for _ in range(2)]
        LVST = [S.new_lane() for _ in range(2)]
        LGV = [S.new_lane() for _ in range(2)]
        LSQ = [S.new_lane() for _ in range(4)]
        LC = S.new_lane()

        banks = RR(range(8))
        banksS = [RR([0, 1]), RR([2, 3])]

        QT0, AT0, GT0, QC0, OC0 = 0, 4, 8, 12, 0

        plan = plan_chunks(n_seq, n_tiles, do_sample)
        wstate = {"issued": 0, "consumed": 0}
        slot_of = {}

        def w_issue(slot):
            i = wstate["issued"]
            if i >= len(plan):
                return
            name, j = plan[i]
            wdt = chunk_width(W_KIND[name])
            wstate["issued"] += 1
            slot_of[i] = slot
            S.dma("sp", LS[slot], lambda e, slot=slot, name=name, j=j, wdt=wdt: e.dma_start(out=ring[slot][:, 0:wdt], in_=scr[name][j, :, :]),
                  reads=[RSCR[name]], writes=[RS[slot]])

        def w_acquire(name, j):
            i = wstate["consumed"]
            assert plan[i] == (name, j), (plan[i], name, j)
            wstate["consumed"] += 1
            return slot_of.pop(i)

        def w_release(slot):
            w_issue(slot)

        def mm(out, lhsT, rhs, start, stop, reads, writes):
            S.op("pe", lambda e: e.matmul(out, lhsT=lhsT, rhs=rhs, start=start, stop=stop), reads, writes)

        def tr(out, in_, reads, writes):
            S.op("pe", lambda e: e.transpose(out=out, in_=in_, identity=ident[:]), list(reads) + [RC], writes)

        def act(out, in_, func, reads, writes, bias=None, scale=None):
            kw = {}
            if bias is not None:
                kw["bias"] = bias
            if scale is not None:
                kw["scale"] = scale
            S.op("act", lambda e: e.activation(out=out, in_=in_, func=func, **kw), reads, writes)

        def cp(eng, out, in_, reads, writes):
            if eng == "act":
                act(out, in_, AF.Copy, reads, writes)
            else:
                S.op(eng, lambda e: e.tensor_copy(out=out, in_=in_), reads, writes)

        def tt_(eng, out, in0, in1, op, reads, writes):
            S.op(eng, lambda e: e.tensor_tensor(out=out, in0=in0, in1=in1, op=op), reads, writes)

        def ts_(eng, out, in0, s1_, s2_, op0, op1, reads, writes):
            if op1 is None:
                S.op(eng, lambda e: e.tensor_scalar(out=out, in0=in0, scalar1=s1_, scalar2=None, op0=op0), reads, writes)
            else:
                S.op(eng, lambda e: e.tensor_scalar(out=out, in0=in0, scalar1=s1_, scalar2=s2_, op0=op0, op1=op1), reads, writes)

        def stt(out, in0, scalar, in1, op0, op1, reads, writes):
            S.op("dve", lambda e: e.scalar_tensor_tensor(out=out, in0=in0, scalar=scalar, in1=in1, op0=op0, op1=op1), reads, writes)

        def dma(lane, out, in_, reads, writes):
            S.dma("sp", lane, lambda e: e.dma_start(out=out, in_=in_), reads, writes)

        for dst, src in ((lnp, lnp_d), (subg, subg_d), (ggb, ggb_d), (ident, ident_d),
                         (alibi, alibi_d), (dmask, dmask_d), (tril, tril_d)):
            dma(LC, dst[:], src, [], [RC])
        for wi, bsd in enumerate((bsp_d, bss_d)):
            dma(LC, bsr[wi][0:1, :], bsd, [], [RC])
            dma(LC, bsr[wi][32:33, :], bsd, [], [RC])
            S.op("pool", lambda e, wi=wi: e.memset(bsb[wi][:, :], 0.0), [], [RC])
            cp("dve", bsb[wi][0:1, :], bsr[wi][0:1, :], [RC], [RC])
            cp("dve", bstmp[32:33, :], bsr[wi][32:33, :], [RC], [RC])
            tt_("dve", bsb[wi][32:33, :], bsr[wi][32:33, :], bstmp[32:33, :], ALU.subtract, [RC], [RC])
        dma(LGV[1], lamv, lam_d, [], [RGVF[1]])
        S.op("pool", lambda e: e.memset(onesD[:], 1.0 / D), [], [RC])
        S.op("pool", lambda e: e.memset(onesE[:], 1.0 / 128), [], [RC])
        S.op("pool", lambda e: e.memset(ones1[:], 1.0), [], [RC])
        S.op("pool", lambda e: e.memset(onesB[:], 1.0), [], [RC])
        tt_("dve", sq[:, 0, 0:64], lamv[:, 0, :], lamv[:, 1, :], ALU.mult, [RGVF[1]], [RSQ[0]])
        tt_("dve", sq[:, 0, 64:128], lamv[:, 2, :], lamv[:, 3, :], ALU.mult, [RGVF[1], RSQ[0]], [RSQ[0]])
        S.op("dve", lambda e: e.tensor_reduce(out=lamc[:, 2:3], in_=sq[:, 0, 0:64], axis=AX.X, op=ALU.add), [RSQ[0]], [RLAM])
        S.op("dve", lambda e: e.tensor_reduce(out=lamc[:, 3:4], in_=sq[:, 0, 64:128], axis=AX.X, op=ALU.add), [RSQ[0], RLAM], [RLAM])
        act(lamc[:, 4:6], lamc[:, 2:4], AF.Exp, [RLAM], [RLAM])
        tt_("dve", lamc[:, 6:7], lamc[:, 5:6], lamc[:, 4:5], ALU.subtract, [RLAM], [RLAM])
        ts_("dve", lamc[:, 0:1], lamc[:, 6:7], -LAM_INIT, None, ALU.add, None, [RLAM], [RLAM])
        ts_("dve", lamc[:, 1:2], subg[:, 0:1], 1.0 - LAM_INIT, None, ALU.mult, None, [RLAM, RC], [RLAM])
        for k_, eps in enumerate((EPS, EPS_DN)):
            S.op("pool", lambda e, k_=k_, eps=eps: e.memset(lamc[:, 8 + k_:9 + k_], eps), [RLAM], [RLAM])
        LNC = (-16.0, -16.0, -4.0, -1.0)
        for h_ in range(4):
            S.op("pool", lambda e, h_=h_: e.memset(lamc[:, 10 + h_:11 + h_], LNC[h_]), [RLAM], [RLAM])
        neg_lam = lamc[:, 0:1]
        g08 = lamc[:, 1:2]
        eps_cols = {EPS: lamc[:, 8:9], EPS_DN: lamc[:, 9:10]}
        for wi, wsd in enumerate((wsp_d, wss_d)):
            dma(LGV[0], wsraw, wsd, [], [RWSRAW])
            for g in range(4):
                b = banks.next()
                tr(PS[b][:, 0:128], wsraw[:, g, :], [RWSRAW], [RPS[b]])
                tt_("dve", WsT[wi][:, g, :], PS[b][:, 0:128], tril[:], ALU.mult, [RPS[b], RC], [RWST])

        for sl in range(NSLOT):
            w_issue(sl)

        xs_rr = RR([0, 1])
        kst_rr = RR([0, 1])
        vst_rr = RR([0, 1])
        gv_rr = RR([0, 1])
        sg_rr = RR([0, 1])
        t1_rr = RR([0, 1, 2, 3])
        ev_rr = RR(["dve", "act"])

        xstage = [
            (xs[0][:, :], [RXS[0]], LXS[0]),
            (xs[1][:, :], [RXS[1]], LXS[1]),
            (sq[:, 0:2, :].rearrange("p c t -> p (c t)"), [RSQ[0], RSQ[1]], LSQ[0]),
            (sq[:, 2:4, :].rearrange("p c t -> p (c t)"), [RSQ[2], RSQ[3]], LSQ[1]),
        ]

        def x_dma(src_ap, k):
            buf, rr_, lane = xstage[k]
            dma(lane, buf, src_ap, [], rr_)

        def x_transpose(blk, k, dst_f32=True, dst_bf=True):
            buf, rr_, lane = xstage[k]
            for half in range(2):
                b = banks.next()
                for cc in range(4):
                    c = half * 4 + cc
                    tr(PS[b][:, cc * 128:(cc + 1) * 128], buf[:, c * 128:(c + 1) * 128], rr_, [RPS[b]])
                pv = PS[b][:, 0:512].rearrange("p (c t) -> p c t", t=128)
                cs = slice(half * 4, half * 4 + 4)
                ts = slice(blk * 128, (blk + 1) * 128)
                if dst_f32:
                    cp("dve", xf[:, cs, ts], pv, [RPS[b]], RX[cs])
                if dst_bf:
                    cp("act", xb[:, cs, ts], pv, [RPS[b]], RXB[cs])

        def load_T(src_fn, NB, dst_f32=True, prefetched=False):
            for blk in range(NB):
                k = blk if NB == 4 else xs_rr.next()
                if not prefetched:
                    x_dma(src_fn(blk), k)
                x_transpose(blk, k, dst_f32)

        def ln_acc(T, oc):
            if oc == 0:
                act(s1[:, 0:T], xf[:, 0, 0:T], AF.Copy, [RX[0]], [RS1])
                act(s2[:, 0:T], xf[:, 0, 0:T], AF.Square, [RX[0]], [RS2])
            else:
                j = t1_rr.next()
                act(t1[j][:, 0:T], xf[:, oc, 0:T], AF.Square, [RX[oc]], [RT1[j]])
                tt_("pool", s1[:, 0:T], s1[:, 0:T], xf[:, oc, 0:T], ALU.add, [RS1, RX[oc]], [RS1])
                tt_("dve", s2[:, 0:T], s2[:, 0:T], t1[j][:, 0:T], ALU.add, [RS2, RT1[j]], [RS2])

        def layer_norm(T, gi, eps, want_bf=True):
            bm, be = banks.next(), banks.next()
            mm(PS[bm][:, 0:T], onesD[:], s1[:, 0:T], True, True, [RC, RS1], [RPS[bm]])
            mm(PS[be][:, 0:T], onesD[:], s2[:, 0:T], True, True, [RC, RS2], [RPS[be]])
            act(m2[:, 0:T], PS[bm][:, 0:T], AF.Square, [RPS[bm]], [RM2])
            tt_("dve", var[:, 0:T], PS[be][:, 0:T], m2[:, 0:T], ALU.subtract, [RPS[be], RM2], [RVAR])
            act(var[:, 0:T], var[:, 0:T], AF.Ln, [RVAR, RLAM], [RVAR], bias=eps_ap(eps), scale=1.0)
            act(rstd[:, 0:T], var[:, 0:T], AF.Exp, [RVAR], [RRSTD], scale=-0.5)
            for c in range(KC):
                j = t1_rr.next()
                tt_("dve", t1[j][:, 0:T], xf[:, c, 0:T], PS[bm][:, 0:T], ALU.subtract, [RX[c], RPS[bm]], [RT1[j]])
                tt_("dve" if c in (0, 3, 6) else "pool", t1[j][:, 0:T], t1[j][:, 0:T], rstd[:, 0:T], ALU.mult, [RT1[j], RRSTD], [RT1[j]])
                gcol = lnp[:, gi * 16 + c:gi * 16 + c + 1]
                bcol = lnp[:, gi * 16 + 8 + c:gi * 16 + 8 + c + 1]
                if want_bf:
                    act(xb[:, c, 0:T], t1[j][:, 0:T], AF.Identity, [RT1[j], RC], [RXB[c]], bias=bcol, scale=gcol)
                act(xf[:, c, 0:T], t1[j][:, 0:T], AF.Identity, [RT1[j], RC], [RX[c]], bias=bcol, scale=gcol)

        def eps_ap(eps):
            return eps_cols[eps]

        FFN_A = 14

        def ffn_gu(T, gu, fc0, fc1):
            NG = 3
            if fc0 == 0:
                sl = [w_acquire(gu, fc) for fc in range(NG)]
                bgs = [(banks.next(), banks.next()) for _ in range(NG)]
                for kc in range(KC):
                    for g in range(NG):
                        mm(PS[bgs[g][0]][:, 0:T], ring[sl[g]][:, kc * 256:kc * 256 + 128], xb[:, kc, 0:T], kc == 0, kc == KC - 1, [RS[sl[g]], RXB[kc]], [RPS[bgs[g][0]]])
                        mm(PS[bgs[g][1]][:, 0:T], ring[sl[g]][:, kc * 256 + 128:kc * 256 + 256], xb[:, kc, 0:T], kc == 0, kc == KC - 1, [RS[sl[g]], RXB[kc]], [RPS[bgs[g][1]]])
                for g in range(NG):
                    w_release(sl[g])
                for g in range(NG):
                    j = sg_rr.next()
                    act(sg[j][:, 0:T], PS[bgs[g][0]][:, 0:T], AF.Silu, [RPS[bgs[g][0]]], [RSG[j]])
                    tt_("dve", hT[:, g, 0:T], PS[bgs[g][1]][:, 0:T], sg[j][:, 0:T], ALU.mult, [RPS[bgs[g][1]], RSG[j]], [RH[g]])
                fc0 = NG
            for fc in range(fc0, fc1):
                s = w_acquire(gu, fc)
                bg, bu = banks.next(), banks.next()
                for kc in range(KC):
                    mm(PS[bg][:, 0:T], ring[s][:, kc * 256:kc * 256 + 128], xb[:, kc, 0:T], kc == 0, kc == KC - 1, [RS[s], RXB[kc]], [RPS[bg]])
                for kc in range(KC):
                    mm(PS[bu][:, 0:T], ring[s][:, kc * 256 + 128:kc * 256 + 256], xb[:, kc, 0:T], kc == 0, kc == KC - 1, [RS[s], RXB[kc]], [RPS[bu]])
                w_release(s)
                j = sg_rr.next()
                act(sg[j][:, 0:T], PS[bg][:, 0:T], AF.Silu, [RPS[bg]], [RSG[j]])
                tt_("dve", hT[:, fc, 0:T], PS[bu][:, 0:T], sg[j][:, 0:T], ALU.mult, [RPS[bu], RSG[j]], [RH[fc]])

        def ffn(T, gu, dn, fc_start=0):
            ffn_gu(T, gu, fc_start, NFC)
            for oc in range(KC):
                s = w_acquire(dn, oc)
                b = banks.next()
                for fc in range(NFC):
                    mm(PS[b][:, 0:T], ring[s][:, fc * 128:(fc + 1) * 128], hT[:, fc, 0:T], fc == 0, fc == NFC - 1, [RS[s], RH[fc]], [RPS[b]])
                w_release(s)
                stt(xf[:, oc, 0:T], PS[b][:, 0:T], 0.5 / ALPHA, xf[:, oc, 0:T], ALU.mult, ALU.add, [RPS[b], RX[oc]], [RX[oc]])
                ln_acc(T, oc)

        def proj_fm(wname, j, T, rhs_fn, rhs_res_fn, evac):
            proj_fm_multi(wname, [j], T, rhs_fn, rhs_res_fn, lambda jj, ol, b: evac(ol, b))

        def proj_fm_multi(wname, js, T, rhs_fn, rhs_res_fn, evac):
            sl = [w_acquire(wname, j) for j in js]
            bs_ = [[banks.next(), banks.next()] for _ in js]
            for kc in range(KC):
                for gi_, s in enumerate(sl):
                    for ol in range(2):
                        b = bs_[gi_][ol]
                        mm(PS[b][:, 0:T], ring[s][:, kc * 256 + ol * 128:kc * 256 + (ol + 1) * 128], rhs_fn(kc), kc == 0, kc == KC - 1,
                           [RS[s], rhs_res_fn(kc)], [RPS[b]])
            for s in sl:
                w_release(s)
            for gi_, j in enumerate(js):
                for ol in range(2):
                    evac(j, ol, bs_[gi_][ol])

        def resid_proj(wname, T, rhs_fn, rhs_res_fn):
            for j in range(4):
                def ev(ol, b, j=j):
                    oc = 2 * j + ol
                    stt(xf[:, oc, 0:T], PS[b][:, 0:T], 1.0 / ALPHA, xf[:, oc, 0:T], ALU.mult, ALU.add, [RPS[b], RX[oc]], [RX[oc]])
                    if oc >= 1:
                        ln_acc(T, oc - 1)
                proj_fm(wname, j, T, rhs_fn, rhs_res_fn, ev)
            ln_acc(T, KC - 1)

        def xb_rhs(T):
            return (lambda kc: xb[:, kc, 0:T]), (lambda kc: RXB[kc])

        def w_in_stage(T, sample, pos0, seq, k_dst, v_dst, gblk0):
            NB = T // 128
            rf, rr_ = xb_rhs(T)
            def evq(j, ol, b):
                h = 2 * j + ol
                cp("act", hT[:, QT0 + h, 0:T], PS[b][:, 0:T], [RPS[b]], [RH[QT0 + h]])
            proj_fm_multi("win", [0, 1], T, rf, rr_, evq)
            tmb = [banks.next() for _ in range(NB)]
            for j in range(2):
                s = w_acquire("win", 2 + j)
                fb = []
                for ol in range(2):
                    b = banks.next()
                    fb.append(b)
                    for kc in range(KC):
                        mm(PS[b][:, 0:T], ring[s][:, kc * 256 + ol * 128:kc * 256 + (ol + 1) * 128], xb[:, kc, 0:T], kc == 0, kc == KC - 1,
                           [RS[s], RXB[kc]], [RPS[b]])
                for blk in range(NB):
                    for kc in range(KC):
                        mm(PS[tmb[blk]][:, j * 256:(j + 1) * 256], xb[:, kc, blk * 128:(blk + 1) * 128], ring[s][:, kc * 256:(kc + 1) * 256],
                           kc == 0, kc == KC - 1, [RS[s], RXB[kc]], [RPS[tmb[blk]]])
                w_release(s)
                for ol in range(2):
                    h = 2 * j + ol
                    if sample:
                        cp("act", kTn[:, h, 0:T], PS[fb[ol]][:, 0:T], [RPS[fb[ol]]], [RKTN])
                    else:
                        cp("act", kTb[:, h * SEQ + pos0:h * SEQ + pos0 + T], PS[fb[ol]][:, 0:T], [RPS[fb[ol]]], [RKT[h]])
            for blk in range(NB):
                i = kst_rr.next()
                cp("dve", kst[i][:, :], PS[tmb[blk]][:, 0:512], [RPS[tmb[blk]]], [RKST[i]])
                dma(LKST[i], k_dst(blk), kst[i][:, :], [RKST[i]], [])
            if not sample:
                tmb = [banks.next() for _ in range(NB)]
                for j in range(2):
                    s = w_acquire("win", 4 + j)
                    for blk in range(NB):
                        for kc in range(KC):
                            mm(PS[tmb[blk]][:, j * 256:(j + 1) * 256], xb[:, kc, blk * 128:(blk + 1) * 128], ring[s][:, kc * 256:(kc + 1) * 256],
                               kc == 0, kc == KC - 1, [RS[s], RXB[kc]], [RPS[tmb[blk]]])
                    w_release(s)
                for blk in range(NB):
                    i = vst_rr.next()
                    cp("dve", vst[i][:, :], PS[tmb[blk]][:, 0:512], [RPS[tmb[blk]]], [RVST[i]])
                    dma(LVST[i], v_dst(blk), vst[i][:, :], [RVST[i]], [])
                    gb = gblk0 + blk
                    cp("act", vSb[:, gb * 512:(gb + 1) * 512], PS[tmb[blk]][:, 0:512], [RPS[tmb[blk]]], [RVS])
            else:
                tmb = [banks.next() for _ in range(NSMP)]
                for j in range(2):
                    s = w_acquire("win", 4 + j)
                    for b_ in range(NSMP):
                        for kc in range(KC):
                            mm(PS[tmb[b_]][0:32, j * 256:(j + 1) * 256], xb[:, kc, b_ * 32:(b_ + 1) * 32], ring[s][:, kc * 256:(kc + 1) * 256],
                               kc == 0, kc == KC - 1, [RS[s], RXB[kc]], [RPS[tmb[b_]]])
                    w_release(s)
                for b_ in range(NSMP):
                    i = vst_rr.next()
                    cp("dve", vst[i][0:32, :], PS[tmb[b_]][0:32, 0:512], [RPS[tmb[b_]]], [RVST[i]])
                    dma(LVST[i], nv_s[b_ * 32:(b_ + 1) * 32, :], vst[i][0:32, :], [RVST[i]], [])
                    cp("act", vnS[0:32, b_, :], PS[tmb[b_]][0:32, 0:512], [RPS[tmb[b_]]], [RVNS])
            for j in range(2):
                def ev(ol, b, j=j):
                    g = 2 * j + ol
                    act(uT[:, g, 0:T], PS[b][:, 0:T], AF.Gelu_apprx_tanh, [RPS[b]], [RU[g]])
                proj_fm("win", 6 + j, T, rf, rr_, ev)
            tmb = [banks.next() for _ in range(NB)]
            for j in range(2):
                s = w_acquire("win", 8 + j)
                for blk in range(NB):
                    for kc in range(KC):
                        mm(PS[tmb[blk]][:, j * 256:(j + 1) * 256], xb[:, kc, blk * 128:(blk + 1) * 128], ring[s][:, kc * 256:(kc + 1) * 256],
                           kc == 0, kc == KC - 1, [RS[s], RXB[kc]], [RPS[tmb[blk]]])
                w_release(s)
            wi = 1 if sample else 0

            def gv_chain(blk, buf, rbuf, i):
                S.op("dve", lambda e: e.bn_stats(out=small[:, 16:22], in_=buf), [rbuf, RSM], [RSM])
                S.op("dve", lambda e: e.bn_aggr(out=small[:, 22:24], in_=small[:, 16:22]), [RSM], [RSM])
                act(small[:, 24:25], small[:, 23:24], AF.Ln, [RSM, RLAM], [RSM], bias=eps_ap(EPS), scale=1.0)
                act(small[:, 25:26], small[:, 24:25], AF.Exp, [RSM], [RSM], scale=-0.5)
                stt(small[:, 26:27], small[:, 22:23], -1.0, small[:, 25:26], ALU.mult, ALU.mult, [RSM], [RSM])
                act(buf, buf, AF.Identity, [rbuf, RSM], [rbuf], bias=small[:, 26:27], scale=small[:, 25:26])
                tt_("pool", buf, buf, ggb[:, 0, :], ALU.mult, [rbuf, RC], [rbuf])
                tt_("pool", buf, buf, ggb[:, 1, :], ALU.add, [rbuf, RC], [rbuf])
                cp("act" if sample else "pool", vnb[i][:, :], buf, [rbuf], [RVNB[i]])
                if sample:
                    dma(LGV[i], ngv_s[:, :], buf, [rbuf], [])

            def gv_spatial(blk, i, b2):
                for g in range(4):
                    mm(PS[b2][:, g * 128:(g + 1) * 128], vnb[i][:, g * 128:(g + 1) * 128], WsT[wi][:, g, :], True, False, [RVNB[i], RWST], [RPS[b2]])
                    mm(PS[b2][:, g * 128:(g + 1) * 128], onesB[0:33, :], bsb[wi][0:33, g * 128:(g + 1) * 128], False, True, [RC], [RPS[b2]])
                ts = slice(blk * 128, (blk + 1) * 128)
                tt_("dve", hT[:, GT0:GT0 + 4, ts], PS[b2][:, 0:512].rearrange("p (g t) -> p g t", t=128), uT[:, :, ts], ALU.mult,
                    [RPS[b2]] + RU, RH[GT0:GT0 + 4])

            deferred = []
            for blk in range(NB):
                b = tmb[blk]
                if sample:
                    i = gv_rr.next()
                    act(gvf[i][:, :], PS[b][:, 0:512], AF.Gelu_apprx_tanh, [RPS[b]], [RGVF[i]])
                    gv_chain(blk, gvf[i][:, :], RGVF[i], i)
                    gv_spatial(blk, i, banks.next())
                else:
                    act(sq[:, blk, :], PS[b][:, 0:512], AF.Gelu_apprx_tanh, [RPS[b]], [RSQ[blk]])
                    i = blk % 2
                    deferred.append((lambda blk=blk, i=i: gv_chain(blk, sq[:, blk, :], RSQ[blk], i),
                                     lambda b2, blk=blk, i=i: gv_spatial(blk, i, b2)))
            return deferred

        def attn_finish(h, T, c0, o1, o2, z1, z2, ro1, ro2, rz1, rz2, sbank, centre=False, mid=None):
            for zap, rz, tj in ((z1, rz1, 0), (z2, rz2, 1)):
                if centre:
                    act(t1[tj][:, 0:T], zap, AF.Ln, [rz], [RT1[tj]], scale=math.exp(LNC[h]))
                    act(t1[tj][:, 0:T], t1[tj][:, 0:T], AF.Exp, [RT1[tj], RLAM], [RT1[tj]], scale=-1.0, bias=lamc[:, 10 + h:11 + h])
                else:
                    act(t1[tj][:, 0:T], zap, AF.Ln, [rz], [RT1[tj]])
                    act(t1[tj][:, 0:T], t1[tj][:, 0:T], AF.Exp, [RT1[tj]], [RT1[tj]], scale=-1.0)
            tt_("dve", s1[:, 0:T], o1, t1[0][:, 0:T], ALU.mult, [ro1, RT1[0]], [RS1])
            tt_("dve", s2[:, 0:T], o2, t1[1][:, 0:T], ALU.mult, [ro2, RT1[1]], [RS2])
            stt(m2[:, 0:T], s2[:, 0:T], neg_lam, s1[:, 0:T], ALU.mult, ALU.add, [RS1, RS2, RLAM], [RM2])
            act(var[:, 0:T], m2[:, 0:T], AF.Square, [RM2], [RVAR])
            b = sbank
            mm(PS[b][:, 0:T], onesE[:], var[:, 0:T], True, True, [RC, RVAR], [RPS[b]])
            if mid is not None:
                mid()
            act(rstd[:, 0:T], PS[b][:, 0:T], AF.Ln, [RPS[b], RLAM], [RRSTD], bias=eps_ap(EPS), scale=1.0)
            act(rstd[:, 0:T], rstd[:, 0:T], AF.Exp, [RRSTD], [RRSTD], scale=-0.5)
            stt(hT[:, AT0 + h, c0:c0 + T], m2[:, 0:T], g08, rstd[:, 0:T], ALU.mult, ALU.mult, [RM2, RRSTD, RLAM], [RH[AT0 + h]])

        def diff_attn_prompt(Q, deferred):
            T = TT
            O1, O2, Z1, Z2 = 4, 5, 6, 7
            OZ = (O1, O2, Z1, Z2)
            nJ = 4 * Q + 4
            steps = [(h, J) for h in range(4) for J in range(nJ)]

            def bcol(d, h):
                return alibi[:, (d + 3) * 4 + h:(d + 3) * 4 + h + 1]

            def emit_S(h, J):
                c0 = max(J - 4 * Q, 0) * 128
                sbk = [banksS[0].next(), banksS[1].next()]
                for m in range(2):
                    mm(PS[sbk[m]][:, c0:T], kTb[m * 64:(m + 1) * 64, h * SEQ + J * 128:h * SEQ + (J + 1) * 128],
                       hT[m * 64:(m + 1) * 64, QT0 + h, c0:T], True, True, [RKT[h], RH[QT0 + h]], [RPS[sbk[m]]])
                return sbk, c0

            def exp_diag(h, m, jb, sb_, qb, d):
                qs = slice(qb * 128, (qb + 1) * 128)
                act(dtmp[m][:, :], PS[sb_][:, qs], AF.Exp, [RPS[sb_], RC], [RDT[m]], bias=bcol(d, h), scale=0.125)
                tt_("dve", Pt[m][jb][:, qs], dtmp[m][:, :], dmask[:, h, :], ALU.mult, [RDT[m], RC], [RP[m][jb][qb]])

            def emit_E(h, J, sbk, c0):
                jb = J % 2
                G = 1 if h == 0 else 4
                qb_lo = c0 // 128
                for m in range(2):
                    sb_ = sbk[m]
                    for g0 in range(0, 4, G):
                        blocks = [qb for qb in range(g0, g0 + G) if qb >= qb_lo]
                        if not blocks:
                            continue
                        d = 4 * Q + g0 - J
                        rest = []
                        for qb in blocks:
                            if 4 * Q + qb == J:
                                exp_diag(h, m, jb, sb_, qb, d)
                            else:
                                rest.append(qb)
                        if rest:
                            lo_, hi_ = rest[0] * 128, (rest[-1] + 1) * 128
                            act(Pt[m][jb][:, lo_:hi_], PS[sb_][:, lo_:hi_], AF.Exp, [RPS[sb_], RC], RP[m][jb][rest[0]:rest[-1] + 1],
                                bias=bcol(d, h), scale=0.125)

            def emit_PV(h, J, c0):
                jb = J % 2
                first, last = (J == 0), (J == nJ - 1)
                for m in range(2):
                    prs = RP[m][jb][c0 // 128:4]
                    mm(PS[OZ[m]][:, c0:T], vSb[:, J * 512 + h * 128:J * 512 + (h + 1) * 128], Pt[m][jb][:, c0:T], first, last,
                       [RVS] + prs, [RPS[OZ[m]]])
                    mm(PS[OZ[2 + m]][:, c0:T], onesB[:], Pt[m][jb][:, c0:T], first, last, [RC] + prs, [RPS[OZ[2 + m]]])

            deferred[0][0]()
            cur = emit_S(*steps[0])
            e_done = set()
            for i, (h, J) in enumerate(steps):
                nxt = emit_S(*steps[i + 1]) if i + 1 < len(steps) else None
                if i not in e_done:
                    emit_E(h, J, *cur)
                emit_PV(h, J, cur[1])
                if J == nJ - 1:
                    mid = None
                    if nxt is not None:
                        def mid(i=i, nxt=nxt):
                            emit_E(steps[i + 1][0], steps[i + 1][1], *nxt)
                            e_done.add(i + 1)
                    attn_finish(h, T, 0, PS[O1][:, 0:T], PS[O2][:, 0:T], PS[Z1][:, 0:T], PS[Z2][:, 0:T], RPS[O1], RPS[O2], RPS[Z1], RPS[Z2], Z1, centre=True, mid=mid)
                    deferred[h][1](O2)
                elif h >= 1 and J == nJ // 2:
                    deferred[h][0]()
                cur = nxt

        def cross_attn(T, c0, mk_res, mv_res):
            for hh in range(4):
                pb = []
                for mb in range(2):
                    b = banks.next()
                    for j in range(2):
                        mm(PS[b][:, 0:T], memkT[:, hh * 2 + j, mb * 128:(mb + 1) * 128], hT[:, QC0 + hh * 2 + j, c0:c0 + T], j == 0, j == 1,
                           [mk_res, RH[QC0 + hh * 2 + j]], [RPS[b]])
                    act(PTc[mb][:, 0:T], PS[b][:, 0:T], AF.Exp, [RPS[b]], RPTC[mb], scale=1.0 / 16.0)
                ob = [banks.next(), banks.next()]
                zb = banks.next()
                for j in range(2):
                    for mb in range(2):
                        mm(PS[ob[j]][:, 0:T], memv[:, mb, hh * 256 + j * 128:hh * 256 + (j + 1) * 128], PTc[mb][:, 0:T], mb == 0, mb == 1,
                           [mv_res] + RPTC[mb], [RPS[ob[j]]])
                for mb in range(2):
                    mm(PS[zb][:, 0:T], onesB[:], PTc[mb][:, 0:T], mb == 0, mb == 1, [RC] + RPTC[mb], [RPS[zb]])
                j_ = t1_rr.next()
                act(t1[j_][:, 0:T], PS[zb][:, 0:T], AF.Ln, [RPS[zb]], [RT1[j_]])
                act(t1[j_][:, 0:T], t1[j_][:, 0:T], AF.Exp, [RT1[j_]], [RT1[j_]], scale=-1.0)
                for j in range(2):
                    tt_("dve", hT[:, OC0 + hh * 2 + j, c0:c0 + T], PS[ob[j]][:, 0:T], t1[j_][:, 0:T], ALU.mult, [RPS[ob[j]], RT1[j_]],
                        [RH[OC0 + hh * 2 + j]])

        def store_y(NB, dst_fn):
            for blk in range(NB):
                i = kst_rr.next()
                vst_rr.next()
                for half in range(2):
                    b = banks.next()
                    for cc in range(4):
                        c = half * 4 + cc
                        tr(PS[b][:, cc * 128:(cc + 1) * 128], xf[:, c, blk * 128:(blk + 1) * 128], [RX[c]], [RPS[b]])
                    stg_, rs_, ln_ = (kst[i], RKST[i], LKST[i]) if half == 0 else (vst[i], RVST[i], LVST[i])
                    cp(ev_rr.next(), stg_[:, :], PS[b][:, 0:512], [RPS[b]], [rs_])
                    dma(ln_, dst_fn(blk)[:, half * 512:(half + 1) * 512], stg_[:, :], [rs_], [])

        def mem_phase_prompt(seq, part=None):
            SUB = 99
            if part in (None, 1):
                load_T(lambda blk: mp[seq, blk * 128:(blk + 1) * 128, :], 2, dst_f32=False)
            todo = (("wk", True), ("wv", False)) if part is None else ((("wk", True),) if part == 1 else (("wv", False),))
            for wname, is_k in todo:
                dst = nmk_p if is_k else nmv_p
                for pair in range(2):
                    mb_ = [banks.next(), banks.next()]
                    for jj in range(2):
                        s = w_acquire(wname, pair * 2 + jj)
                        for blk in range(2):
                            for kc in range(KC):
                                mm(PS[mb_[blk]][:, jj * 256:(jj + 1) * 256], xb[:, kc, blk * 128:(blk + 1) * 128], ring[s][:, kc * 256:(kc + 1) * 256],
                                   kc == 0, kc == KC - 1, [RS[s], RXB[kc]], [RPS[mb_[blk]]])
                        if SUB >= 2:
                            w_release(s)
                    if SUB <= 2:
                        continue
                    for blk in range(2):
                        i = kst_rr.next()
                        cp("dve", kst[i][:, :], PS[mb_[blk]][:, 0:512], [RPS[mb_[blk]]], [RKST[i]])
                        dma(LKST[i], dst[seq, blk * 128:(blk + 1) * 128, pair * 512:(pair + 1) * 512], kst[i][:, :], [RKST[i]], [])
                        if SUB <= 3:
                            continue
                        if is_k:
                            b = banks.next()
                            for cl in range(4):
                                tr(PS[b][:, cl * 128:(cl + 1) * 128], kst[i][:, cl * 128:(cl + 1) * 128], [RKST[i]], [RPS[b]])
                            cp("act", memkT[:, pair * 4:pair * 4 + 4, blk * 128:(blk + 1) * 128],
                               PS[b][:, 0:512].rearrange("p (c t) -> p c t", t=128), [RPS[b]], [RMK])
                        else:
                            cp("act", memv[:, blk, pair * 512:(pair + 1) * 512], PS[mb_[blk]][:, 0:512], [RPS[mb_[blk]]], [RMV])

        def mem_phase_sample(b_):
            for blk in range(2):
                i = xs_rr.next()
                dma(LXS[i], xs[i][:, :], cmk[b_, blk * 128:(blk + 1) * 128, :], [], [RXS[i]])
                for half in range(2):
                    b = banks.next()
                    for cc in range(4):
                        c = half * 4 + cc
                        tr(PS[b][:, cc * 128:(cc + 1) * 128], xs[i][:, c * 128:(c + 1) * 128], [RXS[i]], [RPS[b]])
                    cp("act", memkT[:, half * 4:half * 4 + 4, blk * 128:(blk + 1) * 128],
                       PS[b][:, 0:512].rearrange("p (c t) -> p c t", t=128), [RPS[b]], [RMK])
            for blk in range(2):
                i = xs_rr.next()
                dma(LXS[i], xs[i][:, :], cmv[b_, blk * 128:(blk + 1) * 128, :], [], [RXS[i]])
                cp("pool", memv[:, blk, :], xs[i][:, :], [RXS[i]], [RMV])

        def diff_attn_sample():
            O1, O2, Z1, Z2 = 4, 5, 6, 7
            OZ = (O1, O2, Z1, Z2)
            T = DSEQ
            units = [(b_, h) for b_ in range(NSMP) for h in range(4)]

            def prefetch_fns(u):
                b_, h = units[u]
                bf = u % 2
                kbase = bf * 4128
                vbase = bf * 4224
                ckv = ck[b_].rearrange("(j p) f -> p j f", p=128)
                cvv = cv[b_].rearrange("(j p) f -> p j f", p=128)

                def grp_fn(grp, pbanks=None):
                    i = xs_rr.next()
                    dma(LXS[i], xs[i][:, :].rearrange("p (j f) -> p j f", f=128), ckv[:, grp * 8:(grp + 1) * 8, h * 128:(h + 1) * 128], [], [RXS[i]])
                    for half in range(2):
                        b = pbanks[half] if pbanks else banks.next()
                        for cc in range(4):
                            jj = half * 4 + cc
                            tr(PS[b][:, cc * 128:(cc + 1) * 128], xs[i][:, jj * 128:(jj + 1) * 128], [RXS[i]], [RPS[b]])
                        k0 = kbase + (grp * 8 + half * 4) * 128
                        cp(ev_rr.next(), kTb[:, k0:k0 + 512], PS[b][:, 0:512], [RPS[b]], [RKTC[bf]])
                    g2 = grp % 2
                    dma(LSQ[g2], sq[:, 2 * g2:2 * g2 + 2, :].rearrange("p c (j f) -> p (c j) f", f=128),
                        cvv[:, grp * 8:(grp + 1) * 8, h * 128:(h + 1) * 128], [], [RSQ[2 * g2], RSQ[2 * g2 + 1]])
                    cp("dve", vSb[:, vbase + grp * 1024:vbase + (grp + 1) * 1024].rearrange("p (c t) -> p c t", t=512), sq[:, 2 * g2:2 * g2 + 2, :],
                       [RSQ[2 * g2], RSQ[2 * g2 + 1]], [RVC[bf]])

                def tail_fn():
                    cp("pool", kTb[:, kbase + 4096:kbase + 4128], kTn[:, h, b_ * 32:(b_ + 1) * 32], [RKTN], [RKTC[bf]])

                return [lambda pb=None, g=g: grp_fn(g, pb) for g in range(4)] + [lambda pb=None: tail_fn()]

            for f in prefetch_fns(0):
                f()
            steps = [(u, J) for u in range(len(units)) for J in range(33)]
            sched = {3: 0, 11: 1, 19: 2, 27: 3, 30: 4}
            OB, ZB = 6, 7
            Pbuf = [Pt[0][0], Pt[0][1], Pt[1][0]]
            RPb = [RP[0][0], RP[0][1], RP[1][0]]

            def up(u):
                b_, h = units[u]
                bf = u % 2
                return b_, h, bf, bf * 4128, bf * 4224, slice(b_ * 32, (b_ + 1) * 32)

            def sbanks(n):
                return [2 * (n % 3), 2 * (n % 3) + 1]

            def S_(n):
                u, J = steps[n]
                b_, h, bf, kbase, vbase, qcol = up(u)
                nk = 128 if J < 32 else 32
                sbk = sbanks(n)
                for m in range(2):
                    mm(PS[sbk[m]][0:nk, 0:T], kTb[m * 64:(m + 1) * 64, kbase + J * 128:kbase + J * 128 + nk],
                       hT[m * 64:(m + 1) * 64, QT0 + h, qcol], True, True, [RKTC[bf], RH[QT0 + h]], [RPS[sbk[m]]])

            def E_(n):
                u, J = steps[n]
                b_, h, bf, kbase, vbase, qcol = up(u)
                nk = 128 if J < 32 else 32
                d = 32 - J
                sbk = sbanks(n)
                P = Pbuf[n % 3]
                rps = RPb[n % 3]
                bias = alibi[0:nk, (d + 3) * 4 + h:(d + 3) * 4 + h + 1]
                for m in range(2):
                    pc = slice(m * T, (m + 1) * T)
                    if d > 0:
                        act(P[0:nk, pc], PS[sbk[m]][0:nk, 0:T], AF.Exp, [RPS[sbk[m]], RC], [rps[m]], bias=bias, scale=0.125)
                    else:
                        act(dtmp[m][0:nk, 0:T], PS[sbk[m]][0:nk, 0:T], AF.Exp, [RPS[sbk[m]], RC], [RDT[m]], bias=bias, scale=0.125)
                        tt_("dve", P[0:nk, pc], dtmp[m][0:nk, 0:T], dmask[0:nk, h, 0:T], ALU.mult, [RDT[m], RC], [rps[m]])

            def PV_(n):
                u, J = steps[n]
                b_, h, bf, kbase, vbase, qcol = up(u)
                nk = 128 if J < 32 else 32
                P = Pbuf[n % 3]
                rps = RPb[n % 3][0:2]
                first, last = (J == 0), (J == 32)
                if J < 32:
                    lv = vSb[:, vbase + J * 128:vbase + (J + 1) * 128]
                    lres = RVC[bf]
                else:
                    lv = vnS[0:32, b_, h * 128:(h + 1) * 128]
                    lres = RVNS
                mm(PS[OB][:, 0:2 * T], lv, P[0:nk, 0:2 * T], first, last, [lres] + rps, [RPS[OB]])
                mm(PS[ZB][:, 0:2 * T], onesB[0:nk, :], P[0:nk, 0:2 * T], first, last, [RC] + rps, [RPS[ZB]])

            S_(0)
            if len(steps) > 1:
                S_(1)
            e_done = set()
            nxt_fns = prefetch_fns(1)
            for n, (u, J) in enumerate(steps):
                if n + 2 < len(steps):
                    S_(n + 2)
                if n not in e_done:
                    E_(n)
                PV_(n)
                if J in sched and nxt_fns:
                    nxt_fns[sched[J]](sbanks(n))
                if J == 32:
                    b_, h = units[u]
                    mid = None
                    if n + 1 < len(steps):
                        def mid(n=n):
                            E_(n + 1)
                            e_done.add(n + 1)
                    attn_finish(h, T, b_ * 32, PS[OB][:, 0:T], PS[OB][:, T:2 * T], PS[ZB][:, 0:T], PS[ZB][:, T:2 * T],
                                RPS[OB], RPS[OB], RPS[ZB], RPS[ZB], ZB, mid=mid)
                    nxt_fns = prefetch_fns(u + 2) if u + 2 < len(units) else []

        def mix_rhs(T):
            def f(kc):
                return hT[:, (AT0 + kc) if kc < 4 else (GT0 + kc - 4), 0:T]

            def r(kc):
                return RH[(AT0 + kc) if kc < 4 else (GT0 + kc - 4)]
            return f, r

        def oc_rhs(T):
            return (lambda kc: hT[:, OC0 + kc, 0:T]), (lambda kc: RH[OC0 + kc])

        def q_cross(T):
            rf, rr_ = xb_rhs(T)

            def ev(j, ol, b):
                c8 = 2 * j + ol
                cp("act", hT[:, QC0 + c8, 0:T], PS[b][:, 0:T], [RPS[b]], [RH[QC0 + c8]])
            proj_fm_multi("wq", [0, 1], T, rf, rr_, ev)
            proj_fm_multi("wq", [2, 3], T, rf, rr_, ev)

        class _Stop(Exception):
            pass

        nst = [0]

        def stage(fn, *a):
            if nst[0] >= limit:
                raise _Stop()
            nst[0] += 1
            fn(*a)

        try:
            mem_done = set()
            smp_pre = False
            for seq in range(n_seq):
                if seq not in mem_done:
                    stage(mem_phase_prompt, seq)
                pre = False
                for Q in range(n_tiles):
                    T = TT
                    pos0 = Q * TT
                    if not pre:
                        stage(load_T, lambda blk, seq=seq, pos0=pos0: xp[seq, pos0 + blk * 128:pos0 + (blk + 1) * 128, :], 4, True, Q > 0)
                    if Q + 1 < n_tiles and nst[0] < limit:
                        for k in range(2):
                            x_dma(xp[seq, pos0 + TT + k * 128:pos0 + TT + (k + 1) * 128, :], k)
                    if Q == n_tiles - 1 and seq + 1 == n_seq and do_sample and limit > 10 ** 8:
                        x_dma(xsm[:, :], 0)
                    stage(ffn, T, "gu1", "dn1", FFN_A if pre else 0)
                    stage(layer_norm, T, 0, EPS_DN)
                    dfr = []
                    stage(lambda *a: dfr.extend(w_in_stage(*a)), T, False, pos0, seq,
                          lambda blk, seq=seq, pos0=pos0: nk_p[seq, pos0 + blk * 128:pos0 + (blk + 1) * 128, :],
                          lambda blk, seq=seq, pos0=pos0: nv_p[seq, pos0 + blk * 128:pos0 + (blk + 1) * 128, :],
                          Q * 4)
                    stage(diff_attn_prompt, Q, dfr)
                    if Q + 1 < n_tiles and nst[0] < limit:
                        for k in range(2, 4):
                            x_dma(xp[seq, pos0 + TT + k * 128:pos0 + TT + (k + 1) * 128, :], k)
                    stage(resid_proj, "wout", T, *mix_rhs(T))
                    stage(layer_norm, T, 1, EPS_DN)
                    stage(q_cross, T)
                    stage(cross_attn, T, 0, RMK, RMV)
                    stage(resid_proj, "wo", T, *oc_rhs(T))
                    stage(layer_norm, T, 2, EPS_DN)
                    stage(ffn, T, "gu2", "dn2")
                    pre = (Q + 1 < n_tiles) and nst[0] < limit
                    last = (Q == n_tiles - 1) and nst[0] < limit and limit > 10 ** 8
                    mem_pre = last and seq + 1 < n_seq
                    smp_pre = last and seq + 1 == n_seq and do_sample
                    if mem_pre:
                        mem_phase_prompt(seq + 1, 1)
                    if smp_pre:
                        x_transpose(0, 0, dst_f32=False, dst_bf=True)
                        ffn_gu(128, "gu1", 0, 4)
                    if pre:
                        for blk in range(4):
                            x_transpose(blk, blk, dst_f32=False, dst_bf=True)
                        ffn_gu(T, "gu1", 0, 4)
                    stage(layer_norm, T, 3, EPS_DN, False)
                    if pre:
                        ffn_gu(T, "gu1", 4, 10)
                    if mem_pre:
                        mem_phase_prompt(seq + 1, 2)
                        mem_done.add(seq + 1)
                    if smp_pre:
                        ffn_gu(128, "gu1", 4, 10)
                    stage(store_y, 4, lambda blk, seq=seq, pos0=pos0: y_p[seq, pos0 + blk * 128:pos0 + (blk + 1) * 128, :])
                    if pre:
                        ffn_gu(T, "gu1", 10, FFN_A)
                        for blk in range(4):
                            x_transpose(blk, blk, dst_f32=True, dst_bf=False)
                    if smp_pre:
                        ffn_gu(128, "gu1", 10, FFN_A)
                        x_transpose(0, 0, dst_f32=True, dst_bf=False)
            if do_sample:
                T = 128
                if not smp_pre:
                    stage(load_T, lambda blk: xsm[:, :], 1)
                stage(ffn, T, "gu1", "dn1", FFN_A if smp_pre else 0)
                stage(layer_norm, T, 0, EPS_DN)
                stage(w_in_stage, T, True, 0, 0, lambda blk: nk_s[:, :], None, 0)
                stage(diff_attn_sample)
                stage(resid_proj, "wout", T, *mix_rhs(T))
                stage(layer_norm, T, 1, EPS_DN)
                stage(q_cross, T)
                for b_ in range(NSMP):
                    stage(mem_phase_sample, b_)
                    stage(cross_attn, DSEQ, b_ * 32, RMK, RMV)
                stage(resid_proj, "wo", T, *oc_rhs(T))
                stage(layer_norm, T, 2, EPS_DN)
                stage(ffn, T, "gu2", "dn2")
                stage(layer_norm, T, 3, EPS_DN, False)
                stage(store_y, 1, lambda blk: y_s[:, :])
            assert wstate["consumed"] == len(plan), (wstate, len(plan))
        except _Stop:
            pass
        S.finish()
        with nc.Block() as block:
            S.emit(block)
    return nc, S.n_instr


def _consts():
    slopes = 2.0 ** (-8.0 * np.arange(1, 5, dtype=np.float64) / 4)
    p = np.arange(128, dtype=np.float64)
    alibi = np.zeros((128, 36 * 4), np.float32)
    for d in range(-3, 33):
        for h in range(4):
            alibi[:, (d + 3) * 4 + h] = slopes[h] * (p - 128 * d)
    j = np.arange(128)[:, None]
    i = np.arange(128)[None, :]
    dmask = np.zeros((128, 4, 128), np.float32)
    allowed = (j // 64) <= (i // 64)
    for h in range(4):
        m = np.where(j <= i, 1.0, np.exp(-2.0 * slopes[h] * (j - i)))
        dmask[:, h, :] = np.where(allowed, m, 0.0)
    tril = (i >= j).astype(np.float32)
    return alibi, dmask, tril, np.eye(128, dtype=np.float32)


def _prep_inputs(inp):
    f = lambda a: np.ascontiguousarray(np.asarray(a, dtype=np.float32))
    alibi, dmask, tril, ident = _consts()
    lnp = np.zeros((128, 64), np.float32)
    for gi, (g, b) in enumerate((("ln1_g", "ln1_b"), ("ln2_g", "ln2_b"), ("ln3_g", "ln3_b"), ("ln4_g", "ln4_b"))):
        lnp[:, gi * 16:gi * 16 + 8] = f(inp[g])[0].reshape(8, 128).T
        lnp[:, gi * 16 + 8:gi * 16 + 16] = f(inp[b])[0].reshape(8, 128).T
    ggb = np.zeros((128, 2, 512), np.float32)
    ggb[:, 0, :] = f(inp["gmlp_ln_g"])[0][None, :]
    ggb[:, 1, :] = f(inp["gmlp_ln_b"])[0][None, :]
    ws = f(inp["gmlp_ws"])[0]
    ws_p = np.ascontiguousarray(ws.transpose(1, 0, 2))
    ws_s = np.zeros((128, 4, 128), np.float32)
    for r in range(4):
        ws_s[r * 32:(r + 1) * 32, :, r * 32:(r + 1) * 32] = ws[:, :32, :32].transpose(1, 0, 2)
    bs = f(inp["gmlp_bs"])[0]
    bs_p = bs.reshape(1, 512).copy()
    bs_s = np.ascontiguousarray(np.tile(bs[:, :32], (1, 4)).reshape(1, 512))
    lamv = np.zeros((128, 4, 64), np.float32)
    for k_, n in enumerate(("lambda_q1", "lambda_k1", "lambda_q2", "lambda_k2")):
        lamv[:, k_, :] = f(inp[n])[0][None, :]
    shared = {
        "w_gu1": f(inp["ffn1_w_gu"])[0], "w_dn1": f(inp["ffn1_w_down"])[0], "w_in": f(inp["w_in"])[0],
        "w_out": f(inp["w_out"])[0], "w_q": f(inp["cross_wq"])[0], "w_k": f(inp["cross_wk"])[0],
        "w_v": f(inp["cross_wv"])[0], "w_o": f(inp["cross_wo"])[0], "w_gu2": f(inp["ffn2_w_gu"])[0],
        "w_dn2": f(inp["ffn2_w_down"])[0],
        "lnp": lnp, "subg": f(inp["subln_g"])[0].reshape(128, 1).copy(), "ggb": ggb,
        "ws_p": ws_p, "ws_s": ws_s, "bs_p": bs_p, "bs_s": bs_s, "lamv": lamv,
        "ident": ident, "alibi": alibi, "dmask": dmask, "tril": tril,
    }
    x_prompt, x_sample = f(inp["x_prompt"]), f(inp["x_sample"])
    cache_k, cache_v = f(inp["cache_k"]), f(inp["cache_v"])
    cmk, cmv, mpr = f(inp["cache_mem_k"]), f(inp["cache_mem_v"]), f(inp["mem_prompt"])
    maps = []
    for c in range(N_CORES):
        m = dict(shared)
        m["xp"] = x_prompt[c * NSEQ:(c + 1) * NSEQ]
        m["xsm"] = x_sample[c * NSMP:(c + 1) * NSMP].reshape(128, D)
        m["ck"] = cache_k[0, c * NSMP:(c + 1) * NSMP].reshape(NSMP, PAST, 512)
        m["cv"] = cache_v[0, c * NSMP:(c + 1) * NSMP].reshape(NSMP, PAST, 512)
        m["cmk"] = cmk[0, c * NSMP:(c + 1) * NSMP].reshape(NSMP, NMEM, D)
        m["cmv"] = cmv[0, c * NSMP:(c + 1) * NSMP].reshape(NSMP, NMEM, D)
        m["mp"] = mpr[c * NSEQ:(c + 1) * NSEQ]
        maps.append(m)
    return maps


def _assemble(results):
    cat = lambda k: np.concatenate([np.asarray(r[k]) for r in results], axis=0)
    B, BS = N_CORES * NSEQ, N_CORES * NSMP
    y_p = cat("y_p").reshape(B, SEQ, D)
    y_s = cat("y_s").reshape(BS, DSEQ, D)
    nk_p = cat("nk_p").reshape(1, B, SEQ, 4, 2, 64)
    nv_p = cat("nv_p").reshape(1, B, SEQ, 4, 128)
    nmk_p = cat("nmk_p").reshape(1, B, NMEM, 4, 256)
    nmv_p = cat("nmv_p").reshape(1, B, NMEM, 4, 256)
    nk_s = cat("nk_s").reshape(1, BS, DSEQ, 4, 2, 64)
    nv_s = cat("nv_s").reshape(1, BS, DSEQ, 4, 128)
    ngv_s = cat("ngv_s").reshape(1, BS, DSEQ, 4, 128)
    return tuple(np.ascontiguousarray(a, dtype=np.float32) for a in (y_p, y_s, nk_p, nv_p, nmk_p, nmv_p, nk_s, nv_s, ngv_s))


def kernel(**inputs):
    maps = _prep_inputs(inputs)
    nc, _ = build_nc()
    res = run_bass_kernel_spmd(nc, maps, core_ids=list(range(N_CORES)))
    return _assemble(res.results)
```
